# Optimizing a Trainium2 kernel written in Bass

```python
import math
import jax, jax.numpy as jnp
from jax import lax
import numpy as np

D_MODEL = 2048
BATCH = 2
SEQ = 16384
DEPTH = 1

D_SSM = D_MODEL
SSM_HEAD_DIM = 64
N_SSM_HEADS = D_SSM // SSM_HEAD_DIM
SSM_GROUPS = 4
D_STATE = 128
CONV_WIDTH = 4
CONV_PAD = (2, 1)
CHUNK = 128
D_XBC = D_SSM + 2 * SSM_GROUPS * D_STATE

N_ATTN_HEADS = D_MODEL // 128
QK_NOPE_DIM = 128
QK_ROPE_DIM = 64
V_HEAD_DIM = 128
D_ATTN = N_ATTN_HEADS * V_HEAD_DIM
Q_LORA = 3 * D_MODEL // 8
KV_LORA = D_MODEL // 4
Q_BLOCK = 128
ROPE_THETA = 10000.0

EPS = 1e-6
D_IN = D_SSM + D_XBC + 2 * N_SSM_HEADS + Q_LORA + KV_LORA + QK_ROPE_DIM + D_ATTN + 2 * D_MODEL

kernel_name = "bidir_hybrid_ssd_mla_gated_merge"


def rmsnorm(x, w):
    xf = x.astype(jnp.float32)
    y = xf * lax.rsqrt(jnp.mean(xf * xf, axis=-1, keepdims=True) + EPS)
    return (y * w.astype(jnp.float32)).astype(x.dtype)


def segsum(a):
    t = a.shape[-1]
    ar = jnp.broadcast_to(a[..., :, None], a.shape + (t,))
    strict = jnp.tril(jnp.ones((t, t), dtype=bool), -1)
    cs = jnp.cumsum(jnp.where(strict, ar, 0.0), axis=-2)
    return jnp.where(jnp.tril(jnp.ones((t, t), dtype=bool)), cs, -jnp.inf)


def ssd_scan(xs, dt, a, bm, cm):
    b, s, nh, p = xs.shape
    g = bm.shape[2]
    j = nh // g
    c = s // CHUNK
    da = (dt * a).reshape(b, c, CHUNK, g, j).transpose(0, 3, 4, 1, 2)
    xd = (xs.astype(jnp.float32) * dt[..., None]).reshape(b, c, CHUNK, g, j, p)
    bc = bm.astype(jnp.float32).reshape(b, c, CHUNK, g, -1)
    cc = cm.astype(jnp.float32).reshape(b, c, CHUNK, g, -1)
    a_cs = jnp.cumsum(da, axis=-1)
    lmat = jnp.exp(segsum(da))
    cb = jnp.einsum("bclgn,bcsgn->bcgls", cc, bc)
    y_diag = jnp.einsum("bcgls,bgjcls,bcsgjp->bclgjp", cb, lmat, xd)
    decay_states = jnp.exp(a_cs[..., -1:] - a_cs)
    states = jnp.einsum("bcsgn,bgjcs,bcsgjp->bcgjpn", bc, decay_states, xd)
    a_last = jnp.pad(a_cs[..., -1], ((0, 0), (0, 0), (0, 0), (1, 0)))
    decay_chunk = jnp.exp(segsum(a_last))
    states = jnp.concatenate([jnp.zeros_like(states[:, :1]), states], axis=1)
    states = jnp.einsum("bgjzc,bcgjpn->bzgjpn", decay_chunk, states)[:, :-1]
    y_off = jnp.einsum("bclgn,bcgjpn,bgjcl->bclgjp", cc, states, jnp.exp(a_cs))
    return (y_diag + y_off).reshape(b, s, nh, p)


def dwconv(u, w, bias):
    out = lax.conv_general_dilated(
        u, w[:, None, :].astype(u.dtype), window_strides=(1,), padding=[CONV_PAD],
        dimension_numbers=("NWC", "WIO", "NWC"), feature_group_count=u.shape[-1])
    return out + bias.astype(u.dtype)


def rope_tables(positions):
    inv = 1.0 / (ROPE_THETA ** (jnp.arange(0, QK_ROPE_DIM, 2, dtype=jnp.float32) / QK_ROPE_DIM))
    ang = positions.astype(jnp.float32)[..., None] * inv
    return jnp.cos(ang), jnp.sin(ang)


def apply_rope(t, cos, sin):
    t1, t2 = jnp.split(t.astype(jnp.float32), 2, axis=-1)
    return jnp.concatenate([t1 * cos - t2 * sin, t2 * cos + t1 * sin], axis=-1).astype(t.dtype)


def bidir_attention(qn, qr, kn, kr, v):
    b, s, nh, _ = qn.shape
    nblk = s // Q_BLOCK
    scale = 1.0 / math.sqrt(QK_NOPE_DIM + QK_ROPE_DIM)

    def blocks(t):
        return t.reshape((b, nblk, Q_BLOCK) + t.shape[2:]).swapaxes(0, 1)

    def one(qs):
        qn_b, qr_b = qs
        sc = (jnp.einsum("bqhd,bkhd->bhqk", qn_b, kn).astype(jnp.float32)
              + jnp.einsum("bqhr,bkr->bhqk", qr_b, kr).astype(jnp.float32))
        p = jax.nn.softmax(sc * scale, axis=-1).astype(v.dtype)
        return jnp.einsum("bhqk,bkhd->bqhd", p, v)

    o = lax.map(one, (blocks(qn), blocks(qr)))
    return o.swapaxes(0, 1).reshape(b, s, nh * v.shape[-1])


def hybrid_layer(x, cos, sin, norm_w, w_in, conv_w, conv_b, a_log_fwd, a_log_bwd, dt_bias_fwd, dt_bias_bwd,
                 d_skip, ssm_norm_w, q_norm_w, w_uq, kv_norm_w, w_ukv, w_proj_ssm, w_proj_attn, w_out):
    b, s, _ = x.shape
    h = rmsnorm(x, norm_w)
    u = h @ w_in
    sizes = (D_SSM, D_XBC, N_SSM_HEADS, N_SSM_HEADS, Q_LORA, KV_LORA + QK_ROPE_DIM, D_ATTN, D_MODEL, D_MODEL)
    cuts = np.cumsum(sizes)[:-1].tolist()
    z_ssm, xbc, dt_f, dt_b, q_a, kv_a, z_attn, g_ssm, g_attn = jnp.split(u, cuts, axis=-1)

    xbc = jax.nn.silu(dwconv(xbc, conv_w, conv_b))
    xs, bm, cm = jnp.split(xbc, [D_SSM, D_SSM + SSM_GROUPS * D_STATE], axis=-1)
    xs = xs.reshape(b, s, N_SSM_HEADS, SSM_HEAD_DIM)
    bm = bm.reshape(b, s, SSM_GROUPS, D_STATE)
    cm = cm.reshape(b, s, SSM_GROUPS, D_STATE)
    dtf = jax.nn.softplus(dt_f.astype(jnp.float32) + dt_bias_fwd.astype(jnp.float32))
    dtb = jax.nn.softplus(dt_b.astype(jnp.float32) + dt_bias_bwd.astype(jnp.float32))
    a_f = -jnp.exp(a_log_fwd.astype(jnp.float32))
    a_b = -jnp.exp(a_log_bwd.astype(jnp.float32))
    flip = lambda t: jnp.flip(t, axis=1)
    y_fwd = ssd_scan(xs, dtf, a_f, bm, cm)
    y_bwd = flip(ssd_scan(flip(xs), flip(dtb), a_b, flip(bm), flip(cm)))
    y = (y_fwd + y_bwd + d_skip.astype(jnp.float32)[:, None] * xs.astype(jnp.float32)).astype(x.dtype)
    y = y.reshape(b, s, D_SSM) * jax.nn.silu(z_ssm)
    y = rmsnorm(y.reshape(b, s, SSM_GROUPS, D_SSM // SSM_GROUPS),
                ssm_norm_w.reshape(SSM_GROUPS, D_SSM // SSM_GROUPS)).reshape(b, s, D_SSM)

    c_q = rmsnorm(q_a, q_norm_w)
    q = (c_q @ w_uq).reshape(b, s, N_ATTN_HEADS, QK_NOPE_DIM + QK_ROPE_DIM)
    qn, qr = jnp.split(q, [QK_NOPE_DIM], axis=-1)
    c_kv, kr = jnp.split(kv_a, [KV_LORA], axis=-1)
    c_kv = rmsnorm(c_kv, kv_norm_w)
    kv = (c_kv @ w_ukv).reshape(b, s, N_ATTN_HEADS, QK_NOPE_DIM + V_HEAD_DIM)
    kn, v = jnp.split(kv, [QK_NOPE_DIM], axis=-1)
    qr = apply_rope(qr, cos[:, :, None, :], sin[:, :, None, :])
    kr = apply_rope(kr, cos, sin)
    o = bidir_attention(qn, qr, kn, kr, v) * jax.nn.silu(z_attn)

    merged = jax.nn.sigmoid(g_ssm) * (y @ w_proj_ssm) + jax.nn.sigmoid(g_attn) * (o @ w_proj_attn)
    return x + merged @ w_out


def setup_inputs(seed: int = 0) -> dict:
    key = jax.random.key(seed)
    ks = jax.random.split(key, 24)
    f32 = jnp.float32

    def nrm(k, shape, fan_in):
        return jax.random.normal(k, shape, f32) * (fan_in ** -0.5)

    def gain(k, shape):
        return 1.0 + 0.02 * jax.random.normal(k, shape, f32)

    def a_log(k):
        return jnp.log(jax.random.uniform(k, (DEPTH, N_SSM_HEADS), f32, minval=1.0, maxval=16.0))

    def dt_bias(k):
        lo, hi = math.log(0.001), math.log(0.1)
        dt = jnp.exp(jax.random.uniform(k, (DEPTH, N_SSM_HEADS), f32) * (hi - lo) + lo)
        return dt + jnp.log(-jnp.expm1(-dt))

    x = jax.random.normal(ks[0], (BATCH, SEQ, D_MODEL), f32)
    offsets = jax.random.randint(ks[1], (BATCH, 1), 0, 4096, dtype=jnp.int32)
    positions = (jnp.arange(SEQ, dtype=jnp.int32)[None, :] + offsets).astype(jnp.int32)
    return {
        "x": x,
        "positions": positions,
        "norm_w": gain(ks[2], (DEPTH, D_MODEL)),
        "w_in": nrm(ks[3], (DEPTH, D_MODEL, D_IN), D_MODEL),
        "conv_w": nrm(ks[4], (DEPTH, CONV_WIDTH, D_XBC), CONV_WIDTH),
        "conv_b": 0.02 * jax.random.normal(ks[5], (DEPTH, D_XBC), f32),
        "a_log_fwd": a_log(ks[6]),
        "a_log_bwd": a_log(ks[7]),
        "dt_bias_fwd": dt_bias(ks[8]),
        "dt_bias_bwd": dt_bias(ks[9]),
        "d_skip": gain(ks[10], (DEPTH, N_SSM_HEADS)),
        "ssm_norm_w": gain(ks[11], (DEPTH, D_SSM)),
        "q_norm_w": gain(ks[12], (DEPTH, Q_LORA)),
        "w_uq": nrm(ks[13], (DEPTH, Q_LORA, N_ATTN_HEADS * (QK_NOPE_DIM + QK_ROPE_DIM)), Q_LORA),
        "kv_norm_w": gain(ks[14], (DEPTH, KV_LORA)),
        "w_ukv": nrm(ks[15], (DEPTH, KV_LORA, N_ATTN_HEADS * (QK_NOPE_DIM + V_HEAD_DIM)), KV_LORA),
        "w_proj_ssm": nrm(ks[16], (DEPTH, D_SSM, D_MODEL), D_SSM),
        "w_proj_attn": nrm(ks[17], (DEPTH, D_ATTN, D_MODEL), D_ATTN),
        "w_out": nrm(ks[18], (DEPTH, D_MODEL, D_MODEL), D_MODEL),
        "final_norm_w": gain(ks[19], (D_MODEL,)),
    }


def reference(x, positions, norm_w, w_in, conv_w, conv_b, a_log_fwd, a_log_bwd, dt_bias_fwd, dt_bias_bwd,
              d_skip, ssm_norm_w, q_norm_w, w_uq, kv_norm_w, w_ukv, w_proj_ssm, w_proj_attn, w_out,
              final_norm_w):
    cos, sin = rope_tables(positions)
    for l in range(DEPTH):
        x = hybrid_layer(x, cos, sin, norm_w[l], w_in[l], conv_w[l], conv_b[l], a_log_fwd[l], a_log_bwd[l],
                         dt_bias_fwd[l], dt_bias_bwd[l], d_skip[l], ssm_norm_w[l], q_norm_w[l], w_uq[l],
                         kv_norm_w[l], w_ukv[l], w_proj_ssm[l], w_proj_attn[l], w_out[l])
    return rmsnorm(x, final_norm_w)
```

```python
import contextlib
import types
import numpy as np
import concourse.bass as bass
import concourse.mybir as mybir
from concourse.bass_utils import run_bass_kernel_spmd

F32 = mybir.dt.float32; BF16 = mybir.dt.bfloat16; I32 = mybir.dt.int32
AF = mybir.ActivationFunctionType; ALU = mybir.AluOpType
D = 2048
DIN = 12672
C_ZS, C_XS, C_B, C_C, C_DTF, C_DTB, C_QA, C_KV, C_ZA, C_GS, C_GA = 0, 2048, 4096, 4608, 5120, 5152, 5184, 5952, 6528, 8576, 10624
W = 16384


class Buf:
    __slots__ = ("wc", "wd", "rc", "rd", "const")

    def __init__(s, const=False):
        s.wc = {}; s.wd = {}; s.rc = {}; s.rd = {}; s.const = const


def _snap(fn):
    if fn.__closure__ is None:
        return fn
    cells = []
    for c in fn.__closure__:
        try:
            cells.append(types.CellType(c.cell_contents))
        except ValueError:
            cells.append(c)
    return types.FunctionType(fn.__code__, fn.__globals__, fn.__name__, fn.__defaults__, tuple(cells))


def _addtok(c, d, t):
    if t[0] == 'c':
        if c.get(t[1], -1) < t[2]: c[t[1]] = t[2]
    else:
        if d.get(t[1], 0) < t[2]: d[t[1]] = t[2]


class Sched:
    ENG = ("pe", "act", "dve", "pool", "sp")

    def __init__(s, nc, stack, ndma=24, nwin=14):
        s.nc = nc
        s.q = {e: [] for e in s.ENG}
        s.cnt = {e: 0 for e in s.ENG}
        s.waited_c = {}; s.waited_d = {}
        s.ndma = ndma; s.dma_tot = [0] * ndma; s.dma_next = 0
        s.csem = {e: [stack.enter_context(nc.semaphore(f"c_{e}_{i}")) for i in range(nwin)]
                  for e in ("pe", "act", "dve", "pool")}
        s.dsem = [stack.enter_context(nc.semaphore(f"d_{i}")) for i in range(ndma)]

    def op(s, E, fn, reads=(), writes=(), pwrites=(), dma=False):
        c = {}; d = {}
        for b in reads:
            for k, v in b.wc.items(): _addtok(c, d, ('c', k, v))
            for k, v in b.wd.items(): _addtok(c, d, ('d', k, v))
        for b in writes:
            for k, v in b.wc.items(): _addtok(c, d, ('c', k, v))
            for k, v in b.wd.items(): _addtok(c, d, ('d', k, v))
            for k, v in b.rc.items(): _addtok(c, d, ('c', k, v))
            for k, v in b.rd.items(): _addtok(c, d, ('d', k, v))
        for b in pwrites:
            for k, v in b.rc.items(): _addtok(c, d, ('c', k, v))
            for k, v in b.rd.items(): _addtok(c, d, ('d', k, v))
        if dma:
            si = s.dma_next; s.dma_next = (si + 1) % s.ndma
            if s.dma_tot[si] > 0: _addtok(c, d, ('d', si, s.dma_tot[si]))
            s.dma_tot[si] += 16
            tok = ('d', si, s.dma_tot[si])
        else:
            idx = s.cnt[E]; s.cnt[E] += 1; tok = ('c', E, idx)
        waits = []
        for eng, idx in c.items():
            if eng == E and E == 'pe': continue
            if s.waited_c.get((E, eng), -1) >= idx: continue
            s.waited_c[(E, eng)] = idx
            waits.append((s.csem[eng][idx // W], idx % W + 1))
        for si, val in d.items():
            if s.waited_d.get((E, si), 0) >= val: continue
            s.waited_d[(E, si)] = val
            waits.append((s.dsem[si], val))
        inc = (s.csem[E][tok[2] // W], 1) if tok[0] == 'c' else (s.dsem[tok[1]], 16)
        s.q[E].append((waits, _snap(fn), inc))
        for b in reads:
            if not b.const: _addtok(b.rc, b.rd, tok)
        for b in writes:
            b.wc = {}; b.wd = {}; b.rc = {}; b.rd = {}
            _addtok(b.wc, b.wd, tok)
        for b in pwrites:
            _addtok(b.wc, b.wd, tok)
        return tok

    def flush(s):
        nc = s.nc
        q = s.q; s.q = {e: [] for e in s.ENG}
        ctot = dict(s.cnt); dtot = list(s.dma_tot)

        def replay(E):
            def f(e):
                for waits, fn, inc in q[E]:
                    for sem, val in waits: e.wait_ge(sem, val)
                    fn(e).then_inc(inc[0], inc[1])
                for eng in ("pe", "act", "dve", "pool"):
                    n = ctot[eng]
                    if n > 0 and s.waited_c.get((E, eng), -1) < n - 1:
                        e.wait_ge(s.csem[eng][(n - 1) // W], (n - 1) % W + 1)
                for si in range(s.ndma):
                    if dtot[si] > 0 and s.waited_d.get((E, si), 0) < dtot[si]:
                        e.wait_ge(s.dsem[si], dtot[si])
            return f
        with nc.Block() as block:
            block.sync(replay("sp"))
            block.tensor(replay("pe"))
            block.scalar(replay("act"))
            block.vector(replay("dve"))
            block.gpsimd(replay("pool"))
        for E in s.ENG:
            for eng in ("pe", "act", "dve", "pool"):
                if ctot[eng] > 0: s.waited_c[(E, eng)] = ctot[eng] - 1
            for si in range(s.ndma):
                s.waited_d[(E, si)] = dtot[si]


def bc(ap, shape, axis):
    return ap.unsqueeze(axis).to_broadcast(shape)


def build(S, dbg=False):
    SO = S // 4
    NBK = S // 512
    NBS = (4 * SO) // 512
    NBO = SO // 512
    NCH = NBS * 4
    nc = bass.Bass("TRN2", target_bir_lowering=False)

    def din(name, shape, dt=F32):
        return nc.dram_tensor(name, list(shape), dt, kind="ExternalInput").ap()
    xkv = din("xkv", [S + 4, D]); xfw = din("xfw", [4 * SO + 4, D]); xbw = din("xbw", [4 * SO + 4, D])
    xown = din("xown", [SO, D])
    pos_kv = din("pos_kv", [1, S], I32); pos_own = din("pos_own", [1, SO], I32)
    validf = din("validf", [128, NCH]); validb = din("validb", [128, NCH])
    norm_w = din("norm_w", [1, D]); w_in = din("w_in", [D, DIN]); conv_w = din("conv_w", [4, 3072]); conv_b = din("conv_b", [1, 3072])
    a_log = din("a_log", [1, 64]); dt_bias = din("dt_bias", [1, 64]); d_skip = din("d_skip", [1, 32])
    ssm_norm_w = din("ssm_norm_w", [1, D]); q_norm_w = din("q_norm_w", [1, 768]); w_uq = din("w_uq", [768, 3072])
    kv_norm_w = din("kv_norm_w", [1, 512]); w_ukv = din("w_ukv", [512, 4096])
    w_ps = din("w_proj_ssm", [D, D]); w_pa = din("w_proj_attn", [D, D]); w_out = din("w_out", [D, D])
    final_norm_w = din("final_norm_w", [1, D])
    ident_d = din("ident", [128, 128]); tri_d = din("tri", [2, 128, 128]); neg_d = din("negm", [2, 128, 128])
    invf_d = din("invf", [64, 1])
    out = nc.dram_tensor("out", [SO, D], F32, kind="ExternalOutput").ap()
    KT = nc.dram_tensor("KT", [16, 128, S], BF16, **({"kind": "ExternalOutput"} if dbg else {})).ap()
    KR = nc.dram_tensor("KR", [64, S], BF16, **({"kind": "ExternalOutput"} if dbg else {})).ap()
    VV = nc.dram_tensor("VV", [16, S, 128], BF16, **({"kind": "ExternalOutput"} if dbg else {})).ap()
    QT = nc.dram_tensor("QT", [16, 128, SO], BF16, **({"kind": "ExternalOutput"} if dbg else {})).ap()
    QR = nc.dram_tensor("QR", [16, 64, SO], BF16, **({"kind": "ExternalOutput"} if dbg else {})).ap()
    ZAT = nc.dram_tensor("ZAT", [D, SO], BF16, **({"kind": "ExternalOutput"} if dbg else {})).ap()
    GST = nc.dram_tensor("GST", [D, SO], BF16, **({"kind": "ExternalOutput"} if dbg else {})).ap()
    GAT = nc.dram_tensor("GAT", [D, SO], BF16, **({"kind": "ExternalOutput"} if dbg else {})).ap()
    YF = nc.dram_tensor("YF", [SO, D], F32, **({"kind": "ExternalOutput"} if dbg else {})).ap()
    YNT = nc.dram_tensor("YNT", [D, SO], BF16, **({"kind": "ExternalOutput"} if dbg else {})).ap()
    OT = nc.dram_tensor("OT", [D, SO], BF16, **({"kind": "ExternalOutput"} if dbg else {})).ap()
    b_KT = Buf(); b_KR = Buf(); b_VV = Buf(); b_QT = Buf(); b_QR = Buf(); b_ZAT = Buf(); b_GST = Buf(); b_GAT = Buf()
    b_YF = Buf(); b_YNT = Buf(); b_OT = Buf(); b_out = Buf()

    top = contextlib.ExitStack()
    with top:
        sc = Sched(nc, top)
        psum = [top.enter_context(nc.psum_tensor(f"ps{i}", [128, 512], F32)) for i in range(8)]
        psb = [Buf() for _ in range(8)]
        pstate = [0]

        def nps():
            i = pstate[0]; pstate[0] = (i + 1) % 8
            return psum[i], psb[i]

        def wkw(first, bb):
            return {"writes": [bb]} if first else {"pwrites": [bb]}

        uid = [0]

        def sbt(stack, name, shape, dt):
            uid[0] += 1
            return stack.enter_context(nc.sbuf_tensor(f"s{uid[0]}_{name}", list(shape), dt))
        CB_ = Buf(const=True)
        identf = sbt(top, "identf", [128, 128], F32); ident = sbt(top, "ident", [128, 128], BF16)
        onesf = sbt(top, "onesf", [128, 128], F32); onesb = sbt(top, "onesb", [128, 128], BF16)
        b_init = Buf()
        sc.op("sp", lambda e: e.dma_start(out=identf[:], in_=ident_d[:, :]), writes=[b_init], dma=True)
        sc.op("dve", lambda e: e.tensor_copy(ident[:], identf[:]), reads=[b_init], pwrites=[b_init])
        sc.op("dve", lambda e: e.memset(onesf[:], 1.0), pwrites=[b_init])
        sc.op("dve", lambda e: e.memset(onesb[:], 1.0), pwrites=[b_init])
        sc.flush()

        def make_loader(stack, nw_ap):
            L = {}
            xt0 = sbt(stack, "xt0", [128, D], F32); bx0 = Buf()
            L["xt"] = [xt0, xt0]; L["b_xt"] = [bx0, bx0]
            L["ss"] = sbt(stack, "ss", [128, 8], F32); L["b_ss"] = Buf()
            L["xs"] = [sbt(stack, f"xs{i}", [128, D], BF16) for i in range(2)]; L["b_xs"] = [Buf(), Buf()]
            L["nwbc"] = sbt(stack, "nwbc", [128, D], F32); L["b_nw"] = Buf()
            L["hT"] = sbt(stack, "hT", [128, 16, 515], BF16); L["b_hT"] = Buf()
            sc.op("sp", lambda e: e.dma_start(out=L["nwbc"][:], in_=nw_ap[0:1, :].partition_broadcast(128)), writes=[L["b_nw"]], dma=True)
            L["n"] = 0
            return L

        def load_block(L, xsrc, a, halo=True):
            hT = L["hT"]; b_hT = L["b_hT"]
            subs = [0, 1, 2, 3] + ([4] if halo else [])
            first = True
            for s_ in subs:
                i = L["n"] % 2; L["n"] += 1
                if s_ < 4:
                    P = 128; xt = L["xt"][i]; bxt = L["b_xt"][i]
                    sc.op("sp", lambda e, xt=xt, s_=s_: e.dma_start(out=xt[:], in_=xsrc[a + 2 + s_ * 128:a + 2 + (s_ + 1) * 128, :]), writes=[bxt], dma=True)
                else:
                    P = 3; xt = L["xt"][i]; bxt = L["b_xt"][i]
                    sc.op("sp", lambda e, xt=xt: e.dma_start(out=xt[0:2, :], in_=xsrc[a:a + 2, :]), writes=[bxt], dma=True)
                    sc.op("sp", lambda e, xt=xt: e.dma_start(out=xt[2:3, :], in_=xsrc[a + 514:a + 515, :]), pwrites=[bxt], dma=True)
                ss = L["ss"]; xs = L["xs"][i]; bxs = L["b_xs"][i]
                sc.op("act", lambda e, xt=xt, xs=xs, P=P: e.activation(out=xs[0:P, :], in_=xt[0:P, :], func=AF.Square, accum_out=ss[0:P, 0:1]),
                      reads=[bxt], writes=[bxs, L["b_ss"]])
                sc.op("act", lambda e, P=P: e.activation(out=ss[0:P, 1:2], in_=ss[0:P, 0:1], func=AF.Ln, scale=1.0 / D, bias=1e-6),
                      reads=[L["b_ss"]], pwrites=[L["b_ss"]])
                sc.op("act", lambda e, P=P: e.activation(out=ss[0:P, 2:3], in_=ss[0:P, 1:2], func=AF.Exp, scale=-0.5),
                      reads=[L["b_ss"]], pwrites=[L["b_ss"]])
                sc.op("dve", lambda e, xt=xt, xs=xs, P=P: e.scalar_tensor_tensor(xs[0:P, :], xt[0:P, :], ss[0:P, 2:3], L["nwbc"][0:P, :], ALU.mult, ALU.mult),
                      reads=[bxt, L["b_ss"], L["b_nw"]], writes=[bxs])
                if s_ < 4:
                    for half in range(2):
                        pb, bb = nps(); pv = pb[:].bitcast(BF16)
                        for k8 in range(8):
                            kc = half * 8 + k8
                            sc.op("pe", lambda e, pv=pv, k8=k8, kc=kc, xs=xs: e.transpose(pv[:, k8 * 128:(k8 + 1) * 128], xs[:, kc * 128:(kc + 1) * 128], ident[:]),
                                  reads=[bxs], **wkw(k8 == 0, bb))
                        sc.op("act", lambda e, pv=pv, half=half, s_=s_: e.activation(out=hT[:, half * 8:half * 8 + 8, s_ * 128:(s_ + 1) * 128],
                                                                          in_=pv.rearrange("p (k t) -> p k t", k=8), func=AF.Copy),
                              reads=[bb], **wkw(first, b_hT))
                        first = False
                else:
                    pb, bb = nps(); pv = pb[:].bitcast(BF16)
                    for kc in range(16):
                        sc.op("pe", lambda e, pv=pv, kc=kc, xs=xs: e.transpose(pv[:, kc * 4:kc * 4 + 3], xs[0:3, kc * 128:(kc + 1) * 128], ident[0:3, 0:3]),
                              reads=[bxs], **wkw(kc == 0, bb))
                    sc.op("dve", lambda e, pv=pv: e.tensor_copy(hT[:, :, 512:515], pv[:, 0:64].rearrange("p (k t) -> p k t", t=4)[:, :, 0:3]),
                          reads=[bb], pwrites=[b_hT])

        def rope_alloc(stack, tag):
            R = {"posi": sbt(stack, f"posi{tag}", [64, 512], I32), "ang": sbt(stack, f"ang{tag}", [64, 512], F32),
                 "cs2": sbt(stack, f"cs2{tag}", [64, 512], F32), "sn2": sbt(stack, f"sn2{tag}", [64, 512], F32),
                 "invf": sbt(stack, f"invf{tag}", [64, 1], F32), "b": Buf(), "bi": Buf()}
            sc.op("sp", lambda e: e.dma_start(out=R["invf"][:], in_=invf_d[:, :]), writes=[R["bi"]], dma=True)
            return R

        def rope_block(R, pos_ap, a0):
            posi, ang, b = R["posi"], R["ang"], R["b"]
            sn2, cs2 = R["sn2"], R["cs2"]
            C1 = 6.28125; C2 = 2 * np.pi - 6.28125
            sc.op("sp", lambda e: e.dma_start(out=posi[:], in_=pos_ap[0:1, a0:a0 + 512].partition_broadcast(64)), writes=[b], dma=True)
            sc.op("dve", lambda e: e.tensor_copy(ang[:], posi[:]), reads=[b], pwrites=[b])
            sc.op("dve", lambda e: e.tensor_scalar(ang[:], ang[:], R["invf"][:, 0:1], None, ALU.mult), reads=[b, R["bi"]], pwrites=[b])
            sc.op("dve", lambda e: e.tensor_scalar(cs2[:], ang[:], float(1 / (2 * np.pi)), None, ALU.mult), reads=[b], pwrites=[b])
            sc.op("dve", lambda e: e.tensor_copy(posi[:], cs2[:]), reads=[b], pwrites=[b])
            sc.op("dve", lambda e: e.tensor_copy(cs2[:], posi[:]), reads=[b], pwrites=[b])
            sc.op("dve", lambda e: e.scalar_tensor_tensor(sn2[:], cs2[:], float(-C1), ang[:], ALU.mult, ALU.add), reads=[b], pwrites=[b])
            sc.op("dve", lambda e: e.scalar_tensor_tensor(sn2[:], cs2[:], float(-C2), sn2[:], ALU.mult, ALU.add), reads=[b], pwrites=[b])
            sc.op("dve", lambda e: e.tensor_scalar(cs2[:], sn2[:], float(np.pi), None, ALU.is_gt), reads=[b], pwrites=[b])
            sc.op("dve", lambda e: e.scalar_tensor_tensor(sn2[:], cs2[:], float(-2 * np.pi), sn2[:], ALU.mult, ALU.add), reads=[b], pwrites=[b])
            sc.op("dve", lambda e: e.tensor_scalar(cs2[:], sn2[:], float(-np.pi), None, ALU.is_lt), reads=[b], pwrites=[b])
            sc.op("dve", lambda e: e.scalar_tensor_tensor(sn2[:], cs2[:], float(2 * np.pi), sn2[:], ALU.mult, ALU.add), reads=[b], pwrites=[b])
            sc.op("dve", lambda e: e.tensor_scalar(ang[:], sn2[:], float(np.pi / 2), None, ALU.add), reads=[b], pwrites=[b])
            sc.op("dve", lambda e: e.tensor_scalar(cs2[:], ang[:], float(np.pi), None, ALU.is_gt), reads=[b], pwrites=[b])
            sc.op("dve", lambda e: e.scalar_tensor_tensor(ang[:], cs2[:], float(-2 * np.pi), ang[:], ALU.mult, ALU.add), reads=[b], pwrites=[b])
            sc.op("act", lambda e: e.activation(out=cs2[:], in_=ang[:], func=AF.Sin), reads=[b], pwrites=[b])
            sc.op("act", lambda e: e.activation(out=sn2[:], in_=sn2[:], func=AF.Sin), reads=[b], pwrites=[b])

        def t_rstd(srcs, bsrc, sqt, b_sq, rst, b_rst, n, dim):
            pb, bb = nps()
            for i, sap in enumerate(srcs):
                sc.op("act", lambda e, sap=sap, i=i: e.activation(out=sqt[:, i, 0:n], in_=sap, func=AF.Square), reads=[bsrc], **wkw(i == 0, b_sq))
            for i in range(len(srcs)):
                sc.op("pe", lambda e, i=i, pb=pb: e.matmul(pb[:, 0:n], onesb[:], sqt[:, i, 0:n], start=(i == 0), stop=(i == len(srcs) - 1)),
                      reads=[b_sq], **wkw(i == 0, bb))
            sc.op("act", lambda e, pb=pb: e.activation(out=rst[:, 0:n], in_=pb[:, 0:n], func=AF.Ln, scale=1.0 / dim, bias=1e-6), reads=[bb], writes=[b_rst])
            sc.op("act", lambda e: e.activation(out=rst[:, 0:n], in_=rst[:, 0:n], func=AF.Exp, scale=-0.5), reads=[b_rst], pwrites=[b_rst])

        ph = contextlib.ExitStack()
        with ph:
            L = make_loader(ph, norm_w)
            wkv = sbt(ph, "wkv", [128, 16, 640], BF16); b_wkv = Buf()
            wuk = sbt(ph, "wuk", [128, 4, 16, 128], BF16); wuv = sbt(ph, "wuv", [128, 4, 16, 128], BF16); b_wu = Buf()
            kvnw = sbt(ph, "kvnw", [128, 4], F32)
            ckv = sbt(ph, "ckv", [128, 4, 512], F32); b_ckv = Buf()
            sqt = sbt(ph, "sqt", [128, 4, 512], BF16); b_sq = Buf()
            rst = sbt(ph, "rst", [128, 512], F32); b_rst = Buf()
            ckn = sbt(ph, "ckn", [128, 4, 512], BF16); b_ckn = Buf()
            krr = sbt(ph, "krr", [64, 2, 512], F32); b_krr = Buf()
            krb = sbt(ph, "krb", [64, 512], BF16); b_krb = Buf()
            kst = [sbt(ph, f"kst{i}", [128, 512], BF16) for i in range(4)]; b_kst = [Buf() for _ in range(4)]
            vst = [sbt(ph, f"vst{i}", [128, 512], BF16) for i in range(4)]; b_vst = [Buf() for _ in range(4)]
            RK = rope_alloc(ph, "k"); cs2 = RK["cs2"]; sn2 = RK["sn2"]; b_rope = RK["b"]
            sc.op("pool", lambda e: e.dma_start(out=wkv[:, :, 0:576], in_=w_in[:, C_KV:C_KV + 576].rearrange("(k p) c -> p k c", p=128)), writes=[b_wkv], dma=True)
            sc.op("dve", lambda e: e.tensor_scalar(wkv[:, :, 576:608], wkv[:, :, 544:576], -1.0, None, ALU.mult), reads=[b_wkv], pwrites=[b_wkv])
            sc.op("dve", lambda e: e.tensor_copy(wkv[:, :, 608:640], wkv[:, :, 512:544]), reads=[b_wkv], pwrites=[b_wkv])
            for kc in range(4):
                sc.op("pool", lambda e, kc=kc: e.dma_start(out=wuk[:, kc, :, :], in_=w_ukv[kc * 128:(kc + 1) * 128, :].rearrange("p (h t d) -> p h t d", t=2, d=128)[:, :, 0, :]), pwrites=[b_wu], dma=True)
                sc.op("pool", lambda e, kc=kc: e.dma_start(out=wuv[:, kc, :, :], in_=w_ukv[kc * 128:(kc + 1) * 128, :].rearrange("p (h t d) -> p h t d", t=2, d=128)[:, :, 1, :]), pwrites=[b_wu], dma=True)
            sc.op("sp", lambda e: e.dma_start(out=kvnw[:], in_=kv_norm_w[0:1, :].rearrange("o (k p) -> p (o k)", p=128), allow_slow_non_contiguous=True), pwrites=[b_wu], dma=True)
            for blk in range(NBK):
                a = blk * 512
                load_block(L, xkv, a, halo=False)
                rope_block(RK, pos_kv, a)
                hT = L["hT"]; b_hT = L["b_hT"]
                for cc in range(5):
                    pb, bb = nps()
                    for kc in range(16):
                        sc.op("pe", lambda e, cc=cc, kc=kc, pb=pb: e.matmul(pb[:, :], wkv[:, kc, cc * 128:(cc + 1) * 128], hT[:, kc, 0:512], start=(kc == 0), stop=(kc == 15)),
                              reads=[b_wkv, b_hT], **wkw(kc == 0, bb))
                    if cc < 4:
                        sc.op("dve", lambda e, cc=cc, pb=pb: e.tensor_copy(ckv[:, cc, :], pb[:, :]), reads=[bb], **wkw(cc == 0, b_ckv))
                    else:
                        sc.op("dve", lambda e, pb=pb: e.tensor_tensor(krr[:, 0, :], pb[0:64, :], cs2[:, 0:512], ALU.mult), reads=[bb, b_rope], writes=[b_krr])
                        sc.op("dve", lambda e, pb=pb: e.tensor_copy(krr[:, 1, :], pb[64:128, :]), reads=[bb], pwrites=[b_krr])
                        sc.op("dve", lambda e: e.tensor_tensor(krr[:, 1, :], krr[:, 1, :], sn2[:, 0:512], ALU.mult), reads=[b_krr, b_rope], pwrites=[b_krr])
                        sc.op("dve", lambda e: e.tensor_tensor(krb[:, :], krr[:, 0, :], krr[:, 1, :], ALU.add), reads=[b_krr], writes=[b_krb])
                        sc.op("sp", lambda e: e.dma_start(out=KR[:, a:a + 512], in_=krb[:, :]), reads=[b_krb], pwrites=[b_KR], dma=True)
                t_rstd([ckv[:, i, :] for i in range(4)], b_ckv, sqt, b_sq, rst, b_rst, 512, 512)
                for cc in range(4):
                    sc.op("dve", lambda e, cc=cc: e.scalar_tensor_tensor(ckn[:, cc, :], ckv[:, cc, :], kvnw[:, cc:cc + 1], rst[:, :], ALU.mult, ALU.mult),
                          reads=[b_ckv, b_rst, b_wu], **wkw(cc == 0, b_ckn))
                for h in range(16):
                    pb, bb = nps()
                    for kc in range(4):
                        sc.op("pe", lambda e, h=h, kc=kc, pb=pb: e.matmul(pb[:, :], wuk[:, kc, h, :], ckn[:, kc, :], start=(kc == 0), stop=(kc == 3)),
                              reads=[b_wu, b_ckn], **wkw(kc == 0, bb))
                    i = h % 4
                    eng = "act" if h % 2 == 0 else "dve"
                    if eng == "act":
                        sc.op("act", lambda e, pb=pb, i=i: e.activation(out=kst[i][:, :], in_=pb[:, :], func=AF.Copy), reads=[bb], writes=[b_kst[i]])
                    else:
                        sc.op("dve", lambda e, pb=pb, i=i: e.tensor_copy(kst[i][:, :], pb[:, :]), reads=[bb], writes=[b_kst[i]])
                    sc.op("sp", lambda e, h=h, i=i: e.dma_start(out=KT[h, :, a:a + 512], in_=kst[i][:, :]), reads=[b_kst[i]], pwrites=[b_KT], dma=True)
                for s_ in range(4):
                    for hg in range(4):
                        pb, bb = nps()
                        for kc in range(4):
                            sc.op("pe", lambda e, s_=s_, hg=hg, kc=kc, pb=pb: e.matmul(pb[:, :], ckn[:, kc, s_ * 128:(s_ + 1) * 128],
                                                                                 wuv[:, kc, hg * 4:hg * 4 + 4, :].rearrange("p h d -> p (h d)"), start=(kc == 0), stop=(kc == 3)),
                                  reads=[b_wu, b_ckn], **wkw(kc == 0, bb))
                        i = hg
                        if (s_ + hg) % 2 == 0:
                            sc.op("act", lambda e, pb=pb, i=i: e.activation(out=vst[i][:, :], in_=pb[:, :], func=AF.Copy), reads=[bb], writes=[b_vst[i]])
                        else:
                            sc.op("dve", lambda e, pb=pb, i=i: e.tensor_copy(vst[i][:, :], pb[:, :]), reads=[bb], writes=[b_vst[i]])
                        t0 = a + s_ * 128
                        sc.op("sp", lambda e, hg=hg, i=i, t0=t0: e.dma_start(out=VV[hg * 4:hg * 4 + 4, t0:t0 + 128, :].rearrange("h t d -> t h d"),
                                                                         in_=vst[i][:, :].rearrange("p (h d) -> p h d", d=128)), reads=[b_vst[i]], pwrites=[b_VV], dma=True)
            sc.flush()

        ph = contextlib.ExitStack()
        with ph:
            L = make_loader(ph, norm_w)
            wst0 = sbt(ph, "wst0", [128, 16, 256], BF16); bw0 = Buf(); wst = [wst0, wst0]; b_wst = [bw0, bw0]
            wdt = sbt(ph, "wdt", [128, 16, 64], BF16); cw = sbt(ph, "cw", [128, 24, 4], F32); cb = sbt(ph, "cb", [128, 24], F32)
            abc = sbt(ph, "abc", [128, 64], F32); dtbbc = sbt(ph, "dtbbc", [128, 64], F32); dskbc = sbt(ph, "dskbc", [128, 32], F32)
            snwbc = sbt(ph, "snwbc", [128, D], F32)
            tri = sbt(ph, "tri", [128, 2, 128], F32); negm = sbt(ph, "negm", [128, 2, 128], BF16); negf = sbt(ph, "negf", [128, 2, 128], F32)
            vfl = sbt(ph, "vfl", [128, 2, NCH], F32)
            b_pc = Buf()
            sc.op("pool", lambda e: e.dma_start(out=wdt[:, :, :], in_=w_in[:, C_DTF:C_DTF + 64].rearrange("(k p) c -> p k c", p=128)), pwrites=[b_pc], dma=True)
            for k in range(4):
                sc.op("sp", lambda e, k=k: e.dma_start(out=cw[:, :, k], in_=conv_w[k:k + 1, :].rearrange("o (c p) -> p (o c)", p=128), allow_slow_non_contiguous=True), pwrites=[b_pc], dma=True)
            sc.op("sp", lambda e: e.dma_start(out=cb[:, :], in_=conv_b[0:1, :].rearrange("o (c p) -> p (o c)", p=128), allow_slow_non_contiguous=True), pwrites=[b_pc], dma=True)
            sc.op("sp", lambda e: e.dma_start(out=abc[:], in_=a_log[0:1, :].partition_broadcast(128)), pwrites=[b_pc], dma=True)
            sc.op("sp", lambda e: e.dma_start(out=dtbbc[:], in_=dt_bias[0:1, :].partition_broadcast(128)), pwrites=[b_pc], dma=True)
            sc.op("sp", lambda e: e.dma_start(out=dskbc[:], in_=d_skip[0:1, :].partition_broadcast(128)), pwrites=[b_pc], dma=True)
            sc.op("sp", lambda e: e.dma_start(out=snwbc[:], in_=ssm_norm_w[0:1, :].partition_broadcast(128)), pwrites=[b_pc], dma=True)
            sc.op("sp", lambda e: e.dma_start(out=tri[:, :, :], in_=tri_d[:, :, :].rearrange("t k l -> k t l")), pwrites=[b_pc], dma=True)
            sc.op("sp", lambda e: e.dma_start(out=negf[:, :, :], in_=neg_d[:, :, :].rearrange("t k l -> k t l")), pwrites=[b_pc], dma=True)
            sc.op("sp", lambda e: e.dma_start(out=vfl[:, 0, :], in_=validf[:, :]), pwrites=[b_pc], dma=True)
            sc.op("sp", lambda e: e.dma_start(out=vfl[:, 1, :], in_=validb[:, :]), pwrites=[b_pc], dma=True)
            sc.op("act", lambda e: e.activation(out=abc[:], in_=abc[:], func=AF.Exp), reads=[b_pc], pwrites=[b_pc])
            sc.op("dve", lambda e: e.tensor_scalar(abc[:], abc[:], -1.0, None, ALU.mult), reads=[b_pc], pwrites=[b_pc])
            sc.op("dve", lambda e: e.tensor_copy(negm[:], negf[:]), reads=[b_pc], pwrites=[b_pc])
            XC = sbt(ph, "XC", [128, 24, 512], BF16); b_XC = Buf()
            uext = [sbt(ph, f"uext{i}", [128, 516], F32) for i in range(2)]; b_ue = [Buf(), Buf()]
            acc = [sbt(ph, f"acc{i}", [128, 512], F32) for i in range(2)]; b_acc = [Buf(), Buf()]
            xtok = sbt(ph, "xtok", [128, D], BF16); b_xtok = Buf()
            btok = sbt(ph, "btok", [128, 512], BF16); b_btok = Buf()
            xdw = sbt(ph, "xdw", [128, D], BF16); b_xdw = Buf()
            xd = sbt(ph, "xd", [128, D], BF16); b_xd = Buf()
            sm = sbt(ph, "sm", [128, 8, 32], F32); b_sm = Buf()
            RH = sbt(ph, "RH", [128, 8, 128], F32); b_RH = Buf()
            LH = sbt(ph, "LH", [128, 8, 128], BF16); b_LH = Buf()
            G = sbt(ph, "G", [128, 8, 128], BF16); b_G = Buf()
            CBT = sbt(ph, "CBT", [128, 4, 128], BF16); b_CBT = Buf()
            Sst = sbt(ph, "Sst", [128, 4, 512], F32); b_S = Buf()
            Sbf = sbt(ph, "Sbf", [128, 4, 512], BF16); b_Sbf = Buf()
            ysb = sbt(ph, "ysb", [128, D], F32); b_ysb = Buf()
            yfl = sbt(ph, "yfl", [128, D], F32); b_yfl = Buf()
            tmpg = sbt(ph, "tmpg", [128, 512], F32); b_tmpg = Buf()
            zs = sbt(ph, "zs", [128, D], BF16); b_zs = Buf()
            yn = sbt(ph, "yn", [128, D], BF16); b_yn = Buf()
            gss = sbt(ph, "gss", [128, 12], F32); b_gss = Buf()
            ynT = sbt(ph, "ynT", [128, 16, 128], BF16); b_ynT = Buf()
            est = [sbt(ph, f"est{i}", [128, 512], BF16) for i in range(2)]; b_est = [Buf(), Buf()]
            wuqh = [sbt(ph, f"wuqh{i}", [128, 6, 256], BF16) for i in range(2)]; b_wuqh = [Buf(), Buf()]; b_wuq = Buf()
            qnw = sbt(ph, "qnw", [128, 6], F32)
            qa = sbt(ph, "qa", [128, 6, 512], BF16); b_qa = Buf()
            sqt = sbt(ph, "sqt2", [128, 6, 512], BF16); b_sq = Buf()
            rst = sbt(ph, "rst2", [128, 512], F32); b_rst = Buf()
            cq = sbt(ph, "cq", [128, 6, 512], BF16); b_cq = Buf()
            qrr = sbt(ph, "qrr", [64, 2, 512], F32); b_qrr = Buf()
            RO = rope_alloc(ph, "o"); cs2o = RO["cs2"]; sn2o = RO["sn2"]; b_ropeo = RO["b"]
            sc.op("sp", lambda e: e.dma_start(out=qnw[:], in_=q_norm_w[0:1, :].rearrange("o (k p) -> p (o k)", p=128), allow_slow_non_contiguous=True), pwrites=[b_wuq], dma=True)
            wn = [0]

            def wload(col0):
                i = wn[0] % 2; wn[0] += 1
                sc.op("pool", lambda e, i=i: e.dma_start(out=wst[i][:, :, :], in_=w_in[:, col0:col0 + 256].rearrange("(k p) c -> p k c", p=128)), writes=[b_wst[i]], dma=True)
                return wst[i], b_wst[i]

            def proj_T(col0, ncols, evac, hT, b_hT):
                for c0 in range(0, ncols, 256):
                    wt, bw = wload(col0 + c0)
                    for cc in range(2):
                        pb, bb = nps()
                        for kc in range(16):
                            sc.op("pe", lambda e, wt=wt, cc=cc, kc=kc, pb=pb: e.matmul(pb[:, :], wt[:, kc, cc * 128:(cc + 1) * 128], hT[:, kc, 0:512], start=(kc == 0), stop=(kc == 15)),
                                  reads=[bw, b_hT], **wkw(kc == 0, bb))
                        evac(c0 // 128 + cc, pb, bb, wt, bw, cc)

            for dr in range(2):
                xsrc = xfw if dr == 0 else xbw
                sc.op("dve", lambda e: e.memset(Sst[:], 0.0), writes=[b_S])
                blks = list(range(NBS)) if dr == 0 else list(range(NBS - 1, -1, -1))
                for blk in blks:
                    own = (blk >= NBS - NBO) if dr == 0 else (blk < NBO)
                    ao = (blk - (NBS - NBO)) * 512 if dr == 0 else blk * 512
                    a = blk * 512
                    load_block(L, xsrc, a, halo=True)
                    hT = L["hT"]; b_hT = L["b_hT"]
                    ncc = 24 if own else 20
                    ucnt = [0]

                    def conv_evac(c, pb, bb, wt, bw, cc):
                        i = ucnt[0] % 2; ucnt[0] += 1
                        ph_, bh = nps()
                        for kc in range(16):
                            sc.op("pe", lambda e, kc=kc, ph_=ph_: e.matmul(ph_[:, 0:3], wt[:, kc, cc * 128:(cc + 1) * 128], hT[:, kc, 512:515], start=(kc == 0), stop=(kc == 15)),
                                  reads=[bw, b_hT], **wkw(kc == 0, bh))
                        ue = uext[i]; bu = b_ue[i]; ac = acc[i]; ba = b_acc[i]
                        sc.op("act", lambda e: e.activation(out=ue[:, 2:514], in_=pb[:, :], func=AF.Copy), reads=[bb], writes=[bu])
                        sc.op("dve", lambda e: e.tensor_copy(ue[:, 0:2], ph_[:, 0:2]), reads=[bh], pwrites=[bu])
                        sc.op("dve", lambda e: e.tensor_copy(ue[:, 514:515], ph_[:, 2:3]), reads=[bh], pwrites=[bu])
                        sc.op("dve", lambda e: e.tensor_scalar(ac[:, :], ue[:, 0:512], cw[:, c, 0:1], None, ALU.mult), reads=[bu, b_pc], writes=[ba])
                        for k in range(1, 4):
                            sc.op("dve", lambda e, k=k: e.scalar_tensor_tensor(ac[:, :], ue[:, k:k + 512], cw[:, c, k:k + 1], ac[:, :], ALU.mult, ALU.add), reads=[bu, ba], pwrites=[ba])
                        sc.op("act", lambda e: e.activation(out=XC[:, c, :], in_=ac[:, :], func=AF.Silu, bias=cb[:, c:c + 1]), reads=[ba, b_pc], **wkw(c == 0, b_XC))
                    proj_T(C_XS, ncc * 128, conv_evac, hT, b_hT)

                    if own and dr == 0:
                        ecnt = [0]

                        def mk_evac(dst, bdst, func):
                            def ev(c, pb, bb, wt, bw, cc):
                                i = ecnt[0] % 2; ecnt[0] += 1
                                sc.op("act", lambda e: e.activation(out=est[i][:, :], in_=pb[:, :], func=func), reads=[bb], writes=[b_est[i]])
                                sc.op("sp", lambda e: e.dma_start(out=dst[c * 128:(c + 1) * 128, ao:ao + 512], in_=est[i][:, :]), reads=[b_est[i]], pwrites=[bdst], dma=True)
                            return ev
                        proj_T(C_ZA, 2048, mk_evac(ZAT, b_ZAT, AF.Silu), hT, b_hT)
                        proj_T(C_GS, 2048, mk_evac(GST, b_GST, AF.Sigmoid), hT, b_hT)
                        proj_T(C_GA, 2048, mk_evac(GAT, b_GAT, AF.Sigmoid), hT, b_hT)

                        def qa_evac(c, pb, bb, wt, bw, cc):
                            sc.op("dve", lambda e: e.tensor_copy(qa[:, c, :], pb[:, :]), reads=[bb], **wkw(c == 0, b_qa))
                        proj_T(C_QA, 768, qa_evac, hT, b_hT)
                        rope_block(RO, pos_own, ao)
                        t_rstd([qa[:, i, :] for i in range(6)], b_qa, sqt, b_sq, rst, b_rst, 512, 768)
                        for c in range(6):
                            sc.op("dve", lambda e, c=c: e.scalar_tensor_tensor(cq[:, c, :], qa[:, c, :], qnw[:, c:c + 1], rst[:, :], ALU.mult, ALU.mult),
                                  reads=[b_qa, b_rst, b_wuq], **wkw(c == 0, b_cq))
                        for h in range(16):
                            i = h % 2
                            wq = wuqh[i]; bwq = b_wuqh[i]
                            sc.op("pool", lambda e, h=h, wq=wq: e.dma_start(out=wq[:, :, 0:192], in_=w_uq[:, h * 192:(h + 1) * 192].rearrange("(k p) c -> p k c", p=128)), writes=[bwq], dma=True)
                            sc.op("dve", lambda e, wq=wq: e.tensor_scalar(wq[:, :, 192:224], wq[:, :, 160:192], -1.0, None, ALU.mult), reads=[bwq], pwrites=[bwq])
                            sc.op("dve", lambda e, wq=wq: e.tensor_copy(wq[:, :, 224:256], wq[:, :, 128:160]), reads=[bwq], pwrites=[bwq])
                            pb, bb = nps()
                            for kc in range(6):
                                sc.op("pe", lambda e, wq=wq, kc=kc, pb=pb: e.matmul(pb[:, :], wq[:, kc, 0:128], cq[:, kc, :], start=(kc == 0), stop=(kc == 5)),
                                      reads=[bwq, b_cq], **wkw(kc == 0, bb))
                            sc.op("act", lambda e, pb=pb, i=i: e.activation(out=est[i][:, :], in_=pb[:, :], func=AF.Copy), reads=[bb], writes=[b_est[i]])
                            sc.op("sp", lambda e, h=h, i=i: e.dma_start(out=QT[h, :, ao:ao + 512], in_=est[i][:, :]), reads=[b_est[i]], pwrites=[b_QT], dma=True)
                            pb, bb = nps()
                            for kc in range(6):
                                sc.op("pe", lambda e, wq=wq, kc=kc, pb=pb: e.matmul(pb[:, :], wq[:, kc, 128:256], cq[:, kc, :], start=(kc == 0), stop=(kc == 5)),
                                      reads=[bwq, b_cq], **wkw(kc == 0, bb))
                            sc.op("dve", lambda e, pb=pb: e.tensor_tensor(qrr[:, 0, :], pb[0:64, :], cs2o[:, 0:512], ALU.mult), reads=[bb, b_ropeo], writes=[b_qrr])
                            sc.op("dve", lambda e, pb=pb: e.tensor_copy(qrr[:, 1, :], pb[64:128, :]), reads=[bb], pwrites=[b_qrr])
                            sc.op("dve", lambda e: e.tensor_tensor(qrr[:, 1, :], qrr[:, 1, :], sn2o[:, 0:512], ALU.mult), reads=[b_qrr, b_ropeo], pwrites=[b_qrr])
                            j = (h + 1) % 2
                            sc.op("dve", lambda e, j=j: e.tensor_tensor(est[j][0:64, :], qrr[:, 0, :], qrr[:, 1, :], ALU.add), reads=[b_qrr], writes=[b_est[j]])
                            sc.op("sp", lambda e, h=h, j=j: e.dma_start(out=QR[h, :, ao:ao + 512], in_=est[j][0:64, :]), reads=[b_est[j]], pwrites=[b_QR], dma=True)

                    chs = [0, 1, 2, 3] if dr == 0 else [3, 2, 1, 0]
                    for ch in chs:
                        cg = blk * 4 + ch
                        t0 = ch * 128
                        for half in range(2):
                            pb, bb = nps(); pv = pb[:].bitcast(BF16)
                            for k8 in range(8):
                                c = half * 8 + k8
                                sc.op("pe", lambda e, pv=pv, k8=k8, c=c: e.transpose(pv[:, k8 * 128:(k8 + 1) * 128], XC[:, c, t0:t0 + 128], ident[:]),
                                      reads=[b_XC], **wkw(k8 == 0, bb))
                            sc.op("act", lambda e, pv=pv, half=half: e.activation(out=xtok[:, half * 1024:(half + 1) * 1024], in_=pv[:, :], func=AF.Copy),
                                  reads=[bb], **wkw(half == 0, b_xtok))
                        pb, bb = nps(); pv = pb[:].bitcast(BF16)
                        for k4 in range(4):
                            sc.op("pe", lambda e, pv=pv, k4=k4: e.transpose(pv[:, k4 * 128:(k4 + 1) * 128], XC[:, 16 + k4, t0:t0 + 128], ident[:]),
                                  reads=[b_XC], **wkw(k4 == 0, bb))
                        sc.op("dve", lambda e, pv=pv: e.tensor_copy(btok[:, :], pv[:, 0:512]), reads=[bb], writes=[b_btok])
                        pb, bb = nps()
                        for kc in range(16):
                            sc.op("pe", lambda e, kc=kc, pb=pb: e.matmul(pb[:, 0:32], hT[:, kc, t0:t0 + 128], wdt[:, kc, dr * 32:dr * 32 + 32], start=(kc == 0), stop=(kc == 15)),
                                  reads=[b_hT, b_pc], **wkw(kc == 0, bb))
                        sc.op("dve", lambda e, pb=pb: e.tensor_tensor(sm[:, 0, :], pb[:, 0:32], dtbbc[:, dr * 32:dr * 32 + 32], ALU.add), reads=[bb, b_pc], writes=[b_sm])
                        sc.op("act", lambda e: e.activation(out=sm[:, 0, :], in_=sm[:, 0, :], func=AF.Exp), reads=[b_sm], pwrites=[b_sm])
                        sc.op("act", lambda e: e.activation(out=sm[:, 0, :], in_=sm[:, 0, :], func=AF.Ln, bias=1.0), reads=[b_sm], pwrites=[b_sm])
                        sc.op("dve", lambda e: e.tensor_tensor(sm[:, 1, :], sm[:, 0, :], abc[:, dr * 32:dr * 32 + 32], ALU.mult), reads=[b_sm, b_pc], pwrites=[b_sm])
                        pP, bP = nps()
                        sc.op("pe", lambda e, pP=pP: e.matmul(pP[:, 0:32], tri[:, dr, :], sm[:, 1, :], start=True, stop=True), reads=[b_sm, b_pc], writes=[bP])
                        pT, bT = nps()
                        sc.op("pe", lambda e, pT=pT: e.matmul(pT[:, 0:32], onesf[:, :], sm[:, 1, :], start=True, stop=True), reads=[b_sm], writes=[bT])
                        sc.op("dve", lambda e, pP=pP: e.tensor_copy(sm[:, 2, :], pP[:, 0:32]), reads=[bP], pwrites=[b_sm])
                        sc.op("dve", lambda e, pT=pT: e.tensor_tensor(sm[:, 3, :], pT[:, 0:32], sm[:, 2, :], ALU.subtract), reads=[bT, b_sm], pwrites=[b_sm])
                        sc.op("act", lambda e: e.activation(out=sm[:, 3, :], in_=sm[:, 3, :], func=AF.Exp), reads=[b_sm], pwrites=[b_sm])
                        sc.op("act", lambda e, pT=pT: e.activation(out=sm[:, 4, :], in_=pT[:, 0:32], func=AF.Exp), reads=[bT], pwrites=[b_sm])
                        sc.op("dve", lambda e, cg=cg: e.scalar_tensor_tensor(sm[:, 5, :], sm[:, 0, :], vfl[:, dr, cg:cg + 1], sm[:, 3, :], ALU.mult, ALU.mult), reads=[b_sm, b_pc], pwrites=[b_sm])
                        sc.op("dve", lambda e: e.tensor_tensor(xdw[:, :].rearrange("p (h d) -> p h d", d=64), xtok[:, :].rearrange("p (h d) -> p h d", d=64),
                                                               bc(sm[:, 5, :], [128, 32, 64], 2), ALU.mult), reads=[b_xtok, b_sm], writes=[b_xdw])
                        if own:
                            l0 = ao + t0
                            sc.op("act", lambda e: e.activation(out=sm[:, 6, :], in_=sm[:, 2, :], func=AF.Exp), reads=[b_sm], pwrites=[b_sm])
                            sc.op("dve", lambda e: e.tensor_scalar(sm[:, 7, :], sm[:, 2, :], -1.0, None, ALU.mult), reads=[b_sm], pwrites=[b_sm])
                            sc.op("dve", lambda e: e.tensor_tensor(xd[:, :].rearrange("p (h d) -> p h d", d=64), xtok[:, :].rearrange("p (h d) -> p h d", d=64),
                                                                   bc(sm[:, 0, :], [128, 32, 64], 2), ALU.mult), reads=[b_xtok, b_sm], writes=[b_xd])
                            for g in range(4):
                                sc.op("act", lambda e, g=g: e.activation(out=Sbf[:, g, :], in_=Sst[:, g, :], func=AF.Copy), reads=[b_S], **wkw(g == 0, b_Sbf))
                            pb, bb = nps()
                            for g in range(4):
                                sc.op("pe", lambda e, g=g, pb=pb: e.matmul(pb[:, g * 128:(g + 1) * 128], XC[:, 16 + g, t0:t0 + 128], XC[:, 20 + g, t0:t0 + 128], start=True, stop=True),
                                      reads=[b_XC], **wkw(g == 0, bb))
                            sc.op("act", lambda e, pb=pb: e.activation(out=CBT[:, :, :], in_=pb[:, :].rearrange("p (g l) -> p g l", g=4), func=AF.Copy), reads=[bb], writes=[b_CBT])
                            if dr == 0:
                                sc.op("dve", lambda e: e.tensor_tensor(yn[:, :].rearrange("p (h d) -> p h d", d=64), xtok[:, :].rearrange("p (h d) -> p h d", d=64),
                                                                       bc(dskbc[:, :], [128, 32, 64], 2), ALU.mult), reads=[b_xtok, b_pc], writes=[b_yn])
                            else:
                                sc.op("sp", lambda e, l0=l0: e.dma_start(out=yfl[:, :], in_=YF[l0:l0 + 128, :]), reads=[b_YF], writes=[b_yfl], dma=True)
                            for g in range(4):
                                sc.op("dve", lambda e, g=g: e.tensor_tensor(RH[:, :, :], bc(tri[:, dr, :], [128, 8, 128], 1), bc(sm[:, 1, g * 8:(g + 1) * 8], [128, 8, 128], 2), ALU.mult),
                                      reads=[b_sm, b_pc], writes=[b_RH])
                                for q in range(2):
                                    pb, bb = nps()
                                    sc.op("pe", lambda e, q=q, pb=pb: e.matmul(pb[:, :], onesf[:, :], RH[:, q * 4:(q + 1) * 4, :].rearrange("p h l -> p (h l)"), start=True, stop=False),
                                          reads=[b_RH], writes=[bb])
                                    sc.op("pe", lambda e, pb=pb: e.matmul(pb[:, :].rearrange("p (h l) -> p h l", h=4), ident[:, :], bc(negm[:, dr, :], [128, 4, 128], 1), start=False, stop=True),
                                          reads=[b_pc], pwrites=[bb])
                                    for hh in range(4):
                                        h = g * 8 + q * 4 + hh
                                        sc.op("act", lambda e, pb=pb, hh=hh, h=h, q=q: e.activation(out=LH[:, q * 4 + hh, :], in_=pb[:, hh * 128:(hh + 1) * 128], func=AF.Exp, bias=sm[:, 7, h:h + 1]),
                                              reads=[bb, b_sm], **wkw(q == 0 and hh == 0, b_LH))
                                sc.op("dve", lambda e, g=g: e.tensor_tensor(G[:, :, :], LH[:, :, :], bc(CBT[:, g, :], [128, 8, 128], 1), ALU.mult), reads=[b_LH, b_CBT], writes=[b_G])
                                pY, bY = nps()
                                if dr == 0:
                                    sc.op("pe", lambda e, g=g, pY=pY: e.matmul(pY[:, :], ident[:, :], yn[:, g * 512:(g + 1) * 512], start=True, stop=False), reads=[b_yn], writes=[bY])
                                for hh in range(8):
                                    h = g * 8 + hh
                                    sc.op("pe", lambda e, pY=pY, hh=hh, h=h: e.matmul(pY[:, hh * 64:(hh + 1) * 64], G[:, hh, :], xd[:, h * 64:(h + 1) * 64], start=(dr == 1), stop=True),
                                          reads=[b_G, b_xd], **wkw(dr == 1 and hh == 0, bY))
                                pO, bO = nps()
                                sc.op("pe", lambda e, g=g, pO=pO: e.matmul(pO[:, :], XC[:, 20 + g, t0:t0 + 128], Sbf[:, g, :], start=True, stop=True), reads=[b_XC, b_Sbf], writes=[bO])
                                sc.op("dve", lambda e, g=g, pO=pO: e.tensor_tensor(tmpg[:, :].rearrange("p (h d) -> p h d", d=64), pO[:, :].rearrange("p (h d) -> p h d", d=64),
                                                                            bc(sm[:, 6, g * 8:(g + 1) * 8], [128, 8, 64], 2), ALU.mult), reads=[bO, b_sm], writes=[b_tmpg])
                                sc.op("dve", lambda e, g=g, pY=pY: e.tensor_tensor(ysb[:, g * 512:(g + 1) * 512], pY[:, :], tmpg[:, :], ALU.add), reads=[bY, b_tmpg], **wkw(g == 0, b_ysb))
                            if dr == 0:
                                sc.op("sp", lambda e, l0=l0: e.dma_start(out=YF[l0:l0 + 128, :], in_=ysb[:, :]), reads=[b_ysb], pwrites=[b_YF], dma=True)
                            else:
                                for cg4 in range(4):
                                    for c2 in range(2):
                                        wt, bw = wload(C_ZS + cg4 * 512 + c2 * 256)
                                        pb, bb = nps()
                                        for kc in range(16):
                                            sc.op("pe", lambda e, kc=kc, pb=pb, wt=wt: e.matmul(pb[:, 0:256], hT[:, kc, t0:t0 + 128], wt[:, kc, :], start=(kc == 0), stop=(kc == 15)),
                                                  reads=[b_hT, bw], **wkw(kc == 0, bb))
                                        o0 = cg4 * 512 + c2 * 256
                                        sc.op("act", lambda e, pb=pb, o0=o0: e.activation(out=zs[:, o0:o0 + 256], in_=pb[:, 0:256], func=AF.Silu), reads=[bb], **wkw(o0 == 0, b_zs))
                                sc.op("pool", lambda e: e.tensor_tensor(ysb[:, :], ysb[:, :], yfl[:, :], ALU.add), reads=[b_yfl, b_ysb], pwrites=[b_ysb])
                                sc.op("pool", lambda e: e.tensor_tensor(ysb[:, :], ysb[:, :], zs[:, :], ALU.mult), reads=[b_zs, b_ysb], pwrites=[b_ysb])
                                for g in range(4):
                                    sc.op("act", lambda e, g=g: e.activation(out=tmpg[:, :], in_=ysb[:, g * 512:(g + 1) * 512], func=AF.Square, accum_out=gss[:, g:g + 1]),
                                          reads=[b_ysb], writes=[b_tmpg], pwrites=[b_gss])
                                sc.op("act", lambda e: e.activation(out=gss[:, 4:8], in_=gss[:, 0:4], func=AF.Ln, scale=1.0 / 512, bias=1e-6), reads=[b_gss], pwrites=[b_gss])
                                sc.op("act", lambda e: e.activation(out=gss[:, 8:12], in_=gss[:, 4:8], func=AF.Exp, scale=-0.5), reads=[b_gss], pwrites=[b_gss])
                                for g in range(4):
                                    sc.op("dve", lambda e, g=g: e.scalar_tensor_tensor(yn[:, g * 512:(g + 1) * 512], ysb[:, g * 512:(g + 1) * 512], gss[:, 8 + g:9 + g], snwbc[:, g * 512:(g + 1) * 512], ALU.mult, ALU.mult),
                                          reads=[b_ysb, b_gss, b_pc], **wkw(g == 0, b_yn))
                                for half in range(2):
                                    pb, bb = nps(); pv = pb[:].bitcast(BF16)
                                    for k8 in range(8):
                                        c = half * 8 + k8
                                        sc.op("pe", lambda e, pv=pv, k8=k8, c=c: e.transpose(pv[:, k8 * 128:(k8 + 1) * 128], yn[:, c * 128:(c + 1) * 128], ident[:]),
                                              reads=[b_yn], **wkw(k8 == 0, bb))
                                    sc.op("act", lambda e, pv=pv, half=half: e.activation(out=ynT[:, half * 8:half * 8 + 8, :], in_=pv.rearrange("p (k t) -> p k t", k=8), func=AF.Copy),
                                          reads=[bb], **wkw(half == 0, b_ynT))
                                sc.op("sp", lambda e, l0=l0: e.dma_start(out=YNT[:, l0:l0 + 128].rearrange("(k p) t -> p k t", p=128), in_=ynT[:, :, :]), reads=[b_ynT], pwrites=[b_YNT], dma=True)
                        for g in range(4):
                            pb, bb = nps()
                            sc.op("pe", lambda e, g=g, pb=pb: e.matmul(pb[:, :], btok[:, g * 128:(g + 1) * 128], xdw[:, g * 512:(g + 1) * 512], start=True, stop=True),
                                  reads=[b_btok, b_xdw], writes=[bb])
                            sc.op("dve", lambda e, g=g: e.tensor_tensor(Sst[:, g, :].rearrange("p (h d) -> p h d", d=64), Sst[:, g, :].rearrange("p (h d) -> p h d", d=64),
                                                                      bc(sm[:, 4, g * 8:(g + 1) * 8], [128, 8, 64], 2), ALU.mult), reads=[b_sm, b_Sbf], pwrites=[b_S])
                            sc.op("dve", lambda e, g=g, pb=pb: e.tensor_tensor(Sst[:, g, :], Sst[:, g, :], pb[:, :], ALU.add), reads=[bb], pwrites=[b_S])
            sc.flush()

        SCALE = 1.0 / float(np.sqrt(192.0))
        NKT = S // 128
        ph = contextlib.ExitStack()
        with ph:
            krs = sbt(ph, "krs", [64, S], BF16); b_krs = Buf()
            kts = sbt(ph, "kts", [128, S], BF16); b_kts = Buf()
            vs = sbt(ph, "vs", [128, NKT, 128], BF16); b_vs = Buf()
            qts = sbt(ph, "qts", [128, 512], BF16); qrs = sbt(ph, "qrs", [64, 512], BF16); b_q = Buf()
            pt = [sbt(ph, f"pt{i}", [128, 512], BF16) for i in range(3)]; b_pt = [Buf() for _ in range(3)]
            rec = sbt(ph, "rec", [128, 512], F32); b_rec = Buf()
            of = sbt(ph, "of", [128, 512], F32); b_of = Buf()
            zat = sbt(ph, "zat", [128, 512], BF16); b_zat = Buf()
            ob = sbt(ph, "ob", [128, 512], BF16); b_ob = Buf()
            sc.op("sp", lambda e: e.dma_start(out=krs[:, :], in_=KR[:, :]), reads=[b_KR], writes=[b_krs], dma=True)
            rot = [0]; pn = [0]
            for h in range(16):
                sc.op("sp", lambda e, h=h: e.dma_start(out=kts[:, :], in_=KT[h, :, :]), reads=[b_KT], writes=[b_kts], dma=True)
                sc.op("sp", lambda e, h=h: e.dma_start(out=vs[:, :, :], in_=VV[h, :, :].rearrange("(k p) d -> p k d", p=128)), reads=[b_VV], writes=[b_vs], dma=True)
                for qb in range(SO // 512):
                    q0 = qb * 512
                    sc.op("sp", lambda e, h=h, q0=q0: e.dma_start(out=qts[:, :], in_=QT[h, :, q0:q0 + 512]), reads=[b_QT], writes=[b_q], dma=True)
                    sc.op("sp", lambda e, h=h, q0=q0: e.dma_start(out=qrs[:, :], in_=QR[h, :, q0:q0 + 512]), reads=[b_QR], pwrites=[b_q], dma=True)
                    sc.op("sp", lambda e, h=h, q0=q0: e.dma_start(out=zat[:, :], in_=ZAT[h * 128:(h + 1) * 128, q0:q0 + 512]), reads=[b_ZAT], writes=[b_zat], dma=True)
                    pO, bO = psum[6], psb[6]; pD, bD = psum[7], psb[7]
                    for kt in range(NKT):
                        i = rot[0] % 6; rot[0] += 1
                        pS, bS = psum[i], psb[i]
                        sc.op("pe", lambda e, kt=kt, pS=pS: e.matmul(pS[:, :], kts[:, kt * 128:(kt + 1) * 128], qts[:, :], start=True, stop=False), reads=[b_kts, b_q], writes=[bS])
                        sc.op("pe", lambda e, kt=kt, pS=pS: e.matmul(pS[:, :], krs[:, kt * 128:(kt + 1) * 128], qrs[:, :], start=False, stop=True), reads=[b_krs, b_q], pwrites=[bS])
                        j = pn[0] % 3; pn[0] += 1
                        sc.op("act", lambda e, pS=pS, j=j: e.activation(out=pt[j][:, :], in_=pS[:, :], func=AF.Exp, scale=SCALE), reads=[bS], writes=[b_pt[j]])
                        sc.op("pe", lambda e, kt=kt, j=j: e.matmul(pO[:, :], vs[:, kt, :], pt[j][:, :], start=(kt == 0), stop=(kt == NKT - 1)), reads=[b_vs, b_pt[j]], **wkw(kt == 0, bO))
                        sc.op("pe", lambda e, kt=kt, j=j: e.matmul(pD[:, :], onesb[:, :], pt[j][:, :], start=(kt == 0), stop=(kt == NKT - 1)), reads=[b_pt[j]], **wkw(kt == 0, bD))
                    sc.op("dve", lambda e: e.reciprocal(rec[:, :], pD[:, :]), reads=[bD], writes=[b_rec])
                    sc.op("dve", lambda e: e.tensor_tensor(of[:, :], pO[:, :], rec[:, :], ALU.mult), reads=[bO, b_rec], writes=[b_of])
                    sc.op("dve", lambda e: e.tensor_tensor(ob[:, :], of[:, :], zat[:, :], ALU.mult), reads=[b_of, b_zat], writes=[b_ob])
                    sc.op("sp", lambda e, h=h, q0=q0: e.dma_start(out=OT[h * 128:(h + 1) * 128, q0:q0 + 512], in_=ob[:, :]), reads=[b_ob], pwrites=[b_OT], dma=True)
            sc.flush()

        ph = contextlib.ExitStack()
        with ph:
            wo = sbt(ph, "wo", [128, 16, D], BF16); b_wo = Buf()
            fnw = sbt(ph, "fnw", [128, D], F32); b_fnw = Buf()
            ynb = sbt(ph, "ynb", [128, 16, 512], BF16); otb = sbt(ph, "otb", [128, 16, 512], BF16); b_in = Buf()
            wpc = [sbt(ph, f"wpc{i}", [128, 2, 16, 128], BF16) for i in range(2)]; b_wpc = [Buf(), Buf()]
            gsb = [sbt(ph, f"gsb{i}", [128, 2, 512], BF16) for i in range(2)]; b_gsb = [Buf(), Buf()]
            t1 = sbt(ph, "t1", [128, 512], F32); t2 = sbt(ph, "t2", [128, 512], F32); b_t = Buf()
            mT = sbt(ph, "mT", [128, 16, 512], BF16); b_mT = Buf()
            xo = sbt(ph, "xo", [128, D], F32); b_xo = Buf()
            rs = sbt(ph, "rs", [128, D], F32); b_rs = Buf()
            sq4 = sbt(ph, "sq4", [128, D], BF16); fs = sbt(ph, "fs", [128, 4], F32); b_fs = Buf()
            for kc in range(16):
                sc.op("pool", lambda e, kc=kc: e.dma_start(out=wo[:, kc, :], in_=w_out[kc * 128:(kc + 1) * 128, :]), pwrites=[b_wo], dma=True)
            sc.op("sp", lambda e: e.dma_start(out=fnw[:], in_=final_norm_w[0:1, :].partition_broadcast(128)), writes=[b_fnw], dma=True)
            for tb in range(SO // 512):
                q0 = tb * 512
                sc.op("sp", lambda e, q0=q0: e.dma_start(out=ynb[:, :, :], in_=YNT[:, q0:q0 + 512].rearrange("(k p) t -> p k t", p=128)), reads=[b_YNT], writes=[b_in], dma=True)
                sc.op("sp", lambda e, q0=q0: e.dma_start(out=otb[:, :, :], in_=OT[:, q0:q0 + 512].rearrange("(k p) t -> p k t", p=128)), reads=[b_OT], pwrites=[b_in], dma=True)
                for c in range(16):
                    i = c % 2
                    sc.op("pool", lambda e, c=c, i=i: e.dma_start(out=wpc[i][:, 0, :, :], in_=w_ps[:, c * 128:(c + 1) * 128].rearrange("(k p) m -> p k m", p=128)), writes=[b_wpc[i]], dma=True)
                    sc.op("pool", lambda e, c=c, i=i: e.dma_start(out=wpc[i][:, 1, :, :], in_=w_pa[:, c * 128:(c + 1) * 128].rearrange("(k p) m -> p k m", p=128)), pwrites=[b_wpc[i]], dma=True)
                    sc.op("sp", lambda e, c=c, i=i, q0=q0: e.dma_start(out=gsb[i][:, 0, :], in_=GST[c * 128:(c + 1) * 128, q0:q0 + 512]), reads=[b_GST], writes=[b_gsb[i]], dma=True)
                    sc.op("sp", lambda e, c=c, i=i, q0=q0: e.dma_start(out=gsb[i][:, 1, :], in_=GAT[c * 128:(c + 1) * 128, q0:q0 + 512]), reads=[b_GAT], pwrites=[b_gsb[i]], dma=True)
                    pA, bA = nps()
                    for kc in range(16):
                        sc.op("pe", lambda e, kc=kc, i=i, pA=pA: e.matmul(pA[:, :], wpc[i][:, 0, kc, :], ynb[:, kc, :], start=(kc == 0), stop=(kc == 15)), reads=[b_wpc[i], b_in], **wkw(kc == 0, bA))
                    pB, bB = nps()
                    for kc in range(16):
                        sc.op("pe", lambda e, kc=kc, i=i, pB=pB: e.matmul(pB[:, :], wpc[i][:, 1, kc, :], otb[:, kc, :], start=(kc == 0), stop=(kc == 15)), reads=[b_wpc[i], b_in], **wkw(kc == 0, bB))
                    sc.op("dve", lambda e, i=i, pA=pA: e.tensor_tensor(t1[:, :], pA[:, :], gsb[i][:, 0, :], ALU.mult), reads=[bA, b_gsb[i]], writes=[b_t])
                    sc.op("dve", lambda e, i=i, pB=pB: e.tensor_tensor(t2[:, :], pB[:, :], gsb[i][:, 1, :], ALU.mult), reads=[bB, b_gsb[i]], pwrites=[b_t])
                    sc.op("dve", lambda e, c=c: e.tensor_tensor(mT[:, c, :], t1[:, :], t2[:, :], ALU.add), reads=[b_t], **wkw(c == 0, b_mT))
                for s_ in range(4):
                    r0 = q0 + s_ * 128
                    sc.op("sp", lambda e, r0=r0: e.dma_start(out=xo[:, :], in_=xown[r0:r0 + 128, :]), writes=[b_xo], dma=True)
                    for oc in range(4):
                        pF, bF = nps()
                        for kc in range(16):
                            sc.op("pe", lambda e, kc=kc, oc=oc, s_=s_, pF=pF: e.matmul(pF[:, :], mT[:, kc, s_ * 128:(s_ + 1) * 128], wo[:, kc, oc * 512:(oc + 1) * 512], start=(kc == 0), stop=(kc == 15)),
                                  reads=[b_mT, b_wo], **wkw(kc == 0, bF))
                        sc.op("dve", lambda e, oc=oc, pF=pF: e.tensor_tensor(rs[:, oc * 512:(oc + 1) * 512], pF[:, :], xo[:, oc * 512:(oc + 1) * 512], ALU.add), reads=[bF, b_xo], **wkw(oc == 0, b_rs))
                    sc.op("act", lambda e: e.activation(out=sq4[:, :], in_=rs[:, :], func=AF.Square, accum_out=fs[:, 0:1]), reads=[b_rs], writes=[b_fs])
                    sc.op("act", lambda e: e.activation(out=fs[:, 1:2], in_=fs[:, 0:1], func=AF.Ln, scale=1.0 / D, bias=1e-6), reads=[b_fs], pwrites=[b_fs])
                    sc.op("act", lambda e: e.activation(out=fs[:, 2:3], in_=fs[:, 1:2], func=AF.Exp, scale=-0.5), reads=[b_fs], pwrites=[b_fs])
                    sc.op("dve", lambda e: e.scalar_tensor_tensor(rs[:, :], rs[:, :], fs[:, 2:3], fnw[:, :], ALU.mult, ALU.mult), reads=[b_fs, b_fnw, b_rs], pwrites=[b_rs])
                    sc.op("sp", lambda e, r0=r0: e.dma_start(out=out[r0:r0 + 128, :], in_=rs[:, :]), reads=[b_rs], pwrites=[b_out], dma=True)
            sc.flush()
        return nc


_CACHE = {}


def kernel(x, positions, norm_w, w_in, conv_w, conv_b, a_log_fwd, a_log_bwd, dt_bias_fwd, dt_bias_bwd,
           d_skip, ssm_norm_w, q_norm_w, w_uq, kv_norm_w, w_ukv, w_proj_ssm, w_proj_attn, w_out, final_norm_w):
    x = np.asarray(x, np.float32); B, S, _ = x.shape
    SO = S // 4; NCH = (4 * SO) // 128
    import os
    dbg = bool(os.environ.get("KDBG"))
    cores = [int(c) for c in os.environ.get("KCORES", "0,1,2,3,4,5,6,7").split(",")]
    if (S, dbg) not in _CACHE:
        _CACHE[(S, dbg)] = build(S, dbg)
    nc = _CACHE[(S, dbg)]
    cst = _consts()
    f = lambda a: np.ascontiguousarray(np.asarray(a, np.float32))
    shared = {
        "norm_w": f(norm_w).reshape(1, D), "w_in": f(w_in).reshape(D, DIN), "conv_w": f(conv_w).reshape(4, 3072), "conv_b": f(conv_b).reshape(1, 3072),
        "a_log": np.concatenate([f(a_log_fwd).reshape(1, 32), f(a_log_bwd).reshape(1, 32)], 1),
        "dt_bias": np.concatenate([f(dt_bias_fwd).reshape(1, 32), f(dt_bias_bwd).reshape(1, 32)], 1),
        "d_skip": f(d_skip).reshape(1, 32), "ssm_norm_w": f(ssm_norm_w).reshape(1, D), "q_norm_w": f(q_norm_w).reshape(1, 768),
        "w_uq": f(w_uq).reshape(768, 3072), "kv_norm_w": f(kv_norm_w).reshape(1, 512), "w_ukv": f(w_ukv).reshape(512, 4096),
        "w_proj_ssm": f(w_proj_ssm).reshape(D, D), "w_proj_attn": f(w_proj_attn).reshape(D, D), "w_out": f(w_out).reshape(D, D),
        "final_norm_w": f(final_norm_w).reshape(1, D), **cst,
    }
    pos = np.asarray(positions).astype(np.int32)
    in_maps = []
    for c in range(8):
        b, j = c // 4, c % 4
        T0 = j * SO
        xkv = np.zeros((S + 4, D), np.float32); xkv[2:S + 2] = x[b]
        xfw = np.zeros((4 * SO + 4, D), np.float32)
        base = T0 + SO - 4 * SO
        lo, hi = max(base - 2, 0), min(base + 4 * SO + 2, S)
        xfw[lo - (base - 2):hi - (base - 2)] = x[b, lo:hi]
        xbw = np.zeros((4 * SO + 4, D), np.float32)
        base2 = T0
        lo, hi = max(base2 - 2, 0), min(base2 + 4 * SO + 2, S)
        xbw[lo - (base2 - 2):hi - (base2 - 2)] = x[b, lo:hi]
        ch = np.arange(NCH)
        vf = ((base + ch * 128) >= 0).astype(np.float32)
        vb = ((base2 + ch * 128) < S).astype(np.float32)
        m = dict(shared)
        m.update({"xkv": xkv, "xfw": xfw, "xbw": xbw, "xown": np.ascontiguousarray(x[b, T0:T0 + SO]),
                  "pos_kv": np.ascontiguousarray(pos[b:b + 1, :]), "pos_own": np.ascontiguousarray(pos[b:b + 1, T0:T0 + SO]),
                  "validf": np.ascontiguousarray(np.broadcast_to(vf[None, :], (128, NCH))),
                  "validb": np.ascontiguousarray(np.broadcast_to(vb[None, :], (128, NCH)))})
        in_maps.append(m)
    if len(cores) < 8:
        res = run_bass_kernel_spmd(nc, [in_maps[c] for c in cores], core_ids=list(range(len(cores))))
        kernel.last = {c: res.results[i] for i, c in enumerate(cores)}
        outp = np.zeros((B, S, D), np.float32)
        for i, c in enumerate(cores):
            b, j = c // 4, c % 4
            outp[b, j * SO:(j + 1) * SO] = res.results[i]["out"]
        return outp
    res = run_bass_kernel_spmd(nc, in_maps, core_ids=list(range(8)))
    outp = np.zeros((B, S, D), np.float32)
    for c in range(8):
        b, j = c // 4, c % 4
        outp[b, j * SO:(j + 1) * SO] = res.results[c]["out"]
    return outp


def _consts():
    k = np.arange(128)
    le = (k[:, None] <= k[None, :]).astype(np.float32)
    ge = (k[:, None] >= k[None, :]).astype(np.float32)
    negf = np.where(k[None, :] < k[:, None], -30000.0, 0.0).astype(np.float32)
    negb = np.where(k[None, :] > k[:, None], -30000.0, 0.0).astype(np.float32)
    inv = (1.0 / (10000.0 ** (np.arange(0, 64, 2, dtype=np.float32) / 64))).astype(np.float32)
    invf = np.concatenate([inv, inv])[:, None].astype(np.float32)
    return {"ident": np.eye(128, dtype=np.float32), "tri": np.stack([le, ge]), "negm": np.stack([negf, negb]), "invf": invf}
```

```python
import contextlib
import types
import numpy as np
import concourse.bass as bass
import concourse.mybir as mybir
from concourse.bass_utils import run_bass_kernel_spmd

F32 = mybir.dt.float32; BF16 = mybir.dt.bfloat16; I32 = mybir.dt.int32
AF = mybir.ActivationFunctionType; ALU = mybir.AluOpType
D = 2048
DIN = 12672
C_ZS, C_XS, C_B, C_C, C_DTF, C_DTB, C_QA, C_KV, C_ZA, C_GS, C_GA = 0, 2048, 4096, 4608, 5120, 5152, 5184, 5952, 6528, 8576, 10624
W = 16384


class Buf:
    __slots__ = ("wc", "wd", "rc", "rd", "const")

    def __init__(s, const=False):
        s.wc = {}; s.wd = {}; s.rc = {}; s.rd = {}; s.const = const


def _snap(fn):
    if fn.__closure__ is None:
        return fn
    cells = []
    for c in fn.__closure__:
        try:
            cells.append(types.CellType(c.cell_contents))
        except ValueError:
            cells.append(c)
    return types.FunctionType(fn.__code__, fn.__globals__, fn.__name__, fn.__defaults__, tuple(cells))


def _addtok(c, d, t):
    if t[0] == 'c':
        if c.get(t[1], -1) < t[2]: c[t[1]] = t[2]
    else:
        if d.get(t[1], 0) < t[2]: d[t[1]] = t[2]


class Sched:
    ENG = ("pe", "act", "dve", "pool", "sp")

    def __init__(s, nc, stack, ndma=24, nwin=14):
        s.nc = nc
        s.q = {e: [] for e in s.ENG}
        s.cnt = {e: 0 for e in s.ENG}
        s.waited_c = {}; s.waited_d = {}
        s.ndma = ndma; s.dma_tot = [0] * ndma; s.dma_next = 0
        s.csem = {e: [stack.enter_context(nc.semaphore(f"c_{e}_{i}")) for i in range(nwin)]
                  for e in ("pe", "act", "dve", "pool")}
        s.dsem = [stack.enter_context(nc.semaphore(f"d_{i}")) for i in range(ndma)]

    def op(s, E, fn, reads=(), writes=(), pwrites=(), dma=False):
        c = {}; d = {}
        for b in reads:
            for k, v in b.wc.items(): _addtok(c, d, ('c', k, v))
            for k, v in b.wd.items(): _addtok(c, d, ('d', k, v))
        for b in writes:
            for k, v in b.wc.items(): _addtok(c, d, ('c', k, v))
            for k, v in b.wd.items(): _addtok(c, d, ('d', k, v))
            for k, v in b.rc.items(): _addtok(c, d, ('c', k, v))
            for k, v in b.rd.items(): _addtok(c, d, ('d', k, v))
        for b in pwrites:
            for k, v in b.rc.items(): _addtok(c, d, ('c', k, v))
            for k, v in b.rd.items(): _addtok(c, d, ('d', k, v))
        if dma:
            si = s.dma_next; s.dma_next = (si + 1) % s.ndma
            if s.dma_tot[si] > 0: _addtok(c, d, ('d', si, s.dma_tot[si]))
            s.dma_tot[si] += 16
            tok = ('d', si, s.dma_tot[si])
        else:
            idx = s.cnt[E]; s.cnt[E] += 1; tok = ('c', E, idx)
        waits = []
        for eng, idx in c.items():
            if eng == E and E == 'pe': continue
            if s.waited_c.get((E, eng), -1) >= idx: continue
            s.waited_c[(E, eng)] = idx
            waits.append((s.csem[eng][idx // W], idx % W + 1))
        for si, val in d.items():
            if s.waited_d.get((E, si), 0) >= val: continue
            s.waited_d[(E, si)] = val
            waits.append((s.dsem[si], val))
        inc = (s.csem[E][tok[2] // W], 1) if tok[0] == 'c' else (s.dsem[tok[1]], 16)
        s.q[E].append((waits, _snap(fn), inc))
        for b in reads:
            if not b.const: _addtok(b.rc, b.rd, tok)
        for b in writes:
            b.wc = {}; b.wd = {}; b.rc = {}; b.rd = {}
            _addtok(b.wc, b.wd, tok)
        for b in pwrites:
            _addtok(b.wc, b.wd, tok)
        return tok

    def flush(s):
        nc = s.nc
        q = s.q; s.q = {e: [] for e in s.ENG}
        ctot = dict(s.cnt); dtot = list(s.dma_tot)

        def replay(E):
            def f(e):
                for waits, fn, inc in q[E]:
                    for sem, val in waits: e.wait_ge(sem, val)
                    fn(e).then_inc(inc[0], inc[1])
                for eng in ("pe", "act", "dve", "pool"):
                    n = ctot[eng]
                    if n > 0 and s.waited_c.get((E, eng), -1) < n - 1:
                        e.wait_ge(s.csem[eng][(n - 1) // W], (n - 1) % W + 1)
                for si in range(s.ndma):
                    if dtot[si] > 0 and s.waited_d.get((E, si), 0) < dtot[si]:
                        e.wait_ge(s.dsem[si], dtot[si])
            return f
        with nc.Block() as block:
            block.sync(replay("sp"))
            block.tensor(replay("pe"))
            block.scalar(replay("act"))
            block.vector(replay("dve"))
            block.gpsimd(replay("pool"))
        for E in s.ENG:
            for eng in ("pe", "act", "dve", "pool"):
                if ctot[eng] > 0: s.waited_c[(E, eng)] = ctot[eng] - 1
            for si in range(s.ndma):
                s.waited_d[(E, si)] = dtot[si]


def bc(ap, shape, axis):
    return ap.unsqueeze(axis).to_broadcast(shape)


def build(S, dbg=False):
    SO = S // 4
    NBK = S // 512
    NBS = (4 * SO) // 512
    NBO = SO // 512
    NCH = NBS * 4
    nc = bass.Bass("TRN2", target_bir_lowering=False)

    def din(name, shape, dt=F32):
        return nc.dram_tensor(name, list(shape), dt, kind="ExternalInput").ap()
    xkv = din("xkv", [S + 4, D]); xfw = din("xfw", [4 * SO + 4, D]); xbw = din("xbw", [4 * SO + 4, D])
    xown = din("xown", [SO, D])
    pos_kv = din("pos_kv", [1, S], I32); pos_own = din("pos_own", [1, SO], I32)
    validf = din("validf", [128, NCH]); validb = din("validb", [128, NCH])
    norm_w = din("norm_w", [1, D]); w_in = din("w_in", [D, DIN]); conv_w = din("conv_w", [4, 3072]); conv_b = din("conv_b", [1, 3072])
    a_log = din("a_log", [1, 64]); dt_bias = din("dt_bias", [1, 64]); d_skip = din("d_skip", [1, 32])
    ssm_norm_w = din("ssm_norm_w", [1, D]); q_norm_w = din("q_norm_w", [1, 768]); w_uq = din("w_uq", [768, 3072])
    kv_norm_w = din("kv_norm_w", [1, 512]); w_ukv = din("w_ukv", [512, 4096])
    w_ps = din("w_proj_ssm", [D, D]); w_pa = din("w_proj_attn", [D, D]); w_out = din("w_out", [D, D])
    final_norm_w = din("final_norm_w", [1, D])
    ident_d = din("ident", [128, 128]); tri_d = din("tri", [2, 128, 128]); neg_d = din("negm", [2, 128, 128])
    invf_d = din("invf", [64, 1])
    out = nc.dram_tensor("out", [SO, D], F32, kind="ExternalOutput").ap()
    KT = nc.dram_tensor("KT", [16, 128, S], BF16, **({"kind": "ExternalOutput"} if dbg else {})).ap()
    KR = nc.dram_tensor("KR", [64, S], BF16, **({"kind": "ExternalOutput"} if dbg else {})).ap()
    VV = nc.dram_tensor("VV", [16, S, 128], BF16, **({"kind": "ExternalOutput"} if dbg else {})).ap()
    QT = nc.dram_tensor("QT", [16, 128, SO], BF16, **({"kind": "ExternalOutput"} if dbg else {})).ap()
    QR = nc.dram_tensor("QR", [16, 64, SO], BF16, **({"kind": "ExternalOutput"} if dbg else {})).ap()
    ZAT = nc.dram_tensor("ZAT", [D, SO], BF16, **({"kind": "ExternalOutput"} if dbg else {})).ap()
    GST = nc.dram_tensor("GST", [D, SO], BF16, **({"kind": "ExternalOutput"} if dbg else {})).ap()
    GAT = nc.dram_tensor("GAT", [D, SO], BF16, **({"kind": "ExternalOutput"} if dbg else {})).ap()
    YF = nc.dram_tensor("YF", [SO, D], F32, **({"kind": "ExternalOutput"} if dbg else {})).ap()
    YNT = nc.dram_tensor("YNT", [D, SO], BF16, **({"kind": "ExternalOutput"} if dbg else {})).ap()
    OT = nc.dram_tensor("OT", [D, SO], BF16, **({"kind": "ExternalOutput"} if dbg else {})).ap()
    b_KT = Buf(); b_KR = Buf(); b_VV = Buf(); b_QT = Buf(); b_QR = Buf(); b_ZAT = Buf(); b_GST = Buf(); b_GAT = Buf()
    b_YF = Buf(); b_YNT = Buf(); b_OT = Buf(); b_out = Buf()

    top = contextlib.ExitStack()
    with top:
        sc = Sched(nc, top)
        psum = [top.enter_context(nc.psum_tensor(f"ps{i}", [128, 512], F32)) for i in range(8)]
        psb = [Buf() for _ in range(8)]
        pstate = [0]

        def nps():
            i = pstate[0]; pstate[0] = (i + 1) % 8
            return psum[i], psb[i]

        def wkw(first, bb):
            return {"writes": [bb]} if first else {"pwrites": [bb]}

        uid = [0]

        def sbt(stack, name, shape, dt):
            uid[0] += 1
            return stack.enter_context(nc.sbuf_tensor(f"s{uid[0]}_{name}", list(shape), dt))
        CB_ = Buf(const=True)
        identf = sbt(top, "identf", [128, 128], F32); ident = sbt(top, "ident", [128, 128], BF16)
        onesf = sbt(top, "onesf", [128, 128], F32); onesb = sbt(top, "onesb", [128, 128], BF16)
        b_init = Buf()
        sc.op("sp", lambda e: e.dma_start(out=identf[:], in_=ident_d[:, :]), writes=[b_init], dma=True)
        sc.op("dve", lambda e: e.tensor_copy(ident[:], identf[:]), reads=[b_init], pwrites=[b_init])
        sc.op("dve", lambda e: e.memset(onesf[:], 1.0), pwrites=[b_init])
        sc.op("dve", lambda e: e.memset(onesb[:], 1.0), pwrites=[b_init])
        sc.flush()

        def make_loader(stack, nw_ap):
            L = {}
            xt0 = sbt(stack, "xt0", [128, D], F32); bx0 = Buf()
            L["xt"] = [xt0, xt0]; L["b_xt"] = [bx0, bx0]
            L["ss"] = sbt(stack, "ss", [128, 8], F32); L["b_ss"] = Buf()
            L["xs"] = [sbt(stack, f"xs{i}", [128, D], BF16) for i in range(2)]; L["b_xs"] = [Buf(), Buf()]
            L["nwbc"] = sbt(stack, "nwbc", [128, D], F32); L["b_nw"] = Buf()
            L["hT"] = sbt(stack, "hT", [128, 16, 515], BF16); L["b_hT"] = Buf()
            sc.op("sp", lambda e: e.dma_start(out=L["nwbc"][:], in_=nw_ap[0:1, :].partition_broadcast(128)), writes=[L["b_nw"]], dma=True)
            L["n"] = 0
            return L

        def load_block(L, xsrc, a, halo=True):
            hT = L["hT"]; b_hT = L["b_hT"]
            subs = [0, 1, 2, 3] + ([4] if halo else [])
            first = True
            for s_ in subs:
                i = L["n"] % 2; L["n"] += 1
                if s_ < 4:
                    P = 128; xt = L["xt"][i]; bxt = L["b_xt"][i]
                    sc.op("sp", lambda e, xt=xt, s_=s_: e.dma_start(out=xt[:], in_=xsrc[a + 2 + s_ * 128:a + 2 + (s_ + 1) * 128, :]), writes=[bxt], dma=True)
                else:
                    P = 3; xt = L["xt"][i]; bxt = L["b_xt"][i]
                    sc.op("sp", lambda e, xt=xt: e.dma_start(out=xt[0:2, :], in_=xsrc[a:a + 2, :]), writes=[bxt], dma=True)
                    sc.op("sp", lambda e, xt=xt: e.dma_start(out=xt[2:3, :], in_=xsrc[a + 514:a + 515, :]), pwrites=[bxt], dma=True)
                ss = L["ss"]; xs = L["xs"][i]; bxs = L["b_xs"][i]
                sc.op("act", lambda e, xt=xt, xs=xs, P=P: e.activation(out=xs[0:P, :], in_=xt[0:P, :], func=AF.Square, accum_out=ss[0:P, 0:1]),
                      reads=[bxt], writes=[bxs, L["b_ss"]])
                sc.op("act", lambda e, P=P: e.activation(out=ss[0:P, 1:2], in_=ss[0:P, 0:1], func=AF.Ln, scale=1.0 / D, bias=1e-6),
                      reads=[L["b_ss"]], pwrites=[L["b_ss"]])
                sc.op("act", lambda e, P=P: e.activation(out=ss[0:P, 2:3], in_=ss[0:P, 1:2], func=AF.Exp, scale=-0.5),
                      reads=[L["b_ss"]], pwrites=[L["b_ss"]])
                sc.op("dve", lambda e, xt=xt, xs=xs, P=P: e.scalar_tensor_tensor(xs[0:P, :], xt[0:P, :], ss[0:P, 2:3], L["nwbc"][0:P, :], ALU.mult, ALU.mult),
                      reads=[bxt, L["b_ss"], L["b_nw"]], writes=[bxs])
                if s_ < 4:
                    for half in range(2):
                        pb, bb = nps(); pv = pb[:].bitcast(BF16)
                        for k8 in range(8):
                            kc = half * 8 + k8
                            sc.op("pe", lambda e, pv=pv, k8=k8, kc=kc, xs=xs: e.transpose(pv[:, k8 * 128:(k8 + 1) * 128], xs[:, kc * 128:(kc + 1) * 128], ident[:]),
                                  reads=[bxs], **wkw(k8 == 0, bb))
                        sc.op("act", lambda e, pv=pv, half=half, s_=s_: e.activation(out=hT[:, half * 8:half * 8 + 8, s_ * 128:(s_ + 1) * 128],
                                                                          in_=pv.rearrange("p (k t) -> p k t", k=8), func=AF.Copy),
                              reads=[bb], **wkw(first, b_hT))
                        first = False
                else:
                    pb, bb = nps(); pv = pb[:].bitcast(BF16)
                    for kc in range(16):
                        sc.op("pe", lambda e, pv=pv, kc=kc, xs=xs: e.transpose(pv[:, kc * 4:kc * 4 + 3], xs[0:3, kc * 128:(kc + 1) * 128], ident[0:3, 0:3]),
                              reads=[bxs], **wkw(kc == 0, bb))
                    sc.op("dve", lambda e, pv=pv: e.tensor_copy(hT[:, :, 512:515], pv[:, 0:64].rearrange("p (k t) -> p k t", t=4)[:, :, 0:3]),
                          reads=[bb], pwrites=[b_hT])

        def rope_alloc(stack, tag):
            R = {"posi": sbt(stack, f"posi{tag}", [64, 512], I32), "ang": sbt(stack, f"ang{tag}", [64, 512], F32),
                 "cs2": sbt(stack, f"cs2{tag}", [64, 512], F32), "sn2": sbt(stack, f"sn2{tag}", [64, 512], F32),
                 "invf": sbt(stack, f"invf{tag}", [64, 1], F32), "b": Buf(), "bi": Buf()}
            sc.op("sp", lambda e: e.dma_start(out=R["invf"][:], in_=invf_d[:, :]), writes=[R["bi"]], dma=True)
            return R

        def rope_block(R, pos_ap, a0):
            posi, ang, b = R["posi"], R["ang"], R["b"]
            sn2, cs2 = R["sn2"], R["cs2"]
            C1 = 6.28125; C2 = 2 * np.pi - 6.28125
            sc.op("sp", lambda e: e.dma_start(out=posi[:], in_=pos_ap[0:1, a0:a0 + 512].partition_broadcast(64)), writes=[b], dma=True)
            sc.op("dve", lambda e: e.tensor_copy(ang[:], posi[:]), reads=[b], pwrites=[b])
            sc.op("dve", lambda e: e.tensor_scalar(ang[:], ang[:], R["invf"][:, 0:1], None, ALU.mult), reads=[b, R["bi"]], pwrites=[b])
            sc.op("dve", lambda e: e.tensor_scalar(cs2[:], ang[:], float(1 / (2 * np.pi)), None, ALU.mult), reads=[b], pwrites=[b])
            sc.op("dve", lambda e: e.tensor_copy(posi[:], cs2[:]), reads=[b], pwrites=[b])
            sc.op("dve", lambda e: e.tensor_copy(cs2[:], posi[:]), reads=[b], pwrites=[b])
            sc.op("dve", lambda e: e.scalar_tensor_tensor(sn2[:], cs2[:], float(-C1), ang[:], ALU.mult, ALU.add), reads=[b], pwrites=[b])
            sc.op("dve", lambda e: e.scalar_tensor_tensor(sn2[:], cs2[:], float(-C2), sn2[:], ALU.mult, ALU.add), reads=[b], pwrites=[b])
            sc.op("dve", lambda e: e.tensor_scalar(cs2[:], sn2[:], float(np.pi), None, ALU.is_gt), reads=[b], pwrites=[b])
            sc.op("dve", lambda e: e.scalar_tensor_tensor(sn2[:], cs2[:], float(-2 * np.pi), sn2[:], ALU.mult, ALU.add), reads=[b], pwrites=[b])
            sc.op("dve", lambda e: e.tensor_scalar(cs2[:], sn2[:], float(-np.pi), None, ALU.is_lt), reads=[b], pwrites=[b])
            sc.op("dve", lambda e: e.scalar_tensor_tensor(sn2[:], cs2[:], float(2 * np.pi), sn2[:], ALU.mult, ALU.add), reads=[b], pwrites=[b])
            sc.op("dve", lambda e: e.tensor_scalar(ang[:], sn2[:], float(np.pi / 2), None, ALU.add), reads=[b], pwrites=[b])
            sc.op("dve", lambda e: e.tensor_scalar(cs2[:], ang[:], float(np.pi), None, ALU.is_gt), reads=[b], pwrites=[b])
            sc.op("dve", lambda e: e.scalar_tensor_tensor(ang[:], cs2[:], float(-2 * np.pi), ang[:], ALU.mult, ALU.add), reads=[b], pwrites=[b])
            sc.op("act", lambda e: e.activation(out=cs2[:], in_=ang[:], func=AF.Sin), reads=[b], pwrites=[b])
            sc.op("act", lambda e: e.activation(out=sn2[:], in_=sn2[:], func=AF.Sin), reads=[b], pwrites=[b])

        def t_rstd(srcs, bsrc, sqt, b_sq, rst, b_rst, n, dim):
            pb, bb = nps()
            for i, sap in enumerate(srcs):
                sc.op("act", lambda e, sap=sap, i=i: e.activation(out=sqt[:, i, 0:n], in_=sap, func=AF.Square), reads=[bsrc], **wkw(i == 0, b_sq))
            for i in range(len(srcs)):
                sc.op("pe", lambda e, i=i, pb=pb: e.matmul(pb[:, 0:n], onesb[:], sqt[:, i, 0:n], start=(i == 0), stop=(i == len(srcs) - 1)),
                      reads=[b_sq], **wkw(i == 0, bb))
            sc.op("act", lambda e, pb=pb: e.activation(out=rst[:, 0:n], in_=pb[:, 0:n], func=AF.Ln, scale=1.0 / dim, bias=1e-6), reads=[bb], writes=[b_rst])
            sc.op("act", lambda e: e.activation(out=rst[:, 0:n], in_=rst[:, 0:n], func=AF.Exp, scale=-0.5), reads=[b_rst], pwrites=[b_rst])

        WCOLS = ([C_XS + 256 * k for k in range(12)] + [C_ZA + 256 * k for k in range(8)] + [C_GS + 256 * k for k in range(8)]
                 + [C_GA + 256 * k for k in range(8)] + [C_QA + 256 * k for k in range(3)] + [C_ZS + 256 * k for k in range(8)])
        WIDX = {c0: i for i, c0 in enumerate(WCOLS)}
        WS = nc.dram_tensor("WS", [len(WCOLS), 128, 16 * 256], BF16).ap()
        b_WS = Buf()
        ph = contextlib.ExitStack()
        with ph:
            stg = [sbt(ph, f"wstg{i}", [128, 16, 256], BF16) for i in range(4)]; b_stg = [Buf() for _ in range(4)]
            for gi, c0 in enumerate(WCOLS):
                i = gi % 4
                sc.op("pool", lambda e, i=i, c0=c0: e.dma_start(out=stg[i][:, :, :], in_=w_in[:, c0:c0 + 256].rearrange("(k p) c -> p k c", p=128)), writes=[b_stg[i]], dma=True)
                sc.op("sp", lambda e, i=i, gi=gi: e.dma_start(out=WS[gi, :, :], in_=stg[i][:, :, :].rearrange("p k c -> p (k c)")), reads=[b_stg[i]], pwrites=[b_WS], dma=True)
            sc.flush()

        ph = contextlib.ExitStack()
        with ph:
            L = make_loader(ph, norm_w)
            wkv = sbt(ph, "wkv", [128, 16, 640], BF16); b_wkv = Buf()
            wuk = sbt(ph, "wuk", [128, 4, 16, 128], BF16); wuv = sbt(ph, "wuv", [128, 4, 16, 128], BF16); b_wu = Buf()
            kvnw = sbt(ph, "kvnw", [128, 4], F32)
            ckv = sbt(ph, "ckv", [128, 4, 512], F32); b_ckv = Buf()
            sqt = sbt(ph, "sqt", [128, 4, 512], BF16); b_sq = Buf()
            rst = sbt(ph, "rst", [128, 512], F32); b_rst = Buf()
            ckn = sbt(ph, "ckn", [128, 4, 512], BF16); b_ckn = Buf()
            krr = sbt(ph, "krr", [64, 2, 512], F32); b_krr = Buf()
            krb = sbt(ph, "krb", [64, 512], BF16); b_krb = Buf()
            kst = [sbt(ph, f"kst{i}", [128, 512], BF16) for i in range(4)]; b_kst = [Buf() for _ in range(4)]
            vst = [sbt(ph, f"vst{i}", [128, 512], BF16) for i in range(4)]; b_vst = [Buf() for _ in range(4)]
            RK = rope_alloc(ph, "k"); cs2 = RK["cs2"]; sn2 = RK["sn2"]; b_rope = RK["b"]
            sc.op("pool", lambda e: e.dma_start(out=wkv[:, :, 0:576], in_=w_in[:, C_KV:C_KV + 576].rearrange("(k p) c -> p k c", p=128)), writes=[b_wkv], dma=True)
            sc.op("dve", lambda e: e.tensor_scalar(wkv[:, :, 576:608], wkv[:, :, 544:576], -1.0, None, ALU.mult), reads=[b_wkv], pwrites=[b_wkv])
            sc.op("dve", lambda e: e.tensor_copy(wkv[:, :, 608:640], wkv[:, :, 512:544]), reads=[b_wkv], pwrites=[b_wkv])
            for kc in range(4):
                sc.op("pool", lambda e, kc=kc: e.dma_start(out=wuk[:, kc, :, :], in_=w_ukv[kc * 128:(kc + 1) * 128, :].rearrange("p (h t d) -> p h t d", t=2, d=128)[:, :, 0, :]), pwrites=[b_wu], dma=True)
                sc.op("pool", lambda e, kc=kc: e.dma_start(out=wuv[:, kc, :, :], in_=w_ukv[kc * 128:(kc + 1) * 128, :].rearrange("p (h t d) -> p h t d", t=2, d=128)[:, :, 1, :]), pwrites=[b_wu], dma=True)
            sc.op("sp", lambda e: e.dma_start(out=kvnw[:], in_=kv_norm_w[0:1, :].rearrange("o (k p) -> p (o k)", p=128), allow_slow_non_contiguous=True), pwrites=[b_wu], dma=True)
            for blk in range(NBK):
                a = blk * 512
                load_block(L, xkv, a, halo=False)
                rope_block(RK, pos_kv, a)
                hT = L["hT"]; b_hT = L["b_hT"]
                for cc in range(5):
                    pb, bb = nps()
                    for kc in range(16):
                        sc.op("pe", lambda e, cc=cc, kc=kc, pb=pb: e.matmul(pb[:, :], wkv[:, kc, cc * 128:(cc + 1) * 128], hT[:, kc, 0:512], start=(kc == 0), stop=(kc == 15)),
                              reads=[b_wkv, b_hT], **wkw(kc == 0, bb))
                    if cc < 4:
                        sc.op("dve", lambda e, cc=cc, pb=pb: e.tensor_copy(ckv[:, cc, :], pb[:, :]), reads=[bb], **wkw(cc == 0, b_ckv))
                    else:
                        sc.op("dve", lambda e, pb=pb: e.tensor_tensor(krr[:, 0, :], pb[0:64, :], cs2[:, 0:512], ALU.mult), reads=[bb, b_rope], writes=[b_krr])
                        sc.op("dve", lambda e, pb=pb: e.tensor_copy(krr[:, 1, :], pb[64:128, :]), reads=[bb], pwrites=[b_krr])
                        sc.op("dve", lambda e: e.tensor_tensor(krr[:, 1, :], krr[:, 1, :], sn2[:, 0:512], ALU.mult), reads=[b_krr, b_rope], pwrites=[b_krr])
                        sc.op("dve", lambda e: e.tensor_tensor(krb[:, :], krr[:, 0, :], krr[:, 1, :], ALU.add), reads=[b_krr], writes=[b_krb])
                        sc.op("sp", lambda e: e.dma_start(out=KR[:, a:a + 512], in_=krb[:, :]), reads=[b_krb], pwrites=[b_KR], dma=True)
                t_rstd([ckv[:, i, :] for i in range(4)], b_ckv, sqt, b_sq, rst, b_rst, 512, 512)
                for cc in range(4):
                    sc.op("dve", lambda e, cc=cc: e.scalar_tensor_tensor(ckn[:, cc, :], ckv[:, cc, :], kvnw[:, cc:cc + 1], rst[:, :], ALU.mult, ALU.mult),
                          reads=[b_ckv, b_rst, b_wu], **wkw(cc == 0, b_ckn))
                for h in range(16):
                    pb, bb = nps()
                    for kc in range(4):
                        sc.op("pe", lambda e, h=h, kc=kc, pb=pb: e.matmul(pb[:, :], wuk[:, kc, h, :], ckn[:, kc, :], start=(kc == 0), stop=(kc == 3)),
                              reads=[b_wu, b_ckn], **wkw(kc == 0, bb))
                    i = h % 4
                    eng = "act" if h % 2 == 0 else "dve"
                    if eng == "act":
                        sc.op("act", lambda e, pb=pb, i=i: e.activation(out=kst[i][:, :], in_=pb[:, :], func=AF.Copy), reads=[bb], writes=[b_kst[i]])
                    else:
                        sc.op("dve", lambda e, pb=pb, i=i: e.tensor_copy(kst[i][:, :], pb[:, :]), reads=[bb], writes=[b_kst[i]])
                    sc.op("sp", lambda e, h=h, i=i: e.dma_start(out=KT[h, :, a:a + 512], in_=kst[i][:, :]), reads=[b_kst[i]], pwrites=[b_KT], dma=True)
                for s_ in range(4):
                    for hg in range(4):
                        pb, bb = nps()
                        for kc in range(4):
                            sc.op("pe", lambda e, s_=s_, hg=hg, kc=kc, pb=pb: e.matmul(pb[:, :], ckn[:, kc, s_ * 128:(s_ + 1) * 128],
                                                                                 wuv[:, kc, hg * 4:hg * 4 + 4, :].rearrange("p h d -> p (h d)"), start=(kc == 0), stop=(kc == 3)),
                                  reads=[b_wu, b_ckn], **wkw(kc == 0, bb))
                        i = hg
                        if (s_ + hg) % 2 == 0:
                            sc.op("act", lambda e, pb=pb, i=i: e.activation(out=vst[i][:, :], in_=pb[:, :], func=AF.Copy), reads=[bb], writes=[b_vst[i]])
                        else:
                            sc.op("dve", lambda e, pb=pb, i=i: e.tensor_copy(vst[i][:, :], pb[:, :]), reads=[bb], writes=[b_vst[i]])
                        t0 = a + s_ * 128
                        sc.op("sp", lambda e, hg=hg, i=i, t0=t0: e.dma_start(out=VV[hg * 4:hg * 4 + 4, t0:t0 + 128, :].rearrange("h t d -> t h d"),
                                                                         in_=vst[i][:, :].rearrange("p (h d) -> p h d", d=128)), reads=[b_vst[i]], pwrites=[b_VV], dma=True)
            sc.flush()

        ph = contextlib.ExitStack()
        with ph:
            L = make_loader(ph, norm_w)
            wst = [sbt(ph, f"wst{i}", [128, 16, 256], BF16) for i in range(2)]; b_wst = [Buf(), Buf()]
            wdt = sbt(ph, "wdt", [128, 16, 64], BF16); cw = sbt(ph, "cw", [128, 24, 4], F32); cb = sbt(ph, "cb", [128, 24], F32)
            abc = sbt(ph, "abc", [128, 64], F32); dtbbc = sbt(ph, "dtbbc", [128, 64], F32); dskbc = sbt(ph, "dskbc", [128, 32], F32)
            snwbc = sbt(ph, "snwbc", [128, D], F32)
            tri = sbt(ph, "tri", [128, 2, 128], F32); negm = sbt(ph, "negm", [128, 2, 128], BF16); negf = sbt(ph, "negf", [128, 2, 128], F32)
            vfl = sbt(ph, "vfl", [128, 2, NCH], F32)
            b_pc = Buf()
            sc.op("pool", lambda e: e.dma_start(out=wdt[:, :, :], in_=w_in[:, C_DTF:C_DTF + 64].rearrange("(k p) c -> p k c", p=128)), pwrites=[b_pc], dma=True)
            for k in range(4):
                sc.op("sp", lambda e, k=k: e.dma_start(out=cw[:, :, k], in_=conv_w[k:k + 1, :].rearrange("o (c p) -> p (o c)", p=128), allow_slow_non_contiguous=True), pwrites=[b_pc], dma=True)
            sc.op("sp", lambda e: e.dma_start(out=cb[:, :], in_=conv_b[0:1, :].rearrange("o (c p) -> p (o c)", p=128), allow_slow_non_contiguous=True), pwrites=[b_pc], dma=True)
            sc.op("sp", lambda e: e.dma_start(out=abc[:], in_=a_log[0:1, :].partition_broadcast(128)), pwrites=[b_pc], dma=True)
            sc.op("sp", lambda e: e.dma_start(out=dtbbc[:], in_=dt_bias[0:1, :].partition_broadcast(128)), pwrites=[b_pc], dma=True)
            sc.op("sp", lambda e: e.dma_start(out=dskbc[:], in_=d_skip[0:1, :].partition_broadcast(128)), pwrites=[b_pc], dma=True)
            sc.op("sp", lambda e: e.dma_start(out=snwbc[:], in_=ssm_norm_w[0:1, :].partition_broadcast(128)), pwrites=[b_pc], dma=True)
            sc.op("sp", lambda e: e.dma_start(out=tri[:, :, :], in_=tri_d[:, :, :].rearrange("t k l -> k t l")), pwrites=[b_pc], dma=True)
            sc.op("sp", lambda e: e.dma_start(out=negf[:, :, :], in_=neg_d[:, :, :].rearrange("t k l -> k t l")), pwrites=[b_pc], dma=True)
            sc.op("sp", lambda e: e.dma_start(out=vfl[:, 0, :], in_=validf[:, :]), pwrites=[b_pc], dma=True)
            sc.op("sp", lambda e: e.dma_start(out=vfl[:, 1, :], in_=validb[:, :]), pwrites=[b_pc], dma=True)
            sc.op("act", lambda e: e.activation(out=abc[:], in_=abc[:], func=AF.Exp), reads=[b_pc], pwrites=[b_pc])
            sc.op("dve", lambda e: e.tensor_scalar(abc[:], abc[:], -1.0, None, ALU.mult), reads=[b_pc], pwrites=[b_pc])
            sc.op("dve", lambda e: e.tensor_copy(negm[:], negf[:]), reads=[b_pc], pwrites=[b_pc])
            XC = sbt(ph, "XC", [128, 24, 512], BF16); b_XC = Buf()
            uext = [sbt(ph, f"uext{i}", [128, 516], F32) for i in range(2)]; b_ue = [Buf(), Buf()]
            acc = [sbt(ph, f"acc{i}", [128, 512], F32) for i in range(2)]; b_acc = [Buf(), Buf()]
            xtok = sbt(ph, "xtok", [128, D], BF16); b_xtok = Buf()
            btok = sbt(ph, "btok", [128, 512], BF16); b_btok = Buf()
            xdw = sbt(ph, "xdw", [128, D], BF16); b_xdw = Buf()
            xd = sbt(ph, "xd", [128, D], BF16); b_xd = Buf()
            sm = sbt(ph, "sm", [128, 8, 32], F32); b_sm = Buf()
            RH = sbt(ph, "RH", [128, 8, 128], F32); b_RH = Buf()
            LH = sbt(ph, "LH", [128, 8, 128], BF16); b_LH = Buf()
            G = sbt(ph, "G", [128, 8, 128], BF16); b_G = Buf()
            CBT = sbt(ph, "CBT", [128, 4, 128], BF16); b_CBT = Buf()
            Sst = sbt(ph, "Sst", [128, 4, 512], F32); b_S = Buf()
            Sbf = sbt(ph, "Sbf", [128, 4, 512], BF16); b_Sbf = Buf()
            ysb = sbt(ph, "ysb", [128, D], F32); b_ysb = Buf()
            yfl = sbt(ph, "yfl", [128, D], F32); b_yfl = Buf()
            tmpg = sbt(ph, "tmpg", [128, 512], F32); b_tmpg = Buf()
            zs = sbt(ph, "zs", [128, D], BF16); b_zs = Buf()
            yn = sbt(ph, "yn", [128, D], BF16); b_yn = Buf()
            gss = sbt(ph, "gss", [128, 12], F32); b_gss = Buf()
            ynT = sbt(ph, "ynT", [128, 16, 128], BF16); b_ynT = Buf()
            est = [sbt(ph, f"est{i}", [128, 512], BF16) for i in range(2)]; b_est = [Buf(), Buf()]
            wuqh = [sbt(ph, f"wuqh{i}", [128, 6, 256], BF16) for i in range(2)]; b_wuqh = [Buf(), Buf()]; b_wuq = Buf()
            qnw = sbt(ph, "qnw", [128, 6], F32)
            qa = sbt(ph, "qa", [128, 6, 512], BF16); b_qa = Buf()
            cq = sbt(ph, "cq", [128, 6, 512], BF16); b_cq = Buf()
            sqt = cq; b_sq = b_cq
            rst = tmpg; b_rst = b_tmpg
            qrr = sbt(ph, "qrr", [64, 2, 512], F32); b_qrr = Buf()
            RO = rope_alloc(ph, "o"); cs2o = RO["cs2"]; sn2o = RO["sn2"]; b_ropeo = RO["b"]
            sc.op("sp", lambda e: e.dma_start(out=qnw[:], in_=q_norm_w[0:1, :].rearrange("o (k p) -> p (o k)", p=128), allow_slow_non_contiguous=True), pwrites=[b_wuq], dma=True)
            wn = [0]

            def wload(col0):
                i = wn[0] % 2; wn[0] += 1
                gi = WIDX[col0]
                sc.op("sp", lambda e, i=i, gi=gi: e.dma_start(out=wst[i][:, :, :].rearrange("p k c -> p (k c)"), in_=WS[gi, :, :]), reads=[b_WS], writes=[b_wst[i]], dma=True)
                return wst[i], b_wst[i]

            def proj_T(col0, ncols, evac, hT, b_hT):
                for c0 in range(0, ncols, 256):
                    wt, bw = wload(col0 + c0)
                    for cc in range(2):
                        pb, bb = nps()
                        for kc in range(16):
                            sc.op("pe", lambda e, wt=wt, cc=cc, kc=kc, pb=pb: e.matmul(pb[:, :], wt[:, kc, cc * 128:(cc + 1) * 128], hT[:, kc, 0:512], start=(kc == 0), stop=(kc == 15)),
                                  reads=[bw, b_hT], **wkw(kc == 0, bb))
                        evac(c0 // 128 + cc, pb, bb, wt, bw, cc)

            for dr in range(2):
                xsrc = xfw if dr == 0 else xbw
                sc.op("dve", lambda e: e.memset(Sst[:], 0.0), writes=[b_S])
                blks = list(range(NBS)) if dr == 0 else list(range(NBS - 1, -1, -1))
                for blk in blks:
                    own = (blk >= NBS - NBO) if dr == 0 else (blk < NBO)
                    ao = (blk - (NBS - NBO)) * 512 if dr == 0 else blk * 512
                    a = blk * 512
                    load_block(L, xsrc, a, halo=True)
                    hT = L["hT"]; b_hT = L["b_hT"]
                    ncc = 24 if own else 20
                    ucnt = [0]

                    def conv_evac(c, pb, bb, wt, bw, cc):
                        i = ucnt[0] % 2; ucnt[0] += 1
                        ph_, bh = nps()
                        for kc in range(16):
                            sc.op("pe", lambda e, kc=kc, ph_=ph_: e.matmul(ph_[:, 0:3], wt[:, kc, cc * 128:(cc + 1) * 128], hT[:, kc, 512:515], start=(kc == 0), stop=(kc == 15)),
                                  reads=[bw, b_hT], **wkw(kc == 0, bh))
                        ue = uext[i]; bu = b_ue[i]; ac = acc[i]; ba = b_acc[i]
                        sc.op("act", lambda e: e.activation(out=ue[:, 2:514], in_=pb[:, :], func=AF.Copy), reads=[bb], writes=[bu])
                        sc.op("dve", lambda e: e.tensor_copy(ue[:, 0:2], ph_[:, 0:2]), reads=[bh], pwrites=[bu])
                        sc.op("dve", lambda e: e.tensor_copy(ue[:, 514:515], ph_[:, 2:3]), reads=[bh], pwrites=[bu])
                        sc.op("dve", lambda e: e.tensor_scalar(ac[:, :], ue[:, 0:512], cw[:, c, 0:1], None, ALU.mult), reads=[bu, b_pc], writes=[ba])
                        for k in range(1, 4):
                            sc.op("dve", lambda e, k=k: e.scalar_tensor_tensor(ac[:, :], ue[:, k:k + 512], cw[:, c, k:k + 1], ac[:, :], ALU.mult, ALU.add), reads=[bu, ba], pwrites=[ba])
                        sc.op("act", lambda e: e.activation(out=XC[:, c, :], in_=ac[:, :], func=AF.Silu, bias=cb[:, c:c + 1]), reads=[ba, b_pc], **wkw(c == 0, b_XC))
                    proj_T(C_XS, ncc * 128, conv_evac, hT, b_hT)

                    if own and dr == 0:
                        ecnt = [0]

                        def mk_evac(dst, bdst, func):
                            def ev(c, pb, bb, wt, bw, cc):
                                i = ecnt[0] % 2; ecnt[0] += 1
                                sc.op("act", lambda e: e.activation(out=est[i][:, :], in_=pb[:, :], func=func), reads=[bb], writes=[b_est[i]])
                                sc.op("sp", lambda e: e.dma_start(out=dst[c * 128:(c + 1) * 128, ao:ao + 512], in_=est[i][:, :]), reads=[b_est[i]], pwrites=[bdst], dma=True)
                            return ev
                        proj_T(C_ZA, 2048, mk_evac(ZAT, b_ZAT, AF.Silu), hT, b_hT)
                        proj_T(C_GS, 2048, mk_evac(GST, b_GST, AF.Sigmoid), hT, b_hT)
                        proj_T(C_GA, 2048, mk_evac(GAT, b_GAT, AF.Sigmoid), hT, b_hT)

                        def qa_evac(c, pb, bb, wt, bw, cc):
                            sc.op("dve", lambda e: e.tensor_copy(qa[:, c, :], pb[:, :]), reads=[bb], **wkw(c == 0, b_qa))
                        proj_T(C_QA, 768, qa_evac, hT, b_hT)
                        rope_block(RO, pos_own, ao)
                        t_rstd([qa[:, i, :] for i in range(6)], b_qa, sqt, b_sq, rst, b_rst, 512, 768)
                        for c in range(6):
                            sc.op("dve", lambda e, c=c: e.scalar_tensor_tensor(cq[:, c, :], qa[:, c, :], qnw[:, c:c + 1], rst[:, :], ALU.mult, ALU.mult),
                                  reads=[b_qa, b_rst, b_wuq], **wkw(c == 0, b_cq))
                        for h in range(16):
                            i = h % 2
                            wq = wuqh[i]; bwq = b_wuqh[i]
                            sc.op("pool", lambda e, h=h, wq=wq: e.dma_start(out=wq[:, :, 0:192], in_=w_uq[:, h * 192:(h + 1) * 192].rearrange("(k p) c -> p k c", p=128)), writes=[bwq], dma=True)
                            sc.op("dve", lambda e, wq=wq: e.tensor_scalar(wq[:, :, 192:224], wq[:, :, 160:192], -1.0, None, ALU.mult), reads=[bwq], pwrites=[bwq])
                            sc.op("dve", lambda e, wq=wq: e.tensor_copy(wq[:, :, 224:256], wq[:, :, 128:160]), reads=[bwq], pwrites=[bwq])
                            pb, bb = nps()
                            for kc in range(6):
                                sc.op("pe", lambda e, wq=wq, kc=kc, pb=pb: e.matmul(pb[:, :], wq[:, kc, 0:128], cq[:, kc, :], start=(kc == 0), stop=(kc == 5)),
                                      reads=[bwq, b_cq], **wkw(kc == 0, bb))
                            sc.op("act", lambda e, pb=pb, i=i: e.activation(out=est[i][:, :], in_=pb[:, :], func=AF.Copy), reads=[bb], writes=[b_est[i]])
                            sc.op("sp", lambda e, h=h, i=i: e.dma_start(out=QT[h, :, ao:ao + 512], in_=est[i][:, :]), reads=[b_est[i]], pwrites=[b_QT], dma=True)
                            pb, bb = nps()
                            for kc in range(6):
                                sc.op("pe", lambda e, wq=wq, kc=kc, pb=pb: e.matmul(pb[:, :], wq[:, kc, 128:256], cq[:, kc, :], start=(kc == 0), stop=(kc == 5)),
                                      reads=[bwq, b_cq], **wkw(kc == 0, bb))
                            sc.op("dve", lambda e, pb=pb: e.tensor_tensor(qrr[:, 0, :], pb[0:64, :], cs2o[:, 0:512], ALU.mult), reads=[bb, b_ropeo], writes=[b_qrr])
                            sc.op("dve", lambda e, pb=pb: e.tensor_copy(qrr[:, 1, :], pb[64:128, :]), reads=[bb], pwrites=[b_qrr])
                            sc.op("dve", lambda e: e.tensor_tensor(qrr[:, 1, :], qrr[:, 1, :], sn2o[:, 0:512], ALU.mult), reads=[b_qrr, b_ropeo], pwrites=[b_qrr])
                            j = (h + 1) % 2
                            sc.op("dve", lambda e, j=j: e.tensor_tensor(est[j][0:64, :], qrr[:, 0, :], qrr[:, 1, :], ALU.add), reads=[b_qrr], writes=[b_est[j]])
                            sc.op("sp", lambda e, h=h, j=j: e.dma_start(out=QR[h, :, ao:ao + 512], in_=est[j][0:64, :]), reads=[b_est[j]], pwrites=[b_QR], dma=True)

                    chs = [0, 1, 2, 3] if dr == 0 else [3, 2, 1, 0]
                    for ch in chs:
                        cg = blk * 4 + ch
                        t0 = ch * 128
                        for half in range(2):
                            pb, bb = nps(); pv = pb[:].bitcast(BF16)
                            for k8 in range(8):
                                c = half * 8 + k8
                                sc.op("pe", lambda e, pv=pv, k8=k8, c=c: e.transpose(pv[:, k8 * 128:(k8 + 1) * 128], XC[:, c, t0:t0 + 128], ident[:]),
                                      reads=[b_XC], **wkw(k8 == 0, bb))
                            sc.op("act", lambda e, pv=pv, half=half: e.activation(out=xtok[:, half * 1024:(half + 1) * 1024], in_=pv[:, :], func=AF.Copy),
                                  reads=[bb], **wkw(half == 0, b_xtok))
                        pb, bb = nps(); pv = pb[:].bitcast(BF16)
                        for k4 in range(4):
                            sc.op("pe", lambda e, pv=pv, k4=k4: e.transpose(pv[:, k4 * 128:(k4 + 1) * 128], XC[:, 16 + k4, t0:t0 + 128], ident[:]),
                                  reads=[b_XC], **wkw(k4 == 0, bb))
                        sc.op("dve", lambda e, pv=pv: e.tensor_copy(btok[:, :], pv[:, 0:512]), reads=[bb], writes=[b_btok])
                        pb, bb = nps()
                        for kc in range(16):
                            sc.op("pe", lambda e, kc=kc, pb=pb: e.matmul(pb[:, 0:32], hT[:, kc, t0:t0 + 128], wdt[:, kc, dr * 32:dr * 32 + 32], start=(kc == 0), stop=(kc == 15)),
                                  reads=[b_hT, b_pc], **wkw(kc == 0, bb))
                        sc.op("dve", lambda e, pb=pb: e.tensor_tensor(sm[:, 0, :], pb[:, 0:32], dtbbc[:, dr * 32:dr * 32 + 32], ALU.add), reads=[bb, b_pc], writes=[b_sm])
                        sc.op("act", lambda e: e.activation(out=sm[:, 0, :], in_=sm[:, 0, :], func=AF.Exp), reads=[b_sm], pwrites=[b_sm])
                        sc.op("act", lambda e: e.activation(out=sm[:, 0, :], in_=sm[:, 0, :], func=AF.Ln, bias=1.0), reads=[b_sm], pwrites=[b_sm])
                        sc.op("dve", lambda e: e.tensor_tensor(sm[:, 1, :], sm[:, 0, :], abc[:, dr * 32:dr * 32 + 32], ALU.mult), reads=[b_sm, b_pc], pwrites=[b_sm])
                        pP, bP = nps()
                        sc.op("pe", lambda e, pP=pP: e.matmul(pP[:, 0:32], tri[:, dr, :], sm[:, 1, :], start=True, stop=True), reads=[b_sm, b_pc], writes=[bP])
                        pT, bT = nps()
                        sc.op("pe", lambda e, pT=pT: e.matmul(pT[:, 0:32], onesf[:, :], sm[:, 1, :], start=True, stop=True), reads=[b_sm], writes=[bT])
                        sc.op("dve", lambda e, pP=pP: e.tensor_copy(sm[:, 2, :], pP[:, 0:32]), reads=[bP], pwrites=[b_sm])
                        sc.op("dve", lambda e, pT=pT: e.tensor_tensor(sm[:, 3, :], pT[:, 0:32], sm[:, 2, :], ALU.subtract), reads=[bT, b_sm], pwrites=[b_sm])
                        sc.op("act", lambda e: e.activation(out=sm[:, 3, :], in_=sm[:, 3, :], func=AF.Exp), reads=[b_sm], pwrites=[b_sm])
                        sc.op("act", lambda e, pT=pT: e.activation(out=sm[:, 4, :], in_=pT[:, 0:32], func=AF.Exp), reads=[bT], pwrites=[b_sm])
                        sc.op("dve", lambda e, cg=cg: e.scalar_tensor_tensor(sm[:, 5, :], sm[:, 0, :], vfl[:, dr, cg:cg + 1], sm[:, 3, :], ALU.mult, ALU.mult), reads=[b_sm, b_pc], pwrites=[b_sm])
                        sc.op("dve", lambda e: e.tensor_tensor(xdw[:, :].rearrange("p (h d) -> p h d", d=64), xtok[:, :].rearrange("p (h d) -> p h d", d=64),
                                                               bc(sm[:, 5, :], [128, 32, 64], 2), ALU.mult), reads=[b_xtok, b_sm], writes=[b_xdw])
                        if own:
                            l0 = ao + t0
                            sc.op("act", lambda e: e.activation(out=sm[:, 6, :], in_=sm[:, 2, :], func=AF.Exp), reads=[b_sm], pwrites=[b_sm])
                            sc.op("dve", lambda e: e.tensor_scalar(sm[:, 7, :], sm[:, 2, :], -1.0, None, ALU.mult), reads=[b_sm], pwrites=[b_sm])
                            sc.op("dve", lambda e: e.tensor_tensor(xd[:, :].rearrange("p (h d) -> p h d", d=64), xtok[:, :].rearrange("p (h d) -> p h d", d=64),
                                                                   bc(sm[:, 0, :], [128, 32, 64], 2), ALU.mult), reads=[b_xtok, b_sm], writes=[b_xd])
                            for g in range(4):
                                sc.op("act", lambda e, g=g: e.activation(out=Sbf[:, g, :], in_=Sst[:, g, :], func=AF.Copy), reads=[b_S], **wkw(g == 0, b_Sbf))
                            pb, bb = nps()
                            for g in range(4):
                                sc.op("pe", lambda e, g=g, pb=pb: e.matmul(pb[:, g * 128:(g + 1) * 128], XC[:, 16 + g, t0:t0 + 128], XC[:, 20 + g, t0:t0 + 128], start=True, stop=True),
                                      reads=[b_XC], **wkw(g == 0, bb))
                            sc.op("act", lambda e, pb=pb: e.activation(out=CBT[:, :, :], in_=pb[:, :].rearrange("p (g l) -> p g l", g=4), func=AF.Copy), reads=[bb], writes=[b_CBT])
                            if dr == 0:
                                sc.op("dve", lambda e: e.tensor_tensor(yn[:, :].rearrange("p (h d) -> p h d", d=64), xtok[:, :].rearrange("p (h d) -> p h d", d=64),
                                                                       bc(dskbc[:, :], [128, 32, 64], 2), ALU.mult), reads=[b_xtok, b_pc], writes=[b_yn])
                            else:
                                sc.op("sp", lambda e, l0=l0: e.dma_start(out=yfl[:, :], in_=YF[l0:l0 + 128, :]), reads=[b_YF], writes=[b_yfl], dma=True)
                            for g in range(4):
                                sc.op("dve", lambda e, g=g: e.tensor_tensor(RH[:, :, :], bc(tri[:, dr, :], [128, 8, 128], 1), bc(sm[:, 1, g * 8:(g + 1) * 8], [128, 8, 128], 2), ALU.mult),
                                      reads=[b_sm, b_pc], writes=[b_RH])
                                for q in range(2):
                                    pb, bb = nps()
                                    sc.op("pe", lambda e, q=q, pb=pb: e.matmul(pb[:, :], onesf[:, :], RH[:, q * 4:(q + 1) * 4, :].rearrange("p h l -> p (h l)"), start=True, stop=False),
                                          reads=[b_RH], writes=[bb])
                                    sc.op("pe", lambda e, pb=pb: e.matmul(pb[:, :].rearrange("p (h l) -> p h l", h=4), ident[:, :], bc(negm[:, dr, :], [128, 4, 128], 1), start=False, stop=True),
                                          reads=[b_pc], pwrites=[bb])
                                    for hh in range(4):
                                        h = g * 8 + q * 4 + hh
                                        sc.op("act", lambda e, pb=pb, hh=hh, h=h, q=q: e.activation(out=LH[:, q * 4 + hh, :], in_=pb[:, hh * 128:(hh + 1) * 128], func=AF.Exp, bias=sm[:, 7, h:h + 1]),
                                              reads=[bb, b_sm], **wkw(q == 0 and hh == 0, b_LH))
                                sc.op("dve", lambda e, g=g: e.tensor_tensor(G[:, :, :], LH[:, :, :], bc(CBT[:, g, :], [128, 8, 128], 1), ALU.mult), reads=[b_LH, b_CBT], writes=[b_G])
                                pY, bY = nps()
                                if dr == 0:
                                    sc.op("pe", lambda e, g=g, pY=pY: e.matmul(pY[:, :], ident[:, :], yn[:, g * 512:(g + 1) * 512], start=True, stop=False), reads=[b_yn], writes=[bY])
                                for hh in range(8):
                                    h = g * 8 + hh
                                    sc.op("pe", lambda e, pY=pY, hh=hh, h=h: e.matmul(pY[:, hh * 64:(hh + 1) * 64], G[:, hh, :], xd[:, h * 64:(h + 1) * 64], start=(dr == 1), stop=True),
                                          reads=[b_G, b_xd], **wkw(dr == 1 and hh == 0, bY))
                                pO, bO = nps()
                                sc.op("pe", lambda e, g=g, pO=pO: e.matmul(pO[:, :], XC[:, 20 + g, t0:t0 + 128], Sbf[:, g, :], start=True, stop=True), reads=[b_XC, b_Sbf], writes=[bO])
                                sc.op("dve", lambda e, g=g, pO=pO: e.tensor_tensor(tmpg[:, :].rearrange("p (h d) -> p h d", d=64), pO[:, :].rearrange("p (h d) -> p h d", d=64),
                                                                            bc(sm[:, 6, g * 8:(g + 1) * 8], [128, 8, 64], 2), ALU.mult), reads=[bO, b_sm], writes=[b_tmpg])
                                sc.op("dve", lambda e, g=g, pY=pY: e.tensor_tensor(ysb[:, g * 512:(g + 1) * 512], pY[:, :], tmpg[:, :], ALU.add), reads=[bY, b_tmpg], **wkw(g == 0, b_ysb))
                            if dr == 0:
                                sc.op("sp", lambda e, l0=l0: e.dma_start(out=YF[l0:l0 + 128, :], in_=ysb[:, :]), reads=[b_ysb], pwrites=[b_YF], dma=True)
                            else:
                                for cg4 in range(4):
                                    for c2 in range(2):
                                        wt, bw = wload(C_ZS + cg4 * 512 + c2 * 256)
                                        pb, bb = nps()
                                        for kc in range(16):
                                            sc.op("pe", lambda e, kc=kc, pb=pb, wt=wt: e.matmul(pb[:, 0:256], hT[:, kc, t0:t0 + 128], wt[:, kc, :], start=(kc == 0), stop=(kc == 15)),
                                                  reads=[b_hT, bw], **wkw(kc == 0, bb))
                                        o0 = cg4 * 512 + c2 * 256
                                        sc.op("act", lambda e, pb=pb, o0=o0: e.activation(out=zs[:, o0:o0 + 256], in_=pb[:, 0:256], func=AF.Silu), reads=[bb], **wkw(o0 == 0, b_zs))
                                sc.op("pool", lambda e: e.tensor_tensor(ysb[:, :], ysb[:, :], yfl[:, :], ALU.add), reads=[b_yfl, b_ysb], pwrites=[b_ysb])
                                sc.op("pool", lambda e: e.tensor_tensor(ysb[:, :], ysb[:, :], zs[:, :], ALU.mult), reads=[b_zs, b_ysb], pwrites=[b_ysb])
                                for g in range(4):
                                    sc.op("act", lambda e, g=g: e.activation(out=tmpg[:, :], in_=ysb[:, g * 512:(g + 1) * 512], func=AF.Square, accum_out=gss[:, g:g + 1]),
                                          reads=[b_ysb], writes=[b_tmpg], pwrites=[b_gss])
                                sc.op("act", lambda e: e.activation(out=gss[:, 4:8], in_=gss[:, 0:4], func=AF.Ln, scale=1.0 / 512, bias=1e-6), reads=[b_gss], pwrites=[b_gss])
                                sc.op("act", lambda e: e.activation(out=gss[:, 8:12], in_=gss[:, 4:8], func=AF.Exp, scale=-0.5), reads=[b_gss], pwrites=[b_gss])
                                for g in range(4):
                                    sc.op("dve", lambda e, g=g: e.scalar_tensor_tensor(yn[:, g * 512:(g + 1) * 512], ysb[:, g * 512:(g + 1) * 512], gss[:, 8 + g:9 + g], snwbc[:, g * 512:(g + 1) * 512], ALU.mult, ALU.mult),
                                          reads=[b_ysb, b_gss, b_pc], **wkw(g == 0, b_yn))
                                for half in range(2):
                                    pb, bb = nps(); pv = pb[:].bitcast(BF16)
                                    for k8 in range(8):
                                        c = half * 8 + k8
                                        sc.op("pe", lambda e, pv=pv, k8=k8, c=c: e.transpose(pv[:, k8 * 128:(k8 + 1) * 128], yn[:, c * 128:(c + 1) * 128], ident[:]),
                                              reads=[b_yn], **wkw(k8 == 0, bb))
                                    sc.op("act", lambda e, pv=pv, half=half: e.activation(out=ynT[:, half * 8:half * 8 + 8, :], in_=pv.rearrange("p (k t) -> p k t", k=8), func=AF.Copy),
                                          reads=[bb], **wkw(half == 0, b_ynT))
                                sc.op("sp", lambda e, l0=l0: e.dma_start(out=YNT[:, l0:l0 + 128].rearrange("(k p) t -> p k t", p=128), in_=ynT[:, :, :]), reads=[b_ynT], pwrites=[b_YNT], dma=True)
                        for g in range(4):
                            pb, bb = nps()
                            sc.op("pe", lambda e, g=g, pb=pb: e.matmul(pb[:, :], btok[:, g * 128:(g + 1) * 128], xdw[:, g * 512:(g + 1) * 512], start=True, stop=True),
                                  reads=[b_btok, b_xdw], writes=[bb])
                            sc.op("dve", lambda e, g=g: e.tensor_tensor(Sst[:, g, :].rearrange("p (h d) -> p h d", d=64), Sst[:, g, :].rearrange("p (h d) -> p h d", d=64),
                                                                      bc(sm[:, 4, g * 8:(g + 1) * 8], [128, 8, 64], 2), ALU.mult), reads=[b_sm, b_Sbf], pwrites=[b_S])
                            sc.op("dve", lambda e, g=g, pb=pb: e.tensor_tensor(Sst[:, g, :], Sst[:, g, :], pb[:, :], ALU.add), reads=[bb], pwrites=[b_S])
            sc.flush()

        SCALE = 1.0 / float(np.sqrt(192.0))
        NKT = S // 128
        ph = contextlib.ExitStack()
        with ph:
            krs = sbt(ph, "krs", [64, S], BF16); b_krs = Buf()
            ktsl = [sbt(ph, f"kts{i}", [128, S], BF16) for i in range(2)]; b_ktsl = [Buf(), Buf()]
            vsl = [sbt(ph, f"vs{i}", [128, NKT, 128], BF16) for i in range(2)]; b_vsl = [Buf(), Buf()]
            qts = sbt(ph, "qts", [128, 512], BF16); qrs = sbt(ph, "qrs", [64, 512], BF16); b_q = Buf()
            pt = [sbt(ph, f"pt{i}", [128, 512], BF16) for i in range(3)]; b_pt = [Buf() for _ in range(3)]
            rec = sbt(ph, "rec", [128, 512], F32); b_rec = Buf()
            of = sbt(ph, "of", [128, 512], F32); b_of = Buf()
            zat = sbt(ph, "zat", [128, 512], BF16); b_zat = Buf()
            ob = sbt(ph, "ob", [128, 512], BF16); b_ob = Buf()
            sc.op("sp", lambda e: e.dma_start(out=krs[:, :], in_=KR[:, :]), reads=[b_KR], writes=[b_krs], dma=True)
            rot = [0]; pn = [0]
            for h in range(16):
                kts = ktsl[h % 2]; b_kts = b_ktsl[h % 2]; vs = vsl[h % 2]; b_vs = b_vsl[h % 2]
                sc.op("sp", lambda e, h=h: e.dma_start(out=kts[:, :], in_=KT[h, :, :]), reads=[b_KT], writes=[b_kts], dma=True)
                sc.op("sp", lambda e, h=h: e.dma_start(out=vs[:, :, :], in_=VV[h, :, :].rearrange("(k p) d -> p k d", p=128)), reads=[b_VV], writes=[b_vs], dma=True)
                for qb in range(SO // 512):
                    q0 = qb * 512
                    sc.op("sp", lambda e, h=h, q0=q0: e.dma_start(out=qts[:, :], in_=QT[h, :, q0:q0 + 512]), reads=[b_QT], writes=[b_q], dma=True)
                    sc.op("sp", lambda e, h=h, q0=q0: e.dma_start(out=qrs[:, :], in_=QR[h, :, q0:q0 + 512]), reads=[b_QR], pwrites=[b_q], dma=True)
                    sc.op("sp", lambda e, h=h, q0=q0: e.dma_start(out=zat[:, :], in_=ZAT[h * 128:(h + 1) * 128, q0:q0 + 512]), reads=[b_ZAT], writes=[b_zat], dma=True)
                    pO, bO = psum[6], psb[6]; pD, bD = psum[7], psb[7]
                    for kt in range(NKT):
                        i = rot[0] % 6; rot[0] += 1
                        pS, bS = psum[i], psb[i]
                        sc.op("pe", lambda e, kt=kt, pS=pS: e.matmul(pS[:, :], kts[:, kt * 128:(kt + 1) * 128], qts[:, :], start=True, stop=False), reads=[b_kts, b_q], writes=[bS])
                        sc.op("pe", lambda e, kt=kt, pS=pS: e.matmul(pS[:, :], krs[:, kt * 128:(kt + 1) * 128], qrs[:, :], start=False, stop=True), reads=[b_krs, b_q], pwrites=[bS])
                        j = pn[0] % 3; pn[0] += 1
                        sc.op("act", lambda e, pS=pS, j=j: e.activation(out=pt[j][:, :], in_=pS[:, :], func=AF.Exp, scale=SCALE), reads=[bS], writes=[b_pt[j]])
                        sc.op("pe", lambda e, kt=kt, j=j: e.matmul(pO[:, :], vs[:, kt, :], pt[j][:, :], start=(kt == 0), stop=(kt == NKT - 1)), reads=[b_vs, b_pt[j]], **wkw(kt == 0, bO))
                        sc.op("pe", lambda e, kt=kt, j=j: e.matmul(pD[:, :], onesb[:, :], pt[j][:, :], start=(kt == 0), stop=(kt == NKT - 1)), reads=[b_pt[j]], **wkw(kt == 0, bD))
                    sc.op("dve", lambda e: e.reciprocal(rec[:, :], pD[:, :]), reads=[bD], writes=[b_rec])
                    sc.op("dve", lambda e: e.tensor_tensor(of[:, :], pO[:, :], rec[:, :], ALU.mult), reads=[bO, b_rec], writes=[b_of])
                    sc.op("dve", lambda e: e.tensor_tensor(ob[:, :], of[:, :], zat[:, :], ALU.mult), reads=[b_of, b_zat], writes=[b_ob])
                    sc.op("sp", lambda e, h=h, q0=q0: e.dma_start(out=OT[h * 128:(h + 1) * 128, q0:q0 + 512], in_=ob[:, :]), reads=[b_ob], pwrites=[b_OT], dma=True)
            sc.flush()

        ph = contextlib.ExitStack()
        with ph:
            wo = sbt(ph, "wo", [128, 16, D], BF16); b_wo = Buf()
            fnw = sbt(ph, "fnw", [128, D], F32); b_fnw = Buf()
            ynb = sbt(ph, "ynb", [128, 16, 512], BF16); otb = sbt(ph, "otb", [128, 16, 512], BF16); b_in = Buf()
            wpc = [sbt(ph, f"wpc{i}", [128, 2, 16, 128], BF16) for i in range(2)]; b_wpc = [Buf(), Buf()]
            gsb = [sbt(ph, f"gsb{i}", [128, 2, 512], BF16) for i in range(2)]; b_gsb = [Buf(), Buf()]
            t1 = sbt(ph, "t1", [128, 512], F32); t2 = sbt(ph, "t2", [128, 512], F32); b_t = Buf()
            mT = sbt(ph, "mT", [128, 16, 512], BF16); b_mT = Buf()
            xo = sbt(ph, "xo", [128, D], F32); b_xo = Buf()
            rs = sbt(ph, "rs", [128, D], F32); b_rs = Buf()
            sq4 = sbt(ph, "sq4", [128, D], BF16); fs = sbt(ph, "fs", [128, 4], F32); b_fs = Buf()
            for kc in range(16):
                sc.op("pool", lambda e, kc=kc: e.dma_start(out=wo[:, kc, :], in_=w_out[kc * 128:(kc + 1) * 128, :]), pwrites=[b_wo], dma=True)
            sc.op("sp", lambda e: e.dma_start(out=fnw[:], in_=final_norm_w[0:1, :].partition_broadcast(128)), writes=[b_fnw], dma=True)
            for tb in range(SO // 512):
                q0 = tb * 512
                sc.op("sp", lambda e, q0=q0: e.dma_start(out=ynb[:, :, :], in_=YNT[:, q0:q0 + 512].rearrange("(k p) t -> p k t", p=128)), reads=[b_YNT], writes=[b_in], dma=True)
                sc.op("sp", lambda e, q0=q0: e.dma_start(out=otb[:, :, :], in_=OT[:, q0:q0 + 512].rearrange("(k p) t -> p k t", p=128)), reads=[b_OT], pwrites=[b_in], dma=True)
                for c in range(16):
                    i = c % 2
                    sc.op("pool", lambda e, c=c, i=i: e.dma_start(out=wpc[i][:, 0, :, :], in_=w_ps[:, c * 128:(c + 1) * 128].rearrange("(k p) m -> p k m", p=128)), writes=[b_wpc[i]], dma=True)
                    sc.op("pool", lambda e, c=c, i=i: e.dma_start(out=wpc[i][:, 1, :, :], in_=w_pa[:, c * 128:(c + 1) * 128].rearrange("(k p) m -> p k m", p=128)), pwrites=[b_wpc[i]], dma=True)
                    sc.op("sp", lambda e, c=c, i=i, q0=q0: e.dma_start(out=gsb[i][:, 0, :], in_=GST[c * 128:(c + 1) * 128, q0:q0 + 512]), reads=[b_GST], writes=[b_gsb[i]], dma=True)
                    sc.op("sp", lambda e, c=c, i=i, q0=q0: e.dma_start(out=gsb[i][:, 1, :], in_=GAT[c * 128:(c + 1) * 128, q0:q0 + 512]), reads=[b_GAT], pwrites=[b_gsb[i]], dma=True)
                    pA, bA = nps()
                    for kc in range(16):
                        sc.op("pe", lambda e, kc=kc, i=i, pA=pA: e.matmul(pA[:, :], wpc[i][:, 0, kc, :], ynb[:, kc, :], start=(kc == 0), stop=(kc == 15)), reads=[b_wpc[i], b_in], **wkw(kc == 0, bA))
                    pB, bB = nps()
                    for kc in range(16):
                        sc.op("pe", lambda e, kc=kc, i=i, pB=pB: e.matmul(pB[:, :], wpc[i][:, 1, kc, :], otb[:, kc, :], start=(kc == 0), stop=(kc == 15)), reads=[b_wpc[i], b_in], **wkw(kc == 0, bB))
                    sc.op("dve", lambda e, i=i, pA=pA: e.tensor_tensor(t1[:, :], pA[:, :], gsb[i][:, 0, :], ALU.mult), reads=[bA, b_gsb[i]], writes=[b_t])
                    sc.op("dve", lambda e, i=i, pB=pB: e.tensor_tensor(t2[:, :], pB[:, :], gsb[i][:, 1, :], ALU.mult), reads=[bB, b_gsb[i]], pwrites=[b_t])
                    sc.op("dve", lambda e, c=c: e.tensor_tensor(mT[:, c, :], t1[:, :], t2[:, :], ALU.add), reads=[b_t], **wkw(c == 0, b_mT))
                for s_ in range(4):
                    r0 = q0 + s_ * 128
                    sc.op("sp", lambda e, r0=r0: e.dma_start(out=xo[:, :], in_=xown[r0:r0 + 128, :]), writes=[b_xo], dma=True)
                    for oc in range(4):
                        pF, bF = nps()
                        for kc in range(16):
                            sc.op("pe", lambda e, kc=kc, oc=oc, s_=s_, pF=pF: e.matmul(pF[:, :], mT[:, kc, s_ * 128:(s_ + 1) * 128], wo[:, kc, oc * 512:(oc + 1) * 512], start=(kc == 0), stop=(kc == 15)),
                                  reads=[b_mT, b_wo], **wkw(kc == 0, bF))
                        sc.op("dve", lambda e, oc=oc, pF=pF: e.tensor_tensor(rs[:, oc * 512:(oc + 1) * 512], pF[:, :], xo[:, oc * 512:(oc + 1) * 512], ALU.add), reads=[bF, b_xo], **wkw(oc == 0, b_rs))
                    sc.op("act", lambda e: e.activation(out=sq4[:, :], in_=rs[:, :], func=AF.Square, accum_out=fs[:, 0:1]), reads=[b_rs], writes=[b_fs])
                    sc.op("act", lambda e: e.activation(out=fs[:, 1:2], in_=fs[:, 0:1], func=AF.Ln, scale=1.0 / D, bias=1e-6), reads=[b_fs], pwrites=[b_fs])
                    sc.op("act", lambda e: e.activation(out=fs[:, 2:3], in_=fs[:, 1:2], func=AF.Exp, scale=-0.5), reads=[b_fs], pwrites=[b_fs])
                    sc.op("dve", lambda e: e.scalar_tensor_tensor(rs[:, :], rs[:, :], fs[:, 2:3], fnw[:, :], ALU.mult, ALU.mult), reads=[b_fs, b_fnw, b_rs], pwrites=[b_rs])
                    sc.op("sp", lambda e, r0=r0: e.dma_start(out=out[r0:r0 + 128, :], in_=rs[:, :]), reads=[b_rs], pwrites=[b_out], dma=True)
            sc.flush()
        return nc


_CACHE = {}


def kernel(x, positions, norm_w, w_in, conv_w, conv_b, a_log_fwd, a_log_bwd, dt_bias_fwd, dt_bias_bwd,
           d_skip, ssm_norm_w, q_norm_w, w_uq, kv_norm_w, w_ukv, w_proj_ssm, w_proj_attn, w_out, final_norm_w):
    x = np.asarray(x, np.float32); B, S, _ = x.shape
    SO = S // 4; NCH = (4 * SO) // 128
    import os
    dbg = bool(os.environ.get("KDBG"))
    cores = [int(c) for c in os.environ.get("KCORES", "0,1,2,3,4,5,6,7").split(",")]
    if (S, dbg) not in _CACHE:
        _CACHE[(S, dbg)] = build(S, dbg)
    nc = _CACHE[(S, dbg)]
    cst = _consts()
    f = lambda a: np.ascontiguousarray(np.asarray(a, np.float32))
    shared = {
        "norm_w": f(norm_w).reshape(1, D), "w_in": f(w_in).reshape(D, DIN), "conv_w": f(conv_w).reshape(4, 3072), "conv_b": f(conv_b).reshape(1, 3072),
        "a_log": np.concatenate([f(a_log_fwd).reshape(1, 32), f(a_log_bwd).reshape(1, 32)], 1),
        "dt_bias": np.concatenate([f(dt_bias_fwd).reshape(1, 32), f(dt_bias_bwd).reshape(1, 32)], 1),
        "d_skip": f(d_skip).reshape(1, 32), "ssm_norm_w": f(ssm_norm_w).reshape(1, D), "q_norm_w": f(q_norm_w).reshape(1, 768),
        "w_uq": f(w_uq).reshape(768, 3072), "kv_norm_w": f(kv_norm_w).reshape(1, 512), "w_ukv": f(w_ukv).reshape(512, 4096),
        "w_proj_ssm": f(w_proj_ssm).reshape(D, D), "w_proj_attn": f(w_proj_attn).reshape(D, D), "w_out": f(w_out).reshape(D, D),
        "final_norm_w": f(final_norm_w).reshape(1, D), **cst,
    }
    pos = np.asarray(positions).astype(np.int32)
    in_maps = []
    for c in range(8):
        b, j = c // 4, c % 4
        T0 = j * SO
        xkv = np.zeros((S + 4, D), np.float32); xkv[2:S + 2] = x[b]
        xfw = np.zeros((4 * SO + 4, D), np.float32)
        base = T0 + SO - 4 * SO
        lo, hi = max(base - 2, 0), min(base + 4 * SO + 2, S)
        xfw[lo - (base - 2):hi - (base - 2)] = x[b, lo:hi]
        xbw = np.zeros((4 * SO + 4, D), np.float32)
        base2 = T0
        lo, hi = max(base2 - 2, 0), min(base2 + 4 * SO + 2, S)
        xbw[lo - (base2 - 2):hi - (base2 - 2)] = x[b, lo:hi]
        ch = np.arange(NCH)
        vf = ((base + ch * 128) >= 0).astype(np.float32)
        vb = ((base2 + ch * 128) < S).astype(np.float32)
        m = dict(shared)
        m.update({"xkv": xkv, "xfw": xfw, "xbw": xbw, "xown": np.ascontiguousarray(x[b, T0:T0 + SO]),
                  "pos_kv": np.ascontiguousarray(pos[b:b + 1, :]), "pos_own": np.ascontiguousarray(pos[b:b + 1, T0:T0 + SO]),
                  "validf": np.ascontiguousarray(np.broadcast_to(vf[None, :], (128, NCH))),
                  "validb": np.ascontiguousarray(np.broadcast_to(vb[None, :], (128, NCH)))})
        in_maps.append(m)
    if len(cores) < 8:
        res = run_bass_kernel_spmd(nc, [in_maps[c] for c in cores], core_ids=list(range(len(cores))))
        kernel.last = {c: res.results[i] for i, c in enumerate(cores)}
        outp = np.zeros((B, S, D), np.float32)
        for i, c in enumerate(cores):
            b, j = c // 4, c % 4
            outp[b, j * SO:(j + 1) * SO] = res.results[i]["out"]
        return outp
    res = run_bass_kernel_spmd(nc, in_maps, core_ids=list(range(8)))
    outp = np.zeros((B, S, D), np.float32)
    for c in range(8):
        b, j = c // 4, c % 4
        outp[b, j * SO:(j + 1) * SO] = res.results[c]["out"]
    return outp


def _consts():
    k = np.arange(128)
    le = (k[:, None] <= k[None, :]).astype(np.float32)
    ge = (k[:, None] >= k[None, :]).astype(np.float32)
    negf = np.where(k[None, :] < k[:, None], -30000.0, 0.0).astype(np.float32)
    negb = np.where(k[None, :] > k[:, None], -30000.0, 0.0).astype(np.float32)
    inv = (1.0 / (10000.0 ** (np.arange(0, 64, 2, dtype=np.float32) / 64))).astype(np.float32)
    invf = np.concatenate([inv, inv])[:, None].astype(np.float32)
    return {"ident": np.eye(128, dtype=np.float32), "tri": np.stack([le, ge]), "negm": np.stack([negf, negb]), "invf": invf}
```

```python
import contextlib
import types
import numpy as np
import concourse.bass as bass
import concourse.mybir as mybir
from concourse.bass_utils import run_bass_kernel_spmd

F32 = mybir.dt.float32; BF16 = mybir.dt.bfloat16; I32 = mybir.dt.int32
AF = mybir.ActivationFunctionType; ALU = mybir.AluOpType
D = 2048
DIN = 12672
C_ZS, C_XS, C_B, C_C, C_DTF, C_DTB, C_QA, C_KV, C_ZA, C_GS, C_GA = 0, 2048, 4096, 4608, 5120, 5152, 5184, 5952, 6528, 8576, 10624
W = 16384


class Buf:
    __slots__ = ("wc", "wd", "rc", "rd", "const")

    def __init__(s, const=False):
        s.wc = {}; s.wd = {}; s.rc = {}; s.rd = {}; s.const = const


def _snap(fn):
    if fn.__closure__ is None:
        return fn
    cells = []
    for c in fn.__closure__:
        try:
            cells.append(types.CellType(c.cell_contents))
        except ValueError:
            cells.append(c)
    return types.FunctionType(fn.__code__, fn.__globals__, fn.__name__, fn.__defaults__, tuple(cells))


def _addtok(c, d, t):
    if t[0] == 'c':
        if c.get(t[1], -1) < t[2]: c[t[1]] = t[2]
    else:
        if d.get(t[1], 0) < t[2]: d[t[1]] = t[2]


class Sched:
    ENG = ("pe", "act", "dve", "pool", "sp")

    def __init__(s, nc, stack, ndma=24, nwin=14):
        s.nc = nc
        s.q = {e: [] for e in s.ENG}
        s.cnt = {e: 0 for e in s.ENG}
        s.waited_c = {}; s.waited_d = {}
        s.ndma = ndma; s.dma_tot = [0] * ndma; s.dma_next = 0
        s.csem = {e: [stack.enter_context(nc.semaphore(f"c_{e}_{i}")) for i in range(nwin)]
                  for e in ("pe", "act", "dve", "pool")}
        s.dsem = [stack.enter_context(nc.semaphore(f"d_{i}")) for i in range(ndma)]

    def op(s, E, fn, reads=(), writes=(), pwrites=(), dma=False):
        c = {}; d = {}
        for b in reads:
            for k, v in b.wc.items(): _addtok(c, d, ('c', k, v))
            for k, v in b.wd.items(): _addtok(c, d, ('d', k, v))
        for b in writes:
            for k, v in b.wc.items(): _addtok(c, d, ('c', k, v))
            for k, v in b.wd.items(): _addtok(c, d, ('d', k, v))
            for k, v in b.rc.items(): _addtok(c, d, ('c', k, v))
            for k, v in b.rd.items(): _addtok(c, d, ('d', k, v))
        for b in pwrites:
            for k, v in b.rc.items(): _addtok(c, d, ('c', k, v))
            for k, v in b.rd.items(): _addtok(c, d, ('d', k, v))
        if dma:
            si = s.dma_next; s.dma_next = (si + 1) % s.ndma
            if s.dma_tot[si] > 0: _addtok(c, d, ('d', si, s.dma_tot[si]))
            s.dma_tot[si] += 16
            tok = ('d', si, s.dma_tot[si])
        else:
            idx = s.cnt[E]; s.cnt[E] += 1; tok = ('c', E, idx)
        waits = []
        for eng, idx in c.items():
            if eng == E and E == 'pe': continue
            if s.waited_c.get((E, eng), -1) >= idx: continue
            s.waited_c[(E, eng)] = idx
            waits.append((s.csem[eng][idx // W], idx % W + 1))
        for si, val in d.items():
            if s.waited_d.get((E, si), 0) >= val: continue
            s.waited_d[(E, si)] = val
            waits.append((s.dsem[si], val))
        inc = (s.csem[E][tok[2] // W], 1) if tok[0] == 'c' else (s.dsem[tok[1]], 16)
        s.q[E].append((waits, _snap(fn), inc))
        for b in reads:
            if not b.const: _addtok(b.rc, b.rd, tok)
        for b in writes:
            b.wc = {}; b.wd = {}; b.rc = {}; b.rd = {}
            _addtok(b.wc, b.wd, tok)
        for b in pwrites:
            _addtok(b.wc, b.wd, tok)
        return tok

    def flush(s):
        nc = s.nc
        q = s.q; s.q = {e: [] for e in s.ENG}
        ctot = dict(s.cnt); dtot = list(s.dma_tot)

        def replay(E):
            def f(e):
                for waits, fn, inc in q[E]:
                    for sem, val in waits: e.wait_ge(sem, val)
                    fn(e).then_inc(inc[0], inc[1])
                for eng in ("pe", "act", "dve", "pool"):
                    n = ctot[eng]
                    if n > 0 and s.waited_c.get((E, eng), -1) < n - 1:
                        e.wait_ge(s.csem[eng][(n - 1) // W], (n - 1) % W + 1)
                for si in range(s.ndma):
                    if dtot[si] > 0 and s.waited_d.get((E, si), 0) < dtot[si]:
                        e.wait_ge(s.dsem[si], dtot[si])
            return f
        with nc.Block() as block:
            block.sync(replay("sp"))
            block.tensor(replay("pe"))
            block.scalar(replay("act"))
            block.vector(replay("dve"))
            block.gpsimd(replay("pool"))
        for E in s.ENG:
            for eng in ("pe", "act", "dve", "pool"):
                if ctot[eng] > 0: s.waited_c[(E, eng)] = ctot[eng] - 1
            for si in range(s.ndma):
                s.waited_d[(E, si)] = dtot[si]


def bc(ap, shape, axis):
    return ap.unsqueeze(axis).to_broadcast(shape)


def build(S, dbg=False):
    SO = S // 4
    NBK = S // 512
    NBS = (4 * SO) // 512
    NBO = SO // 512
    NCH = NBS * 4
    nc = bass.Bass("TRN2", target_bir_lowering=False)

    def din(name, shape, dt=F32):
        return nc.dram_tensor(name, list(shape), dt, kind="ExternalInput").ap()
    xkv = din("xkv", [S + 4, D]); xfw = din("xfw", [4 * SO + 4, D]); xbw = din("xbw", [4 * SO + 4, D])
    xown = din("xown", [SO, D])
    pos_kv = din("pos_kv", [1, S], I32); pos_own = din("pos_own", [1, SO], I32)
    validf = din("validf", [128, NCH]); validb = din("validb", [128, NCH])
    norm_w = din("norm_w", [1, D]); w_in = din("w_in", [D, DIN]); conv_w = din("conv_w", [4, 3072]); conv_b = din("conv_b", [1, 3072])
    a_log = din("a_log", [1, 64]); dt_bias = din("dt_bias", [1, 64]); d_skip = din("d_skip", [1, 32])
    ssm_norm_w = din("ssm_norm_w", [1, D]); q_norm_w = din("q_norm_w", [1, 768]); w_uq = din("w_uq", [768, 3072])
    kv_norm_w = din("kv_norm_w", [1, 512]); w_ukv = din("w_ukv", [512, 4096])
    w_ps = din("w_proj_ssm", [D, D]); w_pa = din("w_proj_attn", [D, D]); w_out = din("w_out", [D, D])
    final_norm_w = din("final_norm_w", [1, D])
    ident_d = din("ident", [128, 128]); tri_d = din("tri", [2, 128, 128]); neg_d = din("negm", [2, 128, 128])
    invf_d = din("invf", [64, 1])
    out = nc.dram_tensor("out", [SO, D], F32, kind="ExternalOutput").ap()
    KT = nc.dram_tensor("KT", [16, 128, S], BF16, **({"kind": "ExternalOutput"} if dbg else {})).ap()
    KR = nc.dram_tensor("KR", [64, S], BF16, **({"kind": "ExternalOutput"} if dbg else {})).ap()
    VV = nc.dram_tensor("VV", [16, S, 128], BF16, **({"kind": "ExternalOutput"} if dbg else {})).ap()
    QT = nc.dram_tensor("QT", [16, 128, SO], BF16, **({"kind": "ExternalOutput"} if dbg else {})).ap()
    QR = nc.dram_tensor("QR", [16, 64, SO], BF16, **({"kind": "ExternalOutput"} if dbg else {})).ap()
    ZAT = nc.dram_tensor("ZAT", [D, SO], BF16, **({"kind": "ExternalOutput"} if dbg else {})).ap()
    GST = nc.dram_tensor("GST", [D, SO], BF16, **({"kind": "ExternalOutput"} if dbg else {})).ap()
    GAT = nc.dram_tensor("GAT", [D, SO], BF16, **({"kind": "ExternalOutput"} if dbg else {})).ap()
    YF = nc.dram_tensor("YF", [SO, D], F32, **({"kind": "ExternalOutput"} if dbg else {})).ap()
    YNT = nc.dram_tensor("YNT", [D, SO], BF16, **({"kind": "ExternalOutput"} if dbg else {})).ap()
    OT = nc.dram_tensor("OT", [D, SO], BF16, **({"kind": "ExternalOutput"} if dbg else {})).ap()
    b_KT = Buf(); b_KR = Buf(); b_VV = Buf(); b_QT = Buf(); b_QR = Buf(); b_ZAT = Buf(); b_GST = Buf(); b_GAT = Buf()
    b_YF = Buf(); b_YNT = Buf(); b_OT = Buf(); b_out = Buf()

    top = contextlib.ExitStack()
    with top:
        sc = Sched(nc, top)
        psum = [top.enter_context(nc.psum_tensor(f"ps{i}", [128, 512], F32)) for i in range(8)]
        psb = [Buf() for _ in range(8)]
        pstate = [0]

        def nps():
            i = pstate[0]; pstate[0] = (i + 1) % 8
            return psum[i], psb[i]

        def wkw(first, bb):
            return {"writes": [bb]} if first else {"pwrites": [bb]}

        uid = [0]

        def sbt(stack, name, shape, dt):
            uid[0] += 1
            return stack.enter_context(nc.sbuf_tensor(f"s{uid[0]}_{name}", list(shape), dt))
        CB_ = Buf(const=True)
        identf = sbt(top, "identf", [128, 128], F32); ident = sbt(top, "ident", [128, 128], BF16)
        onesf = sbt(top, "onesf", [128, 128], F32); onesb = sbt(top, "onesb", [128, 128], BF16)
        b_init = Buf()
        sc.op("sp", lambda e: e.dma_start(out=identf[:], in_=ident_d[:, :]), writes=[b_init], dma=True)
        sc.op("dve", lambda e: e.tensor_copy(ident[:], identf[:]), reads=[b_init], pwrites=[b_init])
        sc.op("dve", lambda e: e.memset(onesf[:], 1.0), pwrites=[b_init])
        sc.op("dve", lambda e: e.memset(onesb[:], 1.0), pwrites=[b_init])
        sc.flush()

        def make_loader(stack, nw_ap):
            L = {}
            xt0 = sbt(stack, "xt0", [128, D], F32); bx0 = Buf()
            L["xt"] = [xt0, xt0]; L["b_xt"] = [bx0, bx0]
            L["ss"] = sbt(stack, "ss", [128, 8], F32); L["b_ss"] = Buf()
            L["xs"] = [sbt(stack, f"xs{i}", [128, D], BF16) for i in range(2)]; L["b_xs"] = [Buf(), Buf()]
            L["nwbc"] = sbt(stack, "nwbc", [128, D], F32); L["b_nw"] = Buf()
            L["hT"] = sbt(stack, "hT", [128, 16, 515], BF16); L["b_hT"] = Buf()
            sc.op("sp", lambda e: e.dma_start(out=L["nwbc"][:], in_=nw_ap[0:1, :].partition_broadcast(128)), writes=[L["b_nw"]], dma=True)
            L["n"] = 0
            return L

        def load_block(L, xsrc, a, halo=True):
            hT = L["hT"]; b_hT = L["b_hT"]
            subs = [0, 1, 2, 3] + ([4] if halo else [])
            first = True
            for s_ in subs:
                i = L["n"] % 2; L["n"] += 1
                if s_ < 4:
                    P = 128; xt = L["xt"][i]; bxt = L["b_xt"][i]
                    sc.op("sp", lambda e, xt=xt, s_=s_: e.dma_start(out=xt[:], in_=xsrc[a + 2 + s_ * 128:a + 2 + (s_ + 1) * 128, :]), writes=[bxt], dma=True)
                else:
                    P = 3; xt = L["xt"][i]; bxt = L["b_xt"][i]
                    sc.op("sp", lambda e, xt=xt: e.dma_start(out=xt[0:2, :], in_=xsrc[a:a + 2, :]), writes=[bxt], dma=True)
                    sc.op("sp", lambda e, xt=xt: e.dma_start(out=xt[2:3, :], in_=xsrc[a + 514:a + 515, :]), pwrites=[bxt], dma=True)
                ss = L["ss"]; xs = L["xs"][i]; bxs = L["b_xs"][i]
                sc.op("act", lambda e, xt=xt, xs=xs, P=P: e.activation(out=xs[0:P, :], in_=xt[0:P, :], func=AF.Square, accum_out=ss[0:P, 0:1]),
                      reads=[bxt], writes=[bxs, L["b_ss"]])
                sc.op("act", lambda e, P=P: e.activation(out=ss[0:P, 1:2], in_=ss[0:P, 0:1], func=AF.Ln, scale=1.0 / D, bias=1e-6),
                      reads=[L["b_ss"]], pwrites=[L["b_ss"]])
                sc.op("act", lambda e, P=P: e.activation(out=ss[0:P, 2:3], in_=ss[0:P, 1:2], func=AF.Exp, scale=-0.5),
                      reads=[L["b_ss"]], pwrites=[L["b_ss"]])
                sc.op("dve", lambda e, xt=xt, xs=xs, P=P: e.scalar_tensor_tensor(xs[0:P, :], xt[0:P, :], ss[0:P, 2:3], L["nwbc"][0:P, :], ALU.mult, ALU.mult),
                      reads=[bxt, L["b_ss"], L["b_nw"]], writes=[bxs])
                if s_ < 4:
                    for half in range(2):
                        pb, bb = nps(); pv = pb[:].bitcast(BF16)
                        for k8 in range(8):
                            kc = half * 8 + k8
                            sc.op("pe", lambda e, pv=pv, k8=k8, kc=kc, xs=xs: e.transpose(pv[:, k8 * 128:(k8 + 1) * 128], xs[:, kc * 128:(kc + 1) * 128], ident[:]),
                                  reads=[bxs], **wkw(k8 == 0, bb))
                        sc.op("act", lambda e, pv=pv, half=half, s_=s_: e.activation(out=hT[:, half * 8:half * 8 + 8, s_ * 128:(s_ + 1) * 128],
                                                                          in_=pv.rearrange("p (k t) -> p k t", k=8), func=AF.Copy),
                              reads=[bb], **wkw(first, b_hT))
                        first = False
                else:
                    pb, bb = nps(); pv = pb[:].bitcast(BF16)
                    for kc in range(16):
                        sc.op("pe", lambda e, pv=pv, kc=kc, xs=xs: e.transpose(pv[:, kc * 4:kc * 4 + 3], xs[0:3, kc * 128:(kc + 1) * 128], ident[0:3, 0:3]),
                              reads=[bxs], **wkw(kc == 0, bb))
                    sc.op("dve", lambda e, pv=pv: e.tensor_copy(hT[:, :, 512:515], pv[:, 0:64].rearrange("p (k t) -> p k t", t=4)[:, :, 0:3]),
                          reads=[bb], pwrites=[b_hT])

        def rope_alloc(stack, tag):
            R = {"posi": sbt(stack, f"posi{tag}", [64, 512], I32), "ang": sbt(stack, f"ang{tag}", [64, 512], F32),
                 "cs2": sbt(stack, f"cs2{tag}", [64, 512], F32), "sn2": sbt(stack, f"sn2{tag}", [64, 512], F32),
                 "invf": sbt(stack, f"invf{tag}", [64, 1], F32), "b": Buf(), "bi": Buf()}
            sc.op("sp", lambda e: e.dma_start(out=R["invf"][:], in_=invf_d[:, :]), writes=[R["bi"]], dma=True)
            return R

        def rope_block(R, pos_ap, a0):
            posi, ang, b = R["posi"], R["ang"], R["b"]
            sn2, cs2 = R["sn2"], R["cs2"]
            C1 = 6.28125; C2 = 2 * np.pi - 6.28125
            sc.op("sp", lambda e: e.dma_start(out=posi[:], in_=pos_ap[0:1, a0:a0 + 512].partition_broadcast(64)), writes=[b], dma=True)
            sc.op("dve", lambda e: e.tensor_copy(ang[:], posi[:]), reads=[b], pwrites=[b])
            sc.op("dve", lambda e: e.tensor_scalar(ang[:], ang[:], R["invf"][:, 0:1], None, ALU.mult), reads=[b, R["bi"]], pwrites=[b])
            sc.op("dve", lambda e: e.tensor_scalar(cs2[:], ang[:], float(1 / (2 * np.pi)), None, ALU.mult), reads=[b], pwrites=[b])
            sc.op("dve", lambda e: e.tensor_copy(posi[:], cs2[:]), reads=[b], pwrites=[b])
            sc.op("dve", lambda e: e.tensor_copy(cs2[:], posi[:]), reads=[b], pwrites=[b])
            sc.op("dve", lambda e: e.scalar_tensor_tensor(sn2[:], cs2[:], float(-C1), ang[:], ALU.mult, ALU.add), reads=[b], pwrites=[b])
            sc.op("dve", lambda e: e.scalar_tensor_tensor(sn2[:], cs2[:], float(-C2), sn2[:], ALU.mult, ALU.add), reads=[b], pwrites=[b])
            sc.op("dve", lambda e: e.tensor_scalar(cs2[:], sn2[:], float(np.pi), None, ALU.is_gt), reads=[b], pwrites=[b])
            sc.op("dve", lambda e: e.scalar_tensor_tensor(sn2[:], cs2[:], float(-2 * np.pi), sn2[:], ALU.mult, ALU.add), reads=[b], pwrites=[b])
            sc.op("dve", lambda e: e.tensor_scalar(cs2[:], sn2[:], float(-np.pi), None, ALU.is_lt), reads=[b], pwrites=[b])
            sc.op("dve", lambda e: e.scalar_tensor_tensor(sn2[:], cs2[:], float(2 * np.pi), sn2[:], ALU.mult, ALU.add), reads=[b], pwrites=[b])
            sc.op("dve", lambda e: e.tensor_scalar(ang[:], sn2[:], float(np.pi / 2), None, ALU.add), reads=[b], pwrites=[b])
            sc.op("dve", lambda e: e.tensor_scalar(cs2[:], ang[:], float(np.pi), None, ALU.is_gt), reads=[b], pwrites=[b])
            sc.op("dve", lambda e: e.scalar_tensor_tensor(ang[:], cs2[:], float(-2 * np.pi), ang[:], ALU.mult, ALU.add), reads=[b], pwrites=[b])
            sc.op("act", lambda e: e.activation(out=cs2[:], in_=ang[:], func=AF.Sin), reads=[b], pwrites=[b])
            sc.op("act", lambda e: e.activation(out=sn2[:], in_=sn2[:], func=AF.Sin), reads=[b], pwrites=[b])

        def t_rstd(srcs, bsrc, sqt, b_sq, rst, b_rst, n, dim):
            pb, bb = nps()
            for i, sap in enumerate(srcs):
                sc.op("act", lambda e, sap=sap, i=i: e.activation(out=sqt[:, i, 0:n], in_=sap, func=AF.Square), reads=[bsrc], **wkw(i == 0, b_sq))
            for i in range(len(srcs)):
                sc.op("pe", lambda e, i=i, pb=pb: e.matmul(pb[:, 0:n], onesb[:], sqt[:, i, 0:n], start=(i == 0), stop=(i == len(srcs) - 1)),
                      reads=[b_sq], **wkw(i == 0, bb))
            sc.op("act", lambda e, pb=pb: e.activation(out=rst[:, 0:n], in_=pb[:, 0:n], func=AF.Ln, scale=1.0 / dim, bias=1e-6), reads=[bb], writes=[b_rst])
            sc.op("act", lambda e: e.activation(out=rst[:, 0:n], in_=rst[:, 0:n], func=AF.Exp, scale=-0.5), reads=[b_rst], pwrites=[b_rst])

        WCOLS = ([C_XS + 256 * k for k in range(12)] + [C_ZA + 256 * k for k in range(8)] + [C_GS + 256 * k for k in range(8)]
                 + [C_GA + 256 * k for k in range(8)] + [C_QA + 256 * k for k in range(3)] + [C_ZS + 256 * k for k in range(8)])
        WIDX = {c0: i for i, c0 in enumerate(WCOLS)}
        WS = nc.dram_tensor("WS", [len(WCOLS), 128, 16 * 256], BF16).ap()
        b_WS = Buf()
        ph = contextlib.ExitStack()
        with ph:
            stg = [sbt(ph, f"wstg{i}", [128, 16, 256], BF16) for i in range(4)]; b_stg = [Buf() for _ in range(4)]
            for gi, c0 in enumerate(WCOLS):
                i = gi % 4
                sc.op("pool", lambda e, i=i, c0=c0: e.dma_start(out=stg[i][:, :, :], in_=w_in[:, c0:c0 + 256].rearrange("(k p) c -> p k c", p=128)), writes=[b_stg[i]], dma=True)
                sc.op("sp", lambda e, i=i, gi=gi: e.dma_start(out=WS[gi, :, :], in_=stg[i][:, :, :].rearrange("p k c -> p (k c)")), reads=[b_stg[i]], pwrites=[b_WS], dma=True)
            sc.flush()

        ph = contextlib.ExitStack()
        with ph:
            L = make_loader(ph, norm_w)
            wkv = sbt(ph, "wkv", [128, 16, 640], BF16); b_wkv = Buf()
            wuk = sbt(ph, "wuk", [128, 4, 16, 128], BF16); wuv = sbt(ph, "wuv", [128, 4, 16, 128], BF16); b_wu = Buf()
            kvnw = sbt(ph, "kvnw", [128, 4], F32)
            ckv = sbt(ph, "ckv", [128, 4, 512], F32); b_ckv = Buf()
            sqt = sbt(ph, "sqt", [128, 4, 512], BF16); b_sq = Buf()
            rst = sbt(ph, "rst", [128, 512], F32); b_rst = Buf()
            ckn = sbt(ph, "ckn", [128, 4, 512], BF16); b_ckn = Buf()
            krr = sbt(ph, "krr", [64, 2, 512], F32); b_krr = Buf()
            krb = sbt(ph, "krb", [64, 512], BF16); b_krb = Buf()
            kst = [sbt(ph, f"kst{i}", [128, 512], BF16) for i in range(4)]; b_kst = [Buf() for _ in range(4)]
            vst = [sbt(ph, f"vst{i}", [128, 512], BF16) for i in range(4)]; b_vst = [Buf() for _ in range(4)]
            RK = rope_alloc(ph, "k"); cs2 = RK["cs2"]; sn2 = RK["sn2"]; b_rope = RK["b"]
            sc.op("pool", lambda e: e.dma_start(out=wkv[:, :, 0:576], in_=w_in[:, C_KV:C_KV + 576].rearrange("(k p) c -> p k c", p=128)), writes=[b_wkv], dma=True)
            sc.op("dve", lambda e: e.tensor_scalar(wkv[:, :, 576:608], wkv[:, :, 544:576], -1.0, None, ALU.mult), reads=[b_wkv], pwrites=[b_wkv])
            sc.op("dve", lambda e: e.tensor_copy(wkv[:, :, 608:640], wkv[:, :, 512:544]), reads=[b_wkv], pwrites=[b_wkv])
            for kc in range(4):
                sc.op("pool", lambda e, kc=kc: e.dma_start(out=wuk[:, kc, :, :], in_=w_ukv[kc * 128:(kc + 1) * 128, :].rearrange("p (h t d) -> p h t d", t=2, d=128)[:, :, 0, :]), pwrites=[b_wu], dma=True)
                sc.op("pool", lambda e, kc=kc: e.dma_start(out=wuv[:, kc, :, :], in_=w_ukv[kc * 128:(kc + 1) * 128, :].rearrange("p (h t d) -> p h t d", t=2, d=128)[:, :, 1, :]), pwrites=[b_wu], dma=True)
            sc.op("sp", lambda e: e.dma_start(out=kvnw[:], in_=kv_norm_w[0:1, :].rearrange("o (k p) -> p (o k)", p=128), allow_slow_non_contiguous=True), pwrites=[b_wu], dma=True)
            for blk in range(NBK):
                a = blk * 512
                load_block(L, xkv, a, halo=False)
                rope_block(RK, pos_kv, a)
                hT = L["hT"]; b_hT = L["b_hT"]
                for cc in range(5):
                    pb, bb = nps()
                    for kc in range(16):
                        sc.op("pe", lambda e, cc=cc, kc=kc, pb=pb: e.matmul(pb[:, :], wkv[:, kc, cc * 128:(cc + 1) * 128], hT[:, kc, 0:512], start=(kc == 0), stop=(kc == 15)),
                              reads=[b_wkv, b_hT], **wkw(kc == 0, bb))
                    if cc < 4:
                        sc.op("dve", lambda e, cc=cc, pb=pb: e.tensor_copy(ckv[:, cc, :], pb[:, :]), reads=[bb], **wkw(cc == 0, b_ckv))
                    else:
                        sc.op("dve", lambda e, pb=pb: e.tensor_tensor(krr[:, 0, :], pb[0:64, :], cs2[:, 0:512], ALU.mult), reads=[bb, b_rope], writes=[b_krr])
                        sc.op("dve", lambda e, pb=pb: e.tensor_copy(krr[:, 1, :], pb[64:128, :]), reads=[bb], pwrites=[b_krr])
                        sc.op("dve", lambda e: e.tensor_tensor(krr[:, 1, :], krr[:, 1, :], sn2[:, 0:512], ALU.mult), reads=[b_krr, b_rope], pwrites=[b_krr])
                        sc.op("dve", lambda e: e.tensor_tensor(krb[:, :], krr[:, 0, :], krr[:, 1, :], ALU.add), reads=[b_krr], writes=[b_krb])
                        sc.op("sp", lambda e: e.dma_start(out=KR[:, a:a + 512], in_=krb[:, :]), reads=[b_krb], pwrites=[b_KR], dma=True)
                t_rstd([ckv[:, i, :] for i in range(4)], b_ckv, sqt, b_sq, rst, b_rst, 512, 512)
                for cc in range(4):
                    sc.op("dve", lambda e, cc=cc: e.scalar_tensor_tensor(ckn[:, cc, :], ckv[:, cc, :], kvnw[:, cc:cc + 1], rst[:, :], ALU.mult, ALU.mult),
                          reads=[b_ckv, b_rst, b_wu], **wkw(cc == 0, b_ckn))
                for h in range(16):
                    pb, bb = nps()
                    for kc in range(4):
                        sc.op("pe", lambda e, h=h, kc=kc, pb=pb: e.matmul(pb[:, :], wuk[:, kc, h, :], ckn[:, kc, :], start=(kc == 0), stop=(kc == 3)),
                              reads=[b_wu, b_ckn], **wkw(kc == 0, bb))
                    i = h % 4
                    eng = "act" if h % 2 == 0 else "dve"
                    if eng == "act":
                        sc.op("act", lambda e, pb=pb, i=i: e.activation(out=kst[i][:, :], in_=pb[:, :], func=AF.Copy), reads=[bb], writes=[b_kst[i]])
                    else:
                        sc.op("dve", lambda e, pb=pb, i=i: e.tensor_copy(kst[i][:, :], pb[:, :]), reads=[bb], writes=[b_kst[i]])
                    sc.op("sp", lambda e, h=h, i=i: e.dma_start(out=KT[h, :, a:a + 512], in_=kst[i][:, :]), reads=[b_kst[i]], pwrites=[b_KT], dma=True)
                for s_ in range(4):
                    for hg in range(4):
                        pb, bb = nps()
                        for kc in range(4):
                            sc.op("pe", lambda e, s_=s_, hg=hg, kc=kc, pb=pb: e.matmul(pb[:, :], ckn[:, kc, s_ * 128:(s_ + 1) * 128],
                                                                                 wuv[:, kc, hg * 4:hg * 4 + 4, :].rearrange("p h d -> p (h d)"), start=(kc == 0), stop=(kc == 3)),
                                  reads=[b_wu, b_ckn], **wkw(kc == 0, bb))
                        i = hg
                        if (s_ + hg) % 2 == 0:
                            sc.op("act", lambda e, pb=pb, i=i: e.activation(out=vst[i][:, :], in_=pb[:, :], func=AF.Copy), reads=[bb], writes=[b_vst[i]])
                        else:
                            sc.op("dve", lambda e, pb=pb, i=i: e.tensor_copy(vst[i][:, :], pb[:, :]), reads=[bb], writes=[b_vst[i]])
                        t0 = a + s_ * 128
                        sc.op("sp", lambda e, hg=hg, i=i, t0=t0: e.dma_start(out=VV[hg * 4:hg * 4 + 4, t0:t0 + 128, :].rearrange("h t d -> t h d"),
                                                                         in_=vst[i][:, :].rearrange("p (h d) -> p h d", d=128)), reads=[b_vst[i]], pwrites=[b_VV], dma=True)
            sc.flush()

        ph = contextlib.ExitStack()
        with ph:
            L = make_loader(ph, norm_w)
            wst = [sbt(ph, f"wst{i}", [128, 16, 256], BF16) for i in range(2)]; b_wst = [Buf(), Buf()]
            wdt = sbt(ph, "wdt", [128, 16, 64], BF16); cw = sbt(ph, "cw", [128, 24, 4], F32); cb = sbt(ph, "cb", [128, 24], F32)
            abc = sbt(ph, "abc", [128, 64], F32); dtbbc = sbt(ph, "dtbbc", [128, 64], F32); dskbc = sbt(ph, "dskbc", [128, 32], F32)
            snwbc = sbt(ph, "snwbc", [128, D], F32)
            tri = sbt(ph, "tri", [128, 2, 128], F32); negm = sbt(ph, "negm", [128, 2, 128], BF16); negf = sbt(ph, "negf", [128, 2, 128], F32)
            vfl = sbt(ph, "vfl", [128, 2, NCH], F32)
            b_pc = Buf()
            sc.op("pool", lambda e: e.dma_start(out=wdt[:, :, :], in_=w_in[:, C_DTF:C_DTF + 64].rearrange("(k p) c -> p k c", p=128)), pwrites=[b_pc], dma=True)
            for k in range(4):
                sc.op("sp", lambda e, k=k: e.dma_start(out=cw[:, :, k], in_=conv_w[k:k + 1, :].rearrange("o (c p) -> p (o c)", p=128), allow_slow_non_contiguous=True), pwrites=[b_pc], dma=True)
            sc.op("sp", lambda e: e.dma_start(out=cb[:, :], in_=conv_b[0:1, :].rearrange("o (c p) -> p (o c)", p=128), allow_slow_non_contiguous=True), pwrites=[b_pc], dma=True)
            sc.op("sp", lambda e: e.dma_start(out=abc[:], in_=a_log[0:1, :].partition_broadcast(128)), pwrites=[b_pc], dma=True)
            sc.op("sp", lambda e: e.dma_start(out=dtbbc[:], in_=dt_bias[0:1, :].partition_broadcast(128)), pwrites=[b_pc], dma=True)
            sc.op("sp", lambda e: e.dma_start(out=dskbc[:], in_=d_skip[0:1, :].partition_broadcast(128)), pwrites=[b_pc], dma=True)
            sc.op("sp", lambda e: e.dma_start(out=snwbc[:], in_=ssm_norm_w[0:1, :].partition_broadcast(128)), pwrites=[b_pc], dma=True)
            sc.op("sp", lambda e: e.dma_start(out=tri[:, :, :], in_=tri_d[:, :, :].rearrange("t k l -> k t l")), pwrites=[b_pc], dma=True)
            sc.op("sp", lambda e: e.dma_start(out=negf[:, :, :], in_=neg_d[:, :, :].rearrange("t k l -> k t l")), pwrites=[b_pc], dma=True)
            sc.op("sp", lambda e: e.dma_start(out=vfl[:, 0, :], in_=validf[:, :]), pwrites=[b_pc], dma=True)
            sc.op("sp", lambda e: e.dma_start(out=vfl[:, 1, :], in_=validb[:, :]), pwrites=[b_pc], dma=True)
            sc.op("act", lambda e: e.activation(out=abc[:], in_=abc[:], func=AF.Exp), reads=[b_pc], pwrites=[b_pc])
            sc.op("dve", lambda e: e.tensor_scalar(abc[:], abc[:], -1.0, None, ALU.mult), reads=[b_pc], pwrites=[b_pc])
            sc.op("dve", lambda e: e.tensor_copy(negm[:], negf[:]), reads=[b_pc], pwrites=[b_pc])
            XC = sbt(ph, "XC", [128, 24, 512], BF16); b_XC = Buf()
            uext = [sbt(ph, f"uext{i}", [128, 516], F32) for i in range(2)]; b_ue = [Buf(), Buf()]
            acc = [sbt(ph, f"acc{i}", [128, 512], F32) for i in range(2)]; b_acc = [Buf(), Buf()]
            xtok = sbt(ph, "xtok", [128, D], BF16); b_xtok = Buf()
            btok = sbt(ph, "btok", [128, 512], BF16); b_btok = Buf()
            xdw = sbt(ph, "xdw", [128, D], BF16); b_xdw = Buf()
            xd = sbt(ph, "xd", [128, D], BF16); b_xd = Buf()
            sm = sbt(ph, "sm", [128, 8, 32], F32); b_sm = Buf()
            RH = sbt(ph, "RH", [128, 8, 128], F32); b_RH = Buf()
            LH = sbt(ph, "LH", [128, 8, 128], BF16); b_LH = Buf()
            G = sbt(ph, "G", [128, 8, 128], BF16); b_G = Buf()
            CBT = sbt(ph, "CBT", [128, 4, 128], BF16); b_CBT = Buf()
            Sst = sbt(ph, "Sst", [128, 4, 512], F32); b_S = Buf()
            Sbf = sbt(ph, "Sbf", [128, 4, 512], BF16); b_Sbf = Buf()
            ysb = sbt(ph, "ysb", [128, D], F32); b_ysb = Buf()
            yfl = sbt(ph, "yfl", [128, D], F32); b_yfl = Buf()
            tmpg = sbt(ph, "tmpg", [128, 512], F32); b_tmpg = Buf()
            zs = sbt(ph, "zs", [128, D], BF16); b_zs = Buf()
            yn = sbt(ph, "yn", [128, D], BF16); b_yn = Buf()
            gss = sbt(ph, "gss", [128, 12], F32); b_gss = Buf()
            ynT = sbt(ph, "ynT", [128, 16, 128], BF16); b_ynT = Buf()
            est = [sbt(ph, f"est{i}", [128, 512], BF16) for i in range(2)]; b_est = [Buf(), Buf()]
            wuqh = [sbt(ph, f"wuqh{i}", [128, 6, 256], BF16) for i in range(2)]; b_wuqh = [Buf(), Buf()]; b_wuq = Buf()
            qnw = sbt(ph, "qnw", [128, 6], F32)
            qa = sbt(ph, "qa", [128, 6, 512], BF16); b_qa = Buf()
            cq = sbt(ph, "cq", [128, 6, 512], BF16); b_cq = Buf()
            sqt = cq; b_sq = b_cq
            rst = tmpg; b_rst = b_tmpg
            qrr = sbt(ph, "qrr", [64, 2, 512], F32); b_qrr = Buf()
            RO = rope_alloc(ph, "o"); cs2o = RO["cs2"]; sn2o = RO["sn2"]; b_ropeo = RO["b"]
            sc.op("sp", lambda e: e.dma_start(out=qnw[:], in_=q_norm_w[0:1, :].rearrange("o (k p) -> p (o k)", p=128), allow_slow_non_contiguous=True), pwrites=[b_wuq], dma=True)
            wn = [0]

            def wload(col0):
                i = wn[0] % 2; wn[0] += 1
                gi = WIDX[col0]
                sc.op("sp", lambda e, i=i, gi=gi: e.dma_start(out=wst[i][:, :, :].rearrange("p k c -> p (k c)"), in_=WS[gi, :, :]), reads=[b_WS], writes=[b_wst[i]], dma=True)
                return wst[i], b_wst[i]

            def proj_T(col0, ncols, evac, hT, b_hT):
                for c0 in range(0, ncols, 256):
                    wt, bw = wload(col0 + c0)
                    for cc in range(2):
                        pb, bb = nps()
                        for kc in range(16):
                            sc.op("pe", lambda e, wt=wt, cc=cc, kc=kc, pb=pb: e.matmul(pb[:, :], wt[:, kc, cc * 128:(cc + 1) * 128], hT[:, kc, 0:512], start=(kc == 0), stop=(kc == 15)),
                                  reads=[bw, b_hT], **wkw(kc == 0, bb))
                        evac(c0 // 128 + cc, pb, bb, wt, bw, cc)

            for dr in range(2):
                xsrc = xfw if dr == 0 else xbw
                sc.op("dve", lambda e: e.memset(Sst[:], 0.0), writes=[b_S])
                blks = list(range(NBS)) if dr == 0 else list(range(NBS - 1, -1, -1))
                for blk in blks:
                    own = (blk >= NBS - NBO) if dr == 0 else (blk < NBO)
                    ao = (blk - (NBS - NBO)) * 512 if dr == 0 else blk * 512
                    a = blk * 512
                    load_block(L, xsrc, a, halo=True)
                    hT = L["hT"]; b_hT = L["b_hT"]
                    ncc = 24 if own else 20
                    ucnt = [0]

                    def conv_evac(c, pb, bb, wt, bw, cc):
                        i = ucnt[0] % 2; ucnt[0] += 1
                        ph_, bh = nps()
                        for kc in range(16):
                            sc.op("pe", lambda e, kc=kc, ph_=ph_: e.matmul(ph_[:, 0:3], wt[:, kc, cc * 128:(cc + 1) * 128], hT[:, kc, 512:515], start=(kc == 0), stop=(kc == 15)),
                                  reads=[bw, b_hT], **wkw(kc == 0, bh))
                        ue = uext[i]; bu = b_ue[i]; ac = acc[i]; ba = b_acc[i]
                        sc.op("act", lambda e: e.activation(out=ue[:, 2:514], in_=pb[:, :], func=AF.Copy), reads=[bb], writes=[bu])
                        sc.op("dve", lambda e: e.tensor_copy(ue[:, 0:2], ph_[:, 0:2]), reads=[bh], pwrites=[bu])
                        sc.op("dve", lambda e: e.tensor_copy(ue[:, 514:515], ph_[:, 2:3]), reads=[bh], pwrites=[bu])
                        sc.op("dve", lambda e: e.tensor_scalar(ac[:, :], ue[:, 0:512], cw[:, c, 0:1], None, ALU.mult), reads=[bu, b_pc], writes=[ba])
                        for k in range(1, 4):
                            sc.op("dve", lambda e, k=k: e.scalar_tensor_tensor(ac[:, :], ue[:, k:k + 512], cw[:, c, k:k + 1], ac[:, :], ALU.mult, ALU.add), reads=[bu, ba], pwrites=[ba])
                        sc.op("act", lambda e: e.activation(out=XC[:, c, :], in_=ac[:, :], func=AF.Silu, bias=cb[:, c:c + 1]), reads=[ba, b_pc], **wkw(c == 0, b_XC))
                    proj_T(C_XS, ncc * 128, conv_evac, hT, b_hT)

                    if own and dr == 0:
                        ecnt = [0]

                        def mk_evac(dst, bdst, func):
                            def ev(c, pb, bb, wt, bw, cc):
                                i = ecnt[0] % 2; ecnt[0] += 1
                                sc.op("act", lambda e: e.activation(out=est[i][:, :], in_=pb[:, :], func=func), reads=[bb], writes=[b_est[i]])
                                sc.op("sp", lambda e: e.dma_start(out=dst[c * 128:(c + 1) * 128, ao:ao + 512], in_=est[i][:, :]), reads=[b_est[i]], pwrites=[bdst], dma=True)
                            return ev
                        proj_T(C_ZA, 2048, mk_evac(ZAT, b_ZAT, AF.Silu), hT, b_hT)
                        proj_T(C_GS, 2048, mk_evac(GST, b_GST, AF.Sigmoid), hT, b_hT)
                        proj_T(C_GA, 2048, mk_evac(GAT, b_GAT, AF.Sigmoid), hT, b_hT)

                        def qa_evac(c, pb, bb, wt, bw, cc):
                            sc.op("dve", lambda e: e.tensor_copy(qa[:, c, :], pb[:, :]), reads=[bb], **wkw(c == 0, b_qa))
                        proj_T(C_QA, 768, qa_evac, hT, b_hT)
                        rope_block(RO, pos_own, ao)
                        t_rstd([qa[:, i, :] for i in range(6)], b_qa, sqt, b_sq, rst, b_rst, 512, 768)
                        for c in range(6):
                            sc.op("dve", lambda e, c=c: e.scalar_tensor_tensor(cq[:, c, :], qa[:, c, :], qnw[:, c:c + 1], rst[:, :], ALU.mult, ALU.mult),
                                  reads=[b_qa, b_rst, b_wuq], **wkw(c == 0, b_cq))
                        for h in range(16):
                            i = h % 2
                            wq = wuqh[i]; bwq = b_wuqh[i]
                            sc.op("pool", lambda e, h=h, wq=wq: e.dma_start(out=wq[:, :, 0:192], in_=w_uq[:, h * 192:(h + 1) * 192].rearrange("(k p) c -> p k c", p=128)), writes=[bwq], dma=True)
                            sc.op("dve", lambda e, wq=wq: e.tensor_scalar(wq[:, :, 192:224], wq[:, :, 160:192], -1.0, None, ALU.mult), reads=[bwq], pwrites=[bwq])
                            sc.op("dve", lambda e, wq=wq: e.tensor_copy(wq[:, :, 224:256], wq[:, :, 128:160]), reads=[bwq], pwrites=[bwq])
                            pb, bb = nps()
                            for kc in range(6):
                                sc.op("pe", lambda e, wq=wq, kc=kc, pb=pb: e.matmul(pb[:, :], wq[:, kc, 0:128], cq[:, kc, :], start=(kc == 0), stop=(kc == 5)),
                                      reads=[bwq, b_cq], **wkw(kc == 0, bb))
                            sc.op("act", lambda e, pb=pb, i=i: e.activation(out=est[i][:, :], in_=pb[:, :], func=AF.Copy), reads=[bb], writes=[b_est[i]])
                            sc.op("sp", lambda e, h=h, i=i: e.dma_start(out=QT[h, :, ao:ao + 512], in_=est[i][:, :]), reads=[b_est[i]], pwrites=[b_QT], dma=True)
                            pb, bb = nps()
                            for kc in range(6):
                                sc.op("pe", lambda e, wq=wq, kc=kc, pb=pb: e.matmul(pb[:, :], wq[:, kc, 128:256], cq[:, kc, :], start=(kc == 0), stop=(kc == 5)),
                                      reads=[bwq, b_cq], **wkw(kc == 0, bb))
                            sc.op("dve", lambda e, pb=pb: e.tensor_tensor(qrr[:, 0, :], pb[0:64, :], cs2o[:, 0:512], ALU.mult), reads=[bb, b_ropeo], writes=[b_qrr])
                            sc.op("dve", lambda e, pb=pb: e.tensor_copy(qrr[:, 1, :], pb[64:128, :]), reads=[bb], pwrites=[b_qrr])
                            sc.op("dve", lambda e: e.tensor_tensor(qrr[:, 1, :], qrr[:, 1, :], sn2o[:, 0:512], ALU.mult), reads=[b_qrr, b_ropeo], pwrites=[b_qrr])
                            j = (h + 1) % 2
                            sc.op("dve", lambda e, j=j: e.tensor_tensor(est[j][0:64, :], qrr[:, 0, :], qrr[:, 1, :], ALU.add), reads=[b_qrr], writes=[b_est[j]])
                            sc.op("sp", lambda e, h=h, j=j: e.dma_start(out=QR[h, :, ao:ao + 512], in_=est[j][0:64, :]), reads=[b_est[j]], pwrites=[b_QR], dma=True)

                    chs = [0, 1, 2, 3] if dr == 0 else [3, 2, 1, 0]
                    for ch in chs:
                        cg = blk * 4 + ch
                        t0 = ch * 128
                        for half in range(2):
                            pb, bb = nps(); pv = pb[:].bitcast(BF16)
                            for k8 in range(8):
                                c = half * 8 + k8
                                sc.op("pe", lambda e, pv=pv, k8=k8, c=c: e.transpose(pv[:, k8 * 128:(k8 + 1) * 128], XC[:, c, t0:t0 + 128], ident[:]),
                                      reads=[b_XC], **wkw(k8 == 0, bb))
                            sc.op("act", lambda e, pv=pv, half=half: e.activation(out=xtok[:, half * 1024:(half + 1) * 1024], in_=pv[:, :], func=AF.Copy),
                                  reads=[bb], **wkw(half == 0, b_xtok))
                        pb, bb = nps(); pv = pb[:].bitcast(BF16)
                        for k4 in range(4):
                            sc.op("pe", lambda e, pv=pv, k4=k4: e.transpose(pv[:, k4 * 128:(k4 + 1) * 128], XC[:, 16 + k4, t0:t0 + 128], ident[:]),
                                  reads=[b_XC], **wkw(k4 == 0, bb))
                        sc.op("dve", lambda e, pv=pv: e.tensor_copy(btok[:, :], pv[:, 0:512]), reads=[bb], writes=[b_btok])
                        pb, bb = nps()
                        for kc in range(16):
                            sc.op("pe", lambda e, kc=kc, pb=pb: e.matmul(pb[:, 0:32], hT[:, kc, t0:t0 + 128], wdt[:, kc, dr * 32:dr * 32 + 32], start=(kc == 0), stop=(kc == 15)),
                                  reads=[b_hT, b_pc], **wkw(kc == 0, bb))
                        sc.op("dve", lambda e, pb=pb: e.tensor_tensor(sm[:, 0, :], pb[:, 0:32], dtbbc[:, dr * 32:dr * 32 + 32], ALU.add), reads=[bb, b_pc], writes=[b_sm])
                        sc.op("act", lambda e: e.activation(out=sm[:, 0, :], in_=sm[:, 0, :], func=AF.Exp), reads=[b_sm], pwrites=[b_sm])
                        sc.op("act", lambda e: e.activation(out=sm[:, 0, :], in_=sm[:, 0, :], func=AF.Ln, bias=1.0), reads=[b_sm], pwrites=[b_sm])
                        sc.op("dve", lambda e: e.tensor_tensor(sm[:, 1, :], sm[:, 0, :], abc[:, dr * 32:dr * 32 + 32], ALU.mult), reads=[b_sm, b_pc], pwrites=[b_sm])
                        pP, bP = nps()
                        sc.op("pe", lambda e, pP=pP: e.matmul(pP[:, 0:32], tri[:, dr, :], sm[:, 1, :], start=True, stop=True), reads=[b_sm, b_pc], writes=[bP])
                        pT, bT = nps()
                        sc.op("pe", lambda e, pT=pT: e.matmul(pT[:, 0:32], onesf[:, :], sm[:, 1, :], start=True, stop=True), reads=[b_sm], writes=[bT])
                        sc.op("dve", lambda e, pP=pP: e.tensor_copy(sm[:, 2, :], pP[:, 0:32]), reads=[bP], pwrites=[b_sm])
                        sc.op("dve", lambda e, pT=pT: e.tensor_tensor(sm[:, 3, :], pT[:, 0:32], sm[:, 2, :], ALU.subtract), reads=[bT, b_sm], pwrites=[b_sm])
                        sc.op("act", lambda e: e.activation(out=sm[:, 3, :], in_=sm[:, 3, :], func=AF.Exp), reads=[b_sm], pwrites=[b_sm])
                        sc.op("act", lambda e, pT=pT: e.activation(out=sm[:, 4, :], in_=pT[:, 0:32], func=AF.Exp), reads=[bT], pwrites=[b_sm])
                        sc.op("dve", lambda e, cg=cg: e.scalar_tensor_tensor(sm[:, 5, :], sm[:, 0, :], vfl[:, dr, cg:cg + 1], sm[:, 3, :], ALU.mult, ALU.mult), reads=[b_sm, b_pc], pwrites=[b_sm])
                        sc.op("dve", lambda e: e.tensor_tensor(xdw[:, :].rearrange("p (h d) -> p h d", d=64), xtok[:, :].rearrange("p (h d) -> p h d", d=64),
                                                               bc(sm[:, 5, :], [128, 32, 64], 2), ALU.mult), reads=[b_xtok, b_sm], writes=[b_xdw])
                        if own:
                            l0 = ao + t0
                            sc.op("act", lambda e: e.activation(out=sm[:, 6, :], in_=sm[:, 2, :], func=AF.Exp), reads=[b_sm], pwrites=[b_sm])
                            sc.op("dve", lambda e: e.tensor_scalar(sm[:, 7, :], sm[:, 2, :], -1.0, None, ALU.mult), reads=[b_sm], pwrites=[b_sm])
                            sc.op("dve", lambda e: e.tensor_tensor(xd[:, :].rearrange("p (h d) -> p h d", d=64), xtok[:, :].rearrange("p (h d) -> p h d", d=64),
                                                                   bc(sm[:, 0, :], [128, 32, 64], 2), ALU.mult), reads=[b_xtok, b_sm], writes=[b_xd])
                            for g in range(4):
                                sc.op("act", lambda e, g=g: e.activation(out=Sbf[:, g, :], in_=Sst[:, g, :], func=AF.Copy), reads=[b_S], **wkw(g == 0, b_Sbf))
                            pb, bb = nps()
                            for g in range(4):
                                sc.op("pe", lambda e, g=g, pb=pb: e.matmul(pb[:, g * 128:(g + 1) * 128], XC[:, 16 + g, t0:t0 + 128], XC[:, 20 + g, t0:t0 + 128], start=True, stop=True),
                                      reads=[b_XC], **wkw(g == 0, bb))
                            sc.op("act", lambda e, pb=pb: e.activation(out=CBT[:, :, :], in_=pb[:, :].rearrange("p (g l) -> p g l", g=4), func=AF.Copy), reads=[bb], writes=[b_CBT])
                            if dr == 0:
                                sc.op("dve", lambda e: e.tensor_tensor(yn[:, :].rearrange("p (h d) -> p h d", d=64), xtok[:, :].rearrange("p (h d) -> p h d", d=64),
                                                                       bc(dskbc[:, :], [128, 32, 64], 2), ALU.mult), reads=[b_xtok, b_pc], writes=[b_yn])
                            else:
                                sc.op("sp", lambda e, l0=l0: e.dma_start(out=yfl[:, :], in_=YF[l0:l0 + 128, :]), reads=[b_YF], writes=[b_yfl], dma=True)
                            for g in range(4):
                                sc.op("dve", lambda e, g=g: e.tensor_tensor(RH[:, :, :], bc(tri[:, dr, :], [128, 8, 128], 1), bc(sm[:, 1, g * 8:(g + 1) * 8], [128, 8, 128], 2), ALU.mult),
                                      reads=[b_sm, b_pc], writes=[b_RH])
                                for q in range(2):
                                    pb, bb = nps()
                                    sc.op("pe", lambda e, q=q, pb=pb: e.matmul(pb[:, :], onesf[:, :], RH[:, q * 4:(q + 1) * 4, :].rearrange("p h l -> p (h l)"), start=True, stop=False),
                                          reads=[b_RH], writes=[bb])
                                    sc.op("pe", lambda e, pb=pb: e.matmul(pb[:, :].rearrange("p (h l) -> p h l", h=4), ident[:, :], bc(negm[:, dr, :], [128, 4, 128], 1), start=False, stop=True),
                                          reads=[b_pc], pwrites=[bb])
                                    for hh in range(4):
                                        h = g * 8 + q * 4 + hh
                                        sc.op("act", lambda e, pb=pb, hh=hh, h=h, q=q: e.activation(out=LH[:, q * 4 + hh, :], in_=pb[:, hh * 128:(hh + 1) * 128], func=AF.Exp, bias=sm[:, 7, h:h + 1]),
                                              reads=[bb, b_sm], **wkw(q == 0 and hh == 0, b_LH))
                                sc.op("dve", lambda e, g=g: e.tensor_tensor(G[:, :, :], LH[:, :, :], bc(CBT[:, g, :], [128, 8, 128], 1), ALU.mult), reads=[b_LH, b_CBT], writes=[b_G])
                                pY, bY = nps()
                                if dr == 0:
                                    sc.op("pe", lambda e, g=g, pY=pY: e.matmul(pY[:, :], ident[:, :], yn[:, g * 512:(g + 1) * 512], start=True, stop=False), reads=[b_yn], writes=[bY])
                                for hh in range(8):
                                    h = g * 8 + hh
                                    sc.op("pe", lambda e, pY=pY, hh=hh, h=h: e.matmul(pY[:, hh * 64:(hh + 1) * 64], G[:, hh, :], xd[:, h * 64:(h + 1) * 64], start=(dr == 1), stop=True),
                                          reads=[b_G, b_xd], **wkw(dr == 1 and hh == 0, bY))
                                pO, bO = nps()
                                sc.op("pe", lambda e, g=g, pO=pO: e.matmul(pO[:, :], XC[:, 20 + g, t0:t0 + 128], Sbf[:, g, :], start=True, stop=True), reads=[b_XC, b_Sbf], writes=[bO])
                                sc.op("dve", lambda e, g=g, pO=pO: e.tensor_tensor(tmpg[:, :].rearrange("p (h d) -> p h d", d=64), pO[:, :].rearrange("p (h d) -> p h d", d=64),
                                                                            bc(sm[:, 6, g * 8:(g + 1) * 8], [128, 8, 64], 2), ALU.mult), reads=[bO, b_sm], writes=[b_tmpg])
                                sc.op("dve", lambda e, g=g, pY=pY: e.tensor_tensor(ysb[:, g * 512:(g + 1) * 512], pY[:, :], tmpg[:, :], ALU.add), reads=[bY, b_tmpg], **wkw(g == 0, b_ysb))
                            if dr == 0:
                                sc.op("sp", lambda e, l0=l0: e.dma_start(out=YF[l0:l0 + 128, :], in_=ysb[:, :]), reads=[b_ysb], pwrites=[b_YF], dma=True)
                            else:
                                for cg4 in range(4):
                                    for c2 in range(2):
                                        wt, bw = wload(C_ZS + cg4 * 512 + c2 * 256)
                                        pb, bb = nps()
                                        for kc in range(16):
                                            sc.op("pe", lambda e, kc=kc, pb=pb, wt=wt: e.matmul(pb[:, 0:256], hT[:, kc, t0:t0 + 128], wt[:, kc, :], start=(kc == 0), stop=(kc == 15)),
                                                  reads=[b_hT, bw], **wkw(kc == 0, bb))
                                        o0 = cg4 * 512 + c2 * 256
                                        sc.op("act", lambda e, pb=pb, o0=o0: e.activation(out=zs[:, o0:o0 + 256], in_=pb[:, 0:256], func=AF.Silu), reads=[bb], **wkw(o0 == 0, b_zs))
                                sc.op("pool", lambda e: e.tensor_tensor(ysb[:, :], ysb[:, :], yfl[:, :], ALU.add), reads=[b_yfl, b_ysb], pwrites=[b_ysb])
                                sc.op("pool", lambda e: e.tensor_tensor(ysb[:, :], ysb[:, :], zs[:, :], ALU.mult), reads=[b_zs, b_ysb], pwrites=[b_ysb])
                                for g in range(4):
                                    sc.op("act", lambda e, g=g: e.activation(out=tmpg[:, :], in_=ysb[:, g * 512:(g + 1) * 512], func=AF.Square, accum_out=gss[:, g:g + 1]),
                                          reads=[b_ysb], writes=[b_tmpg], pwrites=[b_gss])
                                sc.op("act", lambda e: e.activation(out=gss[:, 4:8], in_=gss[:, 0:4], func=AF.Ln, scale=1.0 / 512, bias=1e-6), reads=[b_gss], pwrites=[b_gss])
                                sc.op("act", lambda e: e.activation(out=gss[:, 8:12], in_=gss[:, 4:8], func=AF.Exp, scale=-0.5), reads=[b_gss], pwrites=[b_gss])
                                for g in range(4):
                                    sc.op("dve", lambda e, g=g: e.scalar_tensor_tensor(yn[:, g * 512:(g + 1) * 512], ysb[:, g * 512:(g + 1) * 512], gss[:, 8 + g:9 + g], snwbc[:, g * 512:(g + 1) * 512], ALU.mult, ALU.mult),
                                          reads=[b_ysb, b_gss, b_pc], **wkw(g == 0, b_yn))
                                for half in range(2):
                                    pb, bb = nps(); pv = pb[:].bitcast(BF16)
                                    for k8 in range(8):
                                        c = half * 8 + k8
                                        sc.op("pe", lambda e, pv=pv, k8=k8, c=c: e.transpose(pv[:, k8 * 128:(k8 + 1) * 128], yn[:, c * 128:(c + 1) * 128], ident[:]),
                                              reads=[b_yn], **wkw(k8 == 0, bb))
                                    sc.op("act", lambda e, pv=pv, half=half: e.activation(out=ynT[:, half * 8:half * 8 + 8, :], in_=pv.rearrange("p (k t) -> p k t", k=8), func=AF.Copy),
                                          reads=[bb], **wkw(half == 0, b_ynT))
                                sc.op("sp", lambda e, l0=l0: e.dma_start(out=YNT[:, l0:l0 + 128].rearrange("(k p) t -> p k t", p=128), in_=ynT[:, :, :]), reads=[b_ynT], pwrites=[b_YNT], dma=True)
                        for g in range(4):
                            pb, bb = nps()
                            sc.op("pe", lambda e, g=g, pb=pb: e.matmul(pb[:, :], btok[:, g * 128:(g + 1) * 128], xdw[:, g * 512:(g + 1) * 512], start=True, stop=True),
                                  reads=[b_btok, b_xdw], writes=[bb])
                            sc.op("dve", lambda e, g=g: e.tensor_tensor(Sst[:, g, :].rearrange("p (h d) -> p h d", d=64), Sst[:, g, :].rearrange("p (h d) -> p h d", d=64),
                                                                      bc(sm[:, 4, g * 8:(g + 1) * 8], [128, 8, 64], 2), ALU.mult), reads=[b_sm, b_Sbf], pwrites=[b_S])
                            sc.op("dve", lambda e, g=g, pb=pb: e.tensor_tensor(Sst[:, g, :], Sst[:, g, :], pb[:, :], ALU.add), reads=[bb], pwrites=[b_S])
            sc.flush()

        SCALE = 1.0 / float(np.sqrt(192.0))
        NKT = S // 128
        ph = contextlib.ExitStack()
        with ph:
            krs = sbt(ph, "krs", [64, S], BF16); b_krs = Buf()
            ktsl = [sbt(ph, f"kts{i}", [128, S], BF16) for i in range(2)]; b_ktsl = [Buf(), Buf()]
            vsl = [sbt(ph, f"vs{i}", [128, NKT, 128], BF16) for i in range(2)]; b_vsl = [Buf(), Buf()]
            qts = sbt(ph, "qts", [128, 512], BF16); qrs = sbt(ph, "qrs", [64, 512], BF16); b_q = Buf()
            pt = [sbt(ph, f"pt{i}", [128, 512], BF16) for i in range(3)]; b_pt = [Buf() for _ in range(3)]
            rec = sbt(ph, "rec", [128, 512], F32); b_rec = Buf()
            of = sbt(ph, "of", [128, 512], F32); b_of = Buf()
            zat = sbt(ph, "zat", [128, 512], BF16); b_zat = Buf()
            ob = sbt(ph, "ob", [128, 512], BF16); b_ob = Buf()
            sc.op("sp", lambda e: e.dma_start(out=krs[:, :], in_=KR[:, :]), reads=[b_KR], writes=[b_krs], dma=True)
            rot = [0]; pn = [0]
            for h in range(16):
                kts = ktsl[h % 2]; b_kts = b_ktsl[h % 2]; vs = vsl[h % 2]; b_vs = b_vsl[h % 2]
                sc.op("sp", lambda e, h=h: e.dma_start(out=kts[:, :], in_=KT[h, :, :]), reads=[b_KT], writes=[b_kts], dma=True)
                sc.op("sp", lambda e, h=h: e.dma_start(out=vs[:, :, :], in_=VV[h, :, :].rearrange("(k p) d -> p k d", p=128)), reads=[b_VV], writes=[b_vs], dma=True)
                for qb in range(SO // 512):
                    q0 = qb * 512
                    sc.op("sp", lambda e, h=h, q0=q0: e.dma_start(out=qts[:, :], in_=QT[h, :, q0:q0 + 512]), reads=[b_QT], writes=[b_q], dma=True)
                    sc.op("sp", lambda e, h=h, q0=q0: e.dma_start(out=qrs[:, :], in_=QR[h, :, q0:q0 + 512]), reads=[b_QR], pwrites=[b_q], dma=True)
                    sc.op("sp", lambda e, h=h, q0=q0: e.dma_start(out=zat[:, :], in_=ZAT[h * 128:(h + 1) * 128, q0:q0 + 512]), reads=[b_ZAT], writes=[b_zat], dma=True)
                    pO, bO = psum[6], psb[6]; pD, bD = psum[7], psb[7]
                    SK = 2
                    base_r = rot[0]; rot[0] += NKT
                    for step in range(NKT + SK):
                        if step < NKT:
                            kt = step
                            i = (base_r + kt) % 6
                            pS, bS = psum[i], psb[i]
                            j = kt % 3
                            sc.op("pe", lambda e, kt=kt, pS=pS: e.matmul(pS[:, :], kts[:, kt * 128:(kt + 1) * 128], qts[:, :], start=True, stop=False), reads=[b_kts, b_q], writes=[bS])
                            sc.op("pe", lambda e, kt=kt, pS=pS: e.matmul(pS[:, :], krs[:, kt * 128:(kt + 1) * 128], qrs[:, :], start=False, stop=True), reads=[b_krs, b_q], pwrites=[bS])
                            sc.op("act", lambda e, pS=pS, j=j: e.activation(out=pt[j][:, :], in_=pS[:, :], func=AF.Exp, scale=SCALE), reads=[bS], writes=[b_pt[j]])
                        if step >= SK:
                            kt = step - SK
                            j = kt % 3
                            sc.op("pe", lambda e, kt=kt, j=j: e.matmul(pO[:, :], vs[:, kt, :], pt[j][:, :], start=(kt == 0), stop=(kt == NKT - 1)), reads=[b_vs, b_pt[j]], **wkw(kt == 0, bO))
                            sc.op("pe", lambda e, kt=kt, j=j: e.matmul(pD[:, :], onesb[:, :], pt[j][:, :], start=(kt == 0), stop=(kt == NKT - 1)), reads=[b_pt[j]], **wkw(kt == 0, bD))
                    sc.op("dve", lambda e: e.reciprocal(rec[:, :], pD[:, :]), reads=[bD], writes=[b_rec])
                    sc.op("dve", lambda e: e.tensor_tensor(of[:, :], pO[:, :], rec[:, :], ALU.mult), reads=[bO, b_rec], writes=[b_of])
                    sc.op("dve", lambda e: e.tensor_tensor(ob[:, :], of[:, :], zat[:, :], ALU.mult), reads=[b_of, b_zat], writes=[b_ob])
                    sc.op("sp", lambda e, h=h, q0=q0: e.dma_start(out=OT[h * 128:(h + 1) * 128, q0:q0 + 512], in_=ob[:, :]), reads=[b_ob], pwrites=[b_OT], dma=True)
            sc.flush()

        ph = contextlib.ExitStack()
        with ph:
            wo = sbt(ph, "wo", [128, 16, D], BF16); b_wo = Buf()
            fnw = sbt(ph, "fnw", [128, D], F32); b_fnw = Buf()
            ynb = sbt(ph, "ynb", [128, 16, 512], BF16); otb = sbt(ph, "otb", [128, 16, 512], BF16); b_in = Buf()
            wpc = [sbt(ph, f"wpc{i}", [128, 2, 16, 128], BF16) for i in range(2)]; b_wpc = [Buf(), Buf()]
            gsb = [sbt(ph, f"gsb{i}", [128, 2, 512], BF16) for i in range(2)]; b_gsb = [Buf(), Buf()]
            t1 = sbt(ph, "t1", [128, 512], F32); t2 = sbt(ph, "t2", [128, 512], F32); b_t = Buf()
            mT = sbt(ph, "mT", [128, 16, 512], BF16); b_mT = Buf()
            xo = sbt(ph, "xo", [128, D], F32); b_xo = Buf()
            rs = sbt(ph, "rs", [128, D], F32); b_rs = Buf()
            sq4 = sbt(ph, "sq4", [128, D], BF16); fs = sbt(ph, "fs", [128, 4], F32); b_fs = Buf()
            for kc in range(16):
                sc.op("pool", lambda e, kc=kc: e.dma_start(out=wo[:, kc, :], in_=w_out[kc * 128:(kc + 1) * 128, :]), pwrites=[b_wo], dma=True)
            sc.op("sp", lambda e: e.dma_start(out=fnw[:], in_=final_norm_w[0:1, :].partition_broadcast(128)), writes=[b_fnw], dma=True)
            for tb in range(SO // 512):
                q0 = tb * 512
                sc.op("sp", lambda e, q0=q0: e.dma_start(out=ynb[:, :, :], in_=YNT[:, q0:q0 + 512].rearrange("(k p) t -> p k t", p=128)), reads=[b_YNT], writes=[b_in], dma=True)
                sc.op("sp", lambda e, q0=q0: e.dma_start(out=otb[:, :, :], in_=OT[:, q0:q0 + 512].rearrange("(k p) t -> p k t", p=128)), reads=[b_OT], pwrites=[b_in], dma=True)
                for c in range(16):
                    i = c % 2
                    sc.op("pool", lambda e, c=c, i=i: e.dma_start(out=wpc[i][:, 0, :, :], in_=w_ps[:, c * 128:(c + 1) * 128].rearrange("(k p) m -> p k m", p=128)), writes=[b_wpc[i]], dma=True)
                    sc.op("pool", lambda e, c=c, i=i: e.dma_start(out=wpc[i][:, 1, :, :], in_=w_pa[:, c * 128:(c + 1) * 128].rearrange("(k p) m -> p k m", p=128)), pwrites=[b_wpc[i]], dma=True)
                    sc.op("sp", lambda e, c=c, i=i, q0=q0: e.dma_start(out=gsb[i][:, 0, :], in_=GST[c * 128:(c + 1) * 128, q0:q0 + 512]), reads=[b_GST], writes=[b_gsb[i]], dma=True)
                    sc.op("sp", lambda e, c=c, i=i, q0=q0: e.dma_start(out=gsb[i][:, 1, :], in_=GAT[c * 128:(c + 1) * 128, q0:q0 + 512]), reads=[b_GAT], pwrites=[b_gsb[i]], dma=True)
                    pA, bA = nps()
                    for kc in range(16):
                        sc.op("pe", lambda e, kc=kc, i=i, pA=pA: e.matmul(pA[:, :], wpc[i][:, 0, kc, :], ynb[:, kc, :], start=(kc == 0), stop=(kc == 15)), reads=[b_wpc[i], b_in], **wkw(kc == 0, bA))
                    pB, bB = nps()
                    for kc in range(16):
                        sc.op("pe", lambda e, kc=kc, i=i, pB=pB: e.matmul(pB[:, :], wpc[i][:, 1, kc, :], otb[:, kc, :], start=(kc == 0), stop=(kc == 15)), reads=[b_wpc[i], b_in], **wkw(kc == 0, bB))
                    sc.op("dve", lambda e, i=i, pA=pA: e.tensor_tensor(t1[:, :], pA[:, :], gsb[i][:, 0, :], ALU.mult), reads=[bA, b_gsb[i]], writes=[b_t])
                    sc.op("dve", lambda e, i=i, pB=pB: e.tensor_tensor(t2[:, :], pB[:, :], gsb[i][:, 1, :], ALU.mult), reads=[bB, b_gsb[i]], pwrites=[b_t])
                    sc.op("dve", lambda e, c=c: e.tensor_tensor(mT[:, c, :], t1[:, :], t2[:, :], ALU.add), reads=[b_t], **wkw(c == 0, b_mT))
                for s_ in range(4):
                    r0 = q0 + s_ * 128
                    sc.op("sp", lambda e, r0=r0: e.dma_start(out=xo[:, :], in_=xown[r0:r0 + 128, :]), writes=[b_xo], dma=True)
                    for oc in range(4):
                        pF, bF = nps()
                        for kc in range(16):
                            sc.op("pe", lambda e, kc=kc, oc=oc, s_=s_, pF=pF: e.matmul(pF[:, :], mT[:, kc, s_ * 128:(s_ + 1) * 128], wo[:, kc, oc * 512:(oc + 1) * 512], start=(kc == 0), stop=(kc == 15)),
                                  reads=[b_mT, b_wo], **wkw(kc == 0, bF))
                        sc.op("dve", lambda e, oc=oc, pF=pF: e.tensor_tensor(rs[:, oc * 512:(oc + 1) * 512], pF[:, :], xo[:, oc * 512:(oc + 1) * 512], ALU.add), reads=[bF, b_xo], **wkw(oc == 0, b_rs))
                    sc.op("act", lambda e: e.activation(out=sq4[:, :], in_=rs[:, :], func=AF.Square, accum_out=fs[:, 0:1]), reads=[b_rs], writes=[b_fs])
                    sc.op("act", lambda e: e.activation(out=fs[:, 1:2], in_=fs[:, 0:1], func=AF.Ln, scale=1.0 / D, bias=1e-6), reads=[b_fs], pwrites=[b_fs])
                    sc.op("act", lambda e: e.activation(out=fs[:, 2:3], in_=fs[:, 1:2], func=AF.Exp, scale=-0.5), reads=[b_fs], pwrites=[b_fs])
                    sc.op("dve", lambda e: e.scalar_tensor_tensor(rs[:, :], rs[:, :], fs[:, 2:3], fnw[:, :], ALU.mult, ALU.mult), reads=[b_fs, b_fnw, b_rs], pwrites=[b_rs])
                    sc.op("sp", lambda e, r0=r0: e.dma_start(out=out[r0:r0 + 128, :], in_=rs[:, :]), reads=[b_rs], pwrites=[b_out], dma=True)
            sc.flush()
        return nc


_CACHE = {}


def kernel(x, positions, norm_w, w_in, conv_w, conv_b, a_log_fwd, a_log_bwd, dt_bias_fwd, dt_bias_bwd,
           d_skip, ssm_norm_w, q_norm_w, w_uq, kv_norm_w, w_ukv, w_proj_ssm, w_proj_attn, w_out, final_norm_w):
    x = np.asarray(x, np.float32); B, S, _ = x.shape
    SO = S // 4; NCH = (4 * SO) // 128
    import os
    dbg = bool(os.environ.get("KDBG"))
    cores = [int(c) for c in os.environ.get("KCORES", "0,1,2,3,4,5,6,7").split(",")]
    if (S, dbg) not in _CACHE:
        _CACHE[(S, dbg)] = build(S, dbg)
    nc = _CACHE[(S, dbg)]
    cst = _consts()
    f = lambda a: np.ascontiguousarray(np.asarray(a, np.float32))
    shared = {
        "norm_w": f(norm_w).reshape(1, D), "w_in": f(w_in).reshape(D, DIN), "conv_w": f(conv_w).reshape(4, 3072), "conv_b": f(conv_b).reshape(1, 3072),
        "a_log": np.concatenate([f(a_log_fwd).reshape(1, 32), f(a_log_bwd).reshape(1, 32)], 1),
        "dt_bias": np.concatenate([f(dt_bias_fwd).reshape(1, 32), f(dt_bias_bwd).reshape(1, 32)], 1),
        "d_skip": f(d_skip).reshape(1, 32), "ssm_norm_w": f(ssm_norm_w).reshape(1, D), "q_norm_w": f(q_norm_w).reshape(1, 768),
        "w_uq": f(w_uq).reshape(768, 3072), "kv_norm_w": f(kv_norm_w).reshape(1, 512), "w_ukv": f(w_ukv).reshape(512, 4096),
        "w_proj_ssm": f(w_proj_ssm).reshape(D, D), "w_proj_attn": f(w_proj_attn).reshape(D, D), "w_out": f(w_out).reshape(D, D),
        "final_norm_w": f(final_norm_w).reshape(1, D), **cst,
    }
    pos = np.asarray(positions).astype(np.int32)
    in_maps = []
    for c in range(8):
        b, j = c // 4, c % 4
        T0 = j * SO
        xkv = np.zeros((S + 4, D), np.float32); xkv[2:S + 2] = x[b]
        xfw = np.zeros((4 * SO + 4, D), np.float32)
        base = T0 + SO - 4 * SO
        lo, hi = max(base - 2, 0), min(base + 4 * SO + 2, S)
        xfw[lo - (base - 2):hi - (base - 2)] = x[b, lo:hi]
        xbw = np.zeros((4 * SO + 4, D), np.float32)
        base2 = T0
        lo, hi = max(base2 - 2, 0), min(base2 + 4 * SO + 2, S)
        xbw[lo - (base2 - 2):hi - (base2 - 2)] = x[b, lo:hi]
        ch = np.arange(NCH)
        vf = ((base + ch * 128) >= 0).astype(np.float32)
        vb = ((base2 + ch * 128) < S).astype(np.float32)
        m = dict(shared)
        m.update({"xkv": xkv, "xfw": xfw, "xbw": xbw, "xown": np.ascontiguousarray(x[b, T0:T0 + SO]),
                  "pos_kv": np.ascontiguousarray(pos[b:b + 1, :]), "pos_own": np.ascontiguousarray(pos[b:b + 1, T0:T0 + SO]),
                  "validf": np.ascontiguousarray(np.broadcast_to(vf[None, :], (128, NCH))),
                  "validb": np.ascontiguousarray(np.broadcast_to(vb[None, :], (128, NCH)))})
        in_maps.append(m)
    if len(cores) < 8:
        res = run_bass_kernel_spmd(nc, [in_maps[c] for c in cores], core_ids=list(range(len(cores))))
        kernel.last = {c: res.results[i] for i, c in enumerate(cores)}
        outp = np.zeros((B, S, D), np.float32)
        for i, c in enumerate(cores):
            b, j = c // 4, c % 4
            outp[b, j * SO:(j + 1) * SO] = res.results[i]["out"]
        return outp
    res = run_bass_kernel_spmd(nc, in_maps, core_ids=list(range(8)))
    outp = np.zeros((B, S, D), np.float32)
    for c in range(8):
        b, j = c // 4, c % 4
        outp[b, j * SO:(j + 1) * SO] = res.results[c]["out"]
    return outp


def _consts():
    k = np.arange(128)
    le = (k[:, None] <= k[None, :]).astype(np.float32)
    ge = (k[:, None] >= k[None, :]).astype(np.float32)
    negf = np.where(k[None, :] < k[:, None], -30000.0, 0.0).astype(np.float32)
    negb = np.where(k[None, :] > k[:, None], -30000.0, 0.0).astype(np.float32)
    inv = (1.0 / (10000.0 ** (np.arange(0, 64, 2, dtype=np.float32) / 64))).astype(np.float32)
    invf = np.concatenate([inv, inv])[:, None].astype(np.float32)
    return {"ident": np.eye(128, dtype=np.float32), "tri": np.stack([le, ge]), "negm": np.stack([negf, negb]), "invf": invf}
```
